# Optimizing a Trainium2 kernel written in Bass

```python
import math
import jax, jax.numpy as jnp
from jax import lax
import numpy as np

D_MODEL = 2048
BATCH = 4
SEQ = 4096
DEPTH = 2

N_A_LAYERS = DEPTH // 2
N_B_LAYERS = DEPTH - N_A_LAYERS

SSM_EXPAND = 2
D_INNER = SSM_EXPAND * D_MODEL
SSM_HEAD_DIM = 64
N_SSM_HEADS = D_INNER // SSM_HEAD_DIM
D_STATE = 128
N_SSM_GROUPS = 8
SSM_CONV = 4
SSD_CHUNK = 128
CONV_DIM = D_INNER + 2 * N_SSM_GROUPS * D_STATE
IN_PROJ_DIM = 2 * D_INNER + 2 * N_SSM_GROUPS * D_STATE + N_SSM_HEADS

N_HEADS = 16
HEAD_DIM = D_MODEL // N_HEADS
MOBA_BLOCK = 256
MOBA_TOPK = 3
Q_BLOCK = 128

D_FF = 11 * D_MODEL // 4
FFN_CONV = 3

DEEPNORM_ALPHA = (2.0 * DEPTH) ** 0.25
DEEPNORM_BETA = (8.0 * DEPTH) ** -0.25
LN_EPS = 1e-5
RMS_EPS = 1e-5

kernel_name = "yoco_mamba2_moba_deepnorm"


def layer_norm(x, g, b):
    xf = x.astype(jnp.float32)
    mu = jnp.mean(xf, axis=-1, keepdims=True)
    var = jnp.mean(jnp.square(xf - mu), axis=-1, keepdims=True)
    return ((xf - mu) * lax.rsqrt(var + LN_EPS) * g + b).astype(x.dtype)


def causal_dwconv(x, w, b):
    K, C = w.shape
    y = lax.conv_general_dilated(x, w[:, None, :].astype(x.dtype), window_strides=(1,),
                                 padding=[(K - 1, 0)],
                                 dimension_numbers=("NWC", "WIO", "NWC"),
                                 feature_group_count=C)
    return y + b


def ssd_chunked(xdt, adt, b, c):
    Bsz, S, H, P = xdt.shape
    G, N = b.shape[2], b.shape[3]
    R = H // G
    nc, l = S // SSD_CHUNK, SSD_CHUNK
    x = xdt.reshape(Bsz, nc, l, G, R, P)
    a = adt.reshape(Bsz, nc, l, G, R)
    b = b.reshape(Bsz, nc, l, G, N)
    c = c.reshape(Bsz, nc, l, G, N)
    a_cum = jnp.cumsum(a, axis=2)
    causal = jnp.tril(jnp.ones((l, l), dtype=bool))[None, None, :, :, None, None]
    seg = a_cum[:, :, :, None] - a_cum[:, :, None, :]
    decay_in = jnp.exp(jnp.where(causal, seg, -jnp.inf))
    cb = jnp.einsum('bclgn,bcsgn->bclsg', c, b)
    y_diag = jnp.einsum('bclsgr,bcsgrp->bclgrp', cb[..., None] * decay_in, x)
    decay_to_end = jnp.exp(a_cum[:, :, -1:] - a_cum)
    chunk_states = jnp.einsum('bclgn,bclgr,bclgrp->bcgrpn', b, decay_to_end, x)
    chunk_decay = jnp.exp(a_cum[:, :, -1])

    def step(state, inp):
        st, dec = inp
        return state * dec[..., None, None] + st, state

    init = jnp.zeros((Bsz, G, R, P, N), dtype=x.dtype)
    _, prev = lax.scan(step, init, (jnp.moveaxis(chunk_states, 1, 0), jnp.moveaxis(chunk_decay, 1, 0)))
    prev = jnp.moveaxis(prev, 0, 1)
    y_off = jnp.einsum('bclgn,bcgrpn,bclgr->bclgrp', c, prev, jnp.exp(a_cum))
    return (y_diag + y_off).reshape(Bsz, S, H, P)


def mamba2_mixer(x, w_in, conv_w, conv_b, dt_bias, a_log, d_skip, norm_w, w_out):
    Bsz, S, _ = x.shape
    zxbcdt = x @ w_in
    z = zxbcdt[..., :D_INNER]
    xbc = zxbcdt[..., D_INNER:D_INNER + CONV_DIM]
    dt = zxbcdt[..., D_INNER + CONV_DIM:]
    xbc = jax.nn.silu(causal_dwconv(xbc, conv_w, conv_b))
    gn = N_SSM_GROUPS * D_STATE
    xs = xbc[..., :D_INNER].astype(jnp.float32).reshape(Bsz, S, N_SSM_HEADS, SSM_HEAD_DIM)
    b_ssm = xbc[..., D_INNER:D_INNER + gn].astype(jnp.float32).reshape(Bsz, S, N_SSM_GROUPS, D_STATE)
    c_ssm = xbc[..., D_INNER + gn:].astype(jnp.float32).reshape(Bsz, S, N_SSM_GROUPS, D_STATE)
    dt = jax.nn.softplus(dt.astype(jnp.float32) + dt_bias.astype(jnp.float32))
    a = -jnp.exp(a_log.astype(jnp.float32))
    y = ssd_chunked(xs * dt[..., None], a * dt, b_ssm, c_ssm)
    y = y + d_skip.astype(jnp.float32)[:, None] * xs
    y = y.reshape(Bsz, S, D_INNER) * jax.nn.silu(z.astype(jnp.float32))
    yg = y.reshape(Bsz, S, N_SSM_GROUPS, D_INNER // N_SSM_GROUPS)
    yg = yg * lax.rsqrt(jnp.mean(jnp.square(yg), axis=-1, keepdims=True) + RMS_EPS)
    y = (yg.reshape(Bsz, S, D_INNER) * norm_w).astype(x.dtype)
    return y @ w_out


def conv_ffn(x, w_up, conv_w, conv_b, w_down):
    u = causal_dwconv(x @ w_up, conv_w, conv_b)
    gate, val = u[..., :D_FF], u[..., D_FF:]
    return (jax.nn.silu(gate) * val) @ w_down


def shared_kv(h, w_kv):
    Bsz, S, _ = h.shape
    nb = -(-S // MOBA_BLOCK)
    pad = nb * MOBA_BLOCK - S
    kv = h @ w_kv

    def to_blocks(t):
        t = jnp.pad(t.reshape(Bsz, S, N_HEADS, HEAD_DIM), ((0, 0), (0, pad), (0, 0), (0, 0)))
        return t.reshape(Bsz, nb, MOBA_BLOCK, N_HEADS, HEAD_DIM).transpose(3, 0, 1, 2, 4)

    k_blk = to_blocks(kv[..., :D_MODEL])
    v_blk = to_blocks(kv[..., D_MODEL:])
    k_mean = jnp.mean(k_blk.astype(jnp.float32), axis=3).astype(k_blk.dtype)
    return k_blk, v_blk, k_mean


def alibi_slopes():
    return 2.0 ** (-8.0 * jnp.arange(1, N_HEADS + 1, dtype=jnp.float32) / N_HEADS)


def moba_cross_attention(h, w_q, w_o, k_blk, v_blk, k_mean):
    Bsz, S, _ = h.shape
    nb = k_blk.shape[2]
    n_sel = max(1, min(MOBA_TOPK, nb))
    scale = HEAD_DIM ** -0.5
    q = (h @ w_q).reshape(Bsz, S, N_HEADS, HEAD_DIM).transpose(2, 0, 1, 3)
    nq = S // Q_BLOCK
    q_chunks = q.reshape(N_HEADS, Bsz, nq, Q_BLOCK, HEAD_DIM).transpose(2, 0, 1, 3, 4)
    slopes = alibi_slopes()
    blk_ids = jnp.arange(nb)
    key_off = jnp.arange(MOBA_BLOCK)
    bidx = jnp.arange(Bsz)[:, None, None]

    def per_chunk(args):
        cidx, qc = args
        t = cidx * Q_BLOCK + jnp.arange(Q_BLOCK)
        own = (cidx * Q_BLOCK) // MOBA_BLOCK

        def per_head(hargs):
            qh, kh, vh, kmh, m = hargs
            gate = jnp.einsum('bqd,bnd->bqn', qh, kmh).astype(jnp.float32)
            gate = jnp.where((blk_ids < own)[None, None, :], gate, -jnp.inf)
            _, top_idx = lax.top_k(gate, n_sel)
            sel_valid = top_idx < own
            kg = kh[bidx, top_idx]
            vg = vh[bidx, top_idx]
            s_sel = jnp.einsum('bqd,bqjkd->bqjk', qh, kg).astype(jnp.float32) * scale
            pos_sel = top_idx[..., None] * MOBA_BLOCK + key_off
            dist_sel = (t[None, :, None, None] - pos_sel).astype(jnp.float32)
            s_sel = jnp.where(sel_valid[..., None], s_sel - m * dist_sel, -jnp.inf)
            ko = lax.dynamic_index_in_dim(kh, own, axis=1, keepdims=False)
            vo = lax.dynamic_index_in_dim(vh, own, axis=1, keepdims=False)
            s_own = jnp.einsum('bqd,bkd->bqk', qh, ko).astype(jnp.float32) * scale
            dist_own = t[:, None] - (own * MOBA_BLOCK + key_off)[None, :]
            s_own = jnp.where((dist_own >= 0)[None], s_own - m * dist_own.astype(jnp.float32)[None], -jnp.inf)
            scores = jnp.concatenate([s_sel.reshape(Bsz, Q_BLOCK, n_sel * MOBA_BLOCK), s_own], axis=-1)
            p = jax.nn.softmax(scores, axis=-1)
            p_sel = p[..., :n_sel * MOBA_BLOCK].reshape(Bsz, Q_BLOCK, n_sel, MOBA_BLOCK).astype(vh.dtype)
            p_own = p[..., n_sel * MOBA_BLOCK:].astype(vh.dtype)
            return (jnp.einsum('bqjk,bqjkd->bqd', p_sel, vg)
                    + jnp.einsum('bqk,bkd->bqd', p_own, vo))

        return lax.map(per_head, (qc, k_blk, v_blk, k_mean, slopes))

    out = lax.map(per_chunk, (jnp.arange(nq), q_chunks))
    out = out.transpose(2, 0, 3, 1, 4).reshape(Bsz, S, N_HEADS * HEAD_DIM)
    return out @ w_o


def setup_inputs(seed: int = 0) -> dict:
    key = jax.random.key(seed)
    ks = jax.random.split(key, 24)
    f32 = jnp.float32
    nrm = lambda k, shape, fan_in: jax.random.normal(k, shape, f32) * (fan_in ** -0.5)
    dt0 = jnp.exp(jax.random.uniform(ks[4], (N_A_LAYERS, N_SSM_HEADS), f32)
                  * (math.log(0.1) - math.log(0.001)) + math.log(0.001))
    return {
        "x": jax.random.normal(ks[0], (BATCH, SEQ, D_MODEL), f32),
        "mamba_w_in": nrm(ks[1], (N_A_LAYERS, D_MODEL, IN_PROJ_DIM), D_MODEL),
        "mamba_conv_w": nrm(ks[2], (N_A_LAYERS, SSM_CONV, CONV_DIM), SSM_CONV),
        "mamba_conv_b": 0.02 * jax.random.normal(ks[3], (N_A_LAYERS, CONV_DIM), f32),
        "mamba_dt_bias": dt0 + jnp.log(-jnp.expm1(-dt0)),
        "mamba_a_log": jnp.log(jax.random.uniform(ks[5], (N_A_LAYERS, N_SSM_HEADS), f32, 1.0, 16.0)),
        "mamba_d": 1.0 + 0.02 * jax.random.normal(ks[6], (N_A_LAYERS, N_SSM_HEADS), f32),
        "mamba_norm_w": 1.0 + 0.02 * jax.random.normal(ks[7], (N_A_LAYERS, D_INNER), f32),
        "mamba_w_out": DEEPNORM_BETA * nrm(ks[8], (N_A_LAYERS, D_INNER, D_MODEL), D_INNER),
        "ffn_w_up": nrm(ks[9], (DEPTH, D_MODEL, 2 * D_FF), D_MODEL),
        "ffn_conv_w": nrm(ks[10], (DEPTH, FFN_CONV, 2 * D_FF), FFN_CONV),
        "ffn_conv_b": 0.02 * jax.random.normal(ks[11], (DEPTH, 2 * D_FF), f32),
        "ffn_w_down": DEEPNORM_BETA * nrm(ks[12], (DEPTH, D_FF, D_MODEL), D_FF),
        "ln_mix_g": 1.0 + 0.02 * jax.random.normal(ks[13], (DEPTH, D_MODEL), f32),
        "ln_mix_b": 0.02 * jax.random.normal(ks[14], (DEPTH, D_MODEL), f32),
        "ln_ffn_g": 1.0 + 0.02 * jax.random.normal(ks[15], (DEPTH, D_MODEL), f32),
        "ln_ffn_b": 0.02 * jax.random.normal(ks[16], (DEPTH, D_MODEL), f32),
        "w_kv": jnp.concatenate([nrm(ks[17], (D_MODEL, D_MODEL), D_MODEL),
                                 DEEPNORM_BETA * nrm(ks[18], (D_MODEL, D_MODEL), D_MODEL)], axis=1),
        "attn_w_q": nrm(ks[19], (N_B_LAYERS, D_MODEL, N_HEADS * HEAD_DIM), D_MODEL),
        "attn_w_o": DEEPNORM_BETA * nrm(ks[20], (N_B_LAYERS, N_HEADS * HEAD_DIM, D_MODEL), N_HEADS * HEAD_DIM),
    }


def reference(x, mamba_w_in, mamba_conv_w, mamba_conv_b, mamba_dt_bias, mamba_a_log, mamba_d,
              mamba_norm_w, mamba_w_out, ffn_w_up, ffn_conv_w, ffn_conv_b, ffn_w_down,
              ln_mix_g, ln_mix_b, ln_ffn_g, ln_ffn_b, w_kv, attn_w_q, attn_w_o):
    h = x
    k_blk = v_blk = k_mean = None
    for layer in range(DEPTH):
        if layer < N_A_LAYERS:
            mix = mamba2_mixer(h, mamba_w_in[layer], mamba_conv_w[layer], mamba_conv_b[layer],
                               mamba_dt_bias[layer], mamba_a_log[layer], mamba_d[layer],
                               mamba_norm_w[layer], mamba_w_out[layer])
        else:
            j = layer - N_A_LAYERS
            mix = moba_cross_attention(h, attn_w_q[j], attn_w_o[j], k_blk, v_blk, k_mean)
        h = layer_norm(DEEPNORM_ALPHA * h + mix, ln_mix_g[layer], ln_mix_b[layer])
        ffn = conv_ffn(h, ffn_w_up[layer], ffn_conv_w[layer], ffn_conv_b[layer], ffn_w_down[layer])
        h = layer_norm(DEEPNORM_ALPHA * h + ffn, ln_ffn_g[layer], ln_ffn_b[layer])
        if layer == N_A_LAYERS - 1:
            k_blk, v_blk, k_mean = shared_kv(h, w_kv)
    return h
```

```python
import numpy as np
from contextlib import ExitStack
import concourse.bass as bass
import concourse.mybir as mybir
from concourse.bass_utils import run_bass_kernel_spmd

F32, BF16 = mybir.dt.float32, mybir.dt.bfloat16
AF = mybir.ActivationFunctionType
ALU = mybir.AluOpType
AX = mybir.AxisListType

D = 2048
S = 4096
TT = 512
KC = D // 128
DI = 4096
NH = 64
NG = 8
DFF = 5632
FC = DFF // 128
INP = 10304
HEADS = 16
ALPHA = 4.0 ** 0.25
EPS = 1e-5
SCALE = 128 ** -0.5
NEG = -30000.0


class Buf:
    __slots__ = ("name", "t", "w", "r", "ds", "ps")

    def __init__(self, name, t=None, ps=False):
        self.name, self.t, self.w, self.r, self.ds, self.ps = name, t, [], [], None, ps


def _key(ev):
    return (ev[0], ev[1])


class KB:
    def __init__(self, nc, es):
        self.nc, self.es = nc, es
        self.E = dict(pe=nc.tensor, act=nc.scalar, dve=nc.vector, pool=nc.gpsimd, sp=nc.sync)
        self.esem = {k: es.enter_context(nc.semaphore("e_" + k)) for k in self.E}
        self.cnt = {k: 0 for k in self.E}
        self.seen = {k: {} for k in self.E}
        self.dsems, self.dcnt = [], []
        self.skip_ds = set()
        self.ds_by_name = {}
        self.uid = 0

    def sb(self, name, shape, dt, es=None):
        self.uid += 1
        tname = name if es is None else "%s_u%d" % (name, self.uid)
        t = (es or self.es).enter_context(self.nc.sbuf_tensor(tname, shape, dt))
        return Buf(name, t)

    def _wait(self, eng, evs):
        for ev in evs:
            if ev[0] == 'e':
                if eng == 'pe' and ev[1] == 'pe':
                    continue
                key, val, sem = ev[1], ev[2], self.esem[ev[1]]
            else:
                key, val, sem = ('d', ev[1]), self.dcnt[ev[1]], self.dsems[ev[1]]
            if self.seen[eng].get(key, 0) < val:
                self.E[eng].wait_ge(sem, val)
                self.seen[eng][key] = val

    @staticmethod
    def _merge(lst, ev):
        k = _key(ev)
        return [e for e in lst if _key(e) != k] + [ev]

    def _deps(self, reads, writes):
        evs = []
        for b in reads:
            evs += b.w
            if b.ps:
                evs += b.r
        for b in writes:
            evs += b.w
            evs += b.r
        return evs

    def _commit(self, ev, reads, writes):
        for b in reads:
            b.r = self._merge(b.r, ev)
        for b in writes:
            b.w = self._merge(b.w, ev)
            b.r = []

    def op(self, eng, fn, reads=(), writes=()):
        self._wait(eng, self._deps(reads, writes))
        ins = fn(self.E[eng])
        self.cnt[eng] += 1
        ins.then_inc(self.esem[eng], 1)
        ev = ('e', eng, self.cnt[eng])
        self._commit(ev, reads, writes)
        return ev

    def mm(self, ps, out, lhsT, rhs, start, stop, reads=(), sig=None, transpose=False):
        if sig is None:
            sig = stop
        evs = self._deps(reads, ())
        if start:
            evs += ps.w + ps.r
        self._wait('pe', evs)
        if transpose:
            ins = self.nc.tensor.transpose(out, lhsT, rhs)
        else:
            ins = self.nc.tensor.matmul(out, lhsT=lhsT, rhs=rhs, start=start, stop=stop)
        ev = ('e', 'pe', self.cnt['pe'] + 1)
        if sig:
            self.cnt['pe'] += 1
            ins.then_inc(self.esem['pe'], 1)
        for b in reads:
            b.r = self._merge(b.r, ev)
        if stop:
            ps.w = self._merge(ps.w, ev)
            ps.r = []
        return ev

    def dma(self, q, out, in_, sem_buf, reads=(), writes=()):
        self._wait(q, self._deps(reads, writes))
        if sem_buf.ds is None:
            if sem_buf.name in self.ds_by_name:
                sem_buf.ds = self.ds_by_name[sem_buf.name]
            else:
                sem_buf.ds = len(self.dsems)
                self.ds_by_name[sem_buf.name] = sem_buf.ds
                self.dsems.append(self.es.enter_context(self.nc.semaphore("d_%d" % sem_buf.ds)))
                self.dcnt.append(0)
        i = sem_buf.ds
        ins = self.E[q].dma_start(out=out, in_=in_)
        self.dcnt[i] += 16
        ins.then_inc(self.dsems[i], 16)
        ev = ('d', i, self.dcnt[i])
        self._commit(ev, reads, writes)
        return ev

    def barrier(self, engs=('pe', 'act', 'dve', 'pool')):
        evs = [('e', k, self.cnt[k]) for k in self.E if self.cnt[k] > 0]
        for nm, i in self.ds_by_name.items():
            if nm.startswith("wb"):
                self.skip_ds.add(i)
        evs += [('d', i, 0) for i in range(len(self.dsems)) if i not in self.skip_ds]
        for e in engs:
            self._wait(e, evs)

    def finish(self):
        evs = [('e', k, self.cnt[k]) for k in self.E if self.cnt[k] > 0]
        evs += [('d', i, 0) for i in range(len(self.dsems))]
        self._wait('sp', evs)


def build(NT=8, STOP=None):
    nc = bass.Bass("TRN2", target_bir_lowering=False)
    es = ExitStack()
    kb = KB(nc, es)

    def din(name, shape, dt=F32):
        return nc.dram_tensor(name, list(shape), dt, kind="ExternalInput").ap()

    def dscr(name, shape, dt):
        return nc.dram_tensor(name, list(shape), dt, kind="Internal").ap()

    x_d = din("x", [S, D])
    W = {}
    wshapes = dict(w_in=[D, INP], w_out=[DI, D], up0=[D, 2 * DFF], dn0=[DFF, D], w_kv=[D, 2 * D],
                   w_q=[D, D], w_o=[D, D], up1=[D, 2 * DFF], dn1=[DFF, D])
    for n, shp in wshapes.items():
        W[n] = (din(n, shp), dscr(n + "_b", shp, BF16), Buf("wd_" + n))
    mcw_d = din("mcw", [128, 48 * 4]); mcb_d = din("mcb", [128, 48])
    fcw_d = din("fcw", [128, 2 * 88 * 3]); fcb_d = din("fcb", [128, 2 * 88])
    lng_d = din("lng", [128, 4 * 16]); lnb_d = din("lnb", [128, 4 * 16])
    dtb_d = din("dtb", [64, 1]); arow_d = din("arow", [128, 64]); drow_d = din("drow", [128, 64])
    nrmw_d = din("nrmw", [128, DI])
    albias_d = din("albias", [128, HEADS * 36]); dmat_d = din("dmat", [128, HEADS * 128]); fq_d = din("fq", [128, HEADS])
    vbias_d = din("vbias", [128, 8 * 64]); esel_d = din("esel", [16, 16 * 128])
    out_d = nc.dram_tensor("out", [S, D], F32, kind="ExternalOutput").ap()
    KT_d = dscr("KT", [HEADS, 128, S], BF16)
    V_d = dscr("V", [HEADS, S // 128, 128, 129], BF16)

    ident_f = kb.sb("ident_f", [128, 128], F32); ident_b = kb.sb("ident_b", [128, 128], BF16)
    tri_f = kb.sb("tri_f", [128, 128], F32); mgt_f = kb.sb("mgt_f", [128, 128], F32); ones_f = kb.sb("ones_f", [128, 128], F32)
    hT_f = kb.sb("hT_f", [128, KC, TT], F32); hT_b = kb.sb("hT_b", [128, KC, TT], BF16)
    NWB = 3
    wb = [kb.sb("wb%d" % i, [128, 16, 256], BF16) for i in range(NWB)]
    S_f = kb.sb("S_f", [128, NG, 512], F32)
    uhalo = kb.sb("uhalo", [128, 48, 3], F32); fhalo = kb.sb("fhalo", [128, 2, 88, 2], F32)
    mcw = kb.sb("mcw_s", [128, 48, 4], F32); mcb = kb.sb("mcb_s", [128, 48], F32)
    fcw = kb.sb("fcw_s", [128, 2, 88, 3], F32); fcb = kb.sb("fcb_s", [128, 2, 88], F32)
    lng = kb.sb("lng_s", [128, 4, 16], F32); lnb = kb.sb("lnb_s", [128, 4, 16], F32)
    dtb = kb.sb("dtb_s", [64, 1], F32); arow = kb.sb("arow_s", [128, 64], F32); drow = kb.sb("drow_s", [128, 64], F32)
    albias = kb.sb("albias_s", [128, HEADS, 36], F32)
    fq = kb.sb("fq_s", [128, HEADS], F32); vbias = kb.sb("vbias_s", [128, 8, 64], F32)
    esel = kb.sb("esel_s", [16, 16, 128], BF16)
    kmT = kb.sb("kmT", [128, HEADS, 16], BF16)
    bigT = kb.sb("bigT", [128, 32, TT], BF16)
    mean_s = kb.sb("mean_s", [128, TT], F32); rstd_s = kb.sb("rstd_s", [128, TT], F32)
    sq_s = [kb.sb("sq%d" % i, [128, TT], F32) for i in range(2)]
    lnt = [kb.sb("lnt%d" % i, [128, TT], F32) for i in range(2)]
    PS = []
    for i in range(8):
        t = es.enter_context(nc.psum_tensor("ps%d" % i, [128, 512], F32))
        PS.append(Buf("ps%d" % i, t, ps=True))

    def psb(i):
        return PS[i].t[:].bitcast(BF16)

    V, A, G = 'dve', 'act', 'pool'

    kb.op(G, lambda e: e.memset(ident_f.t[:], 1.0), writes=[ident_f])
    kb.op(G, lambda e: e.affine_select(out=ident_f.t[:], in_=ident_f.t[:], pattern=[[-1, 128]], compare_op=ALU.is_equal,
                                      fill=0.0, base=0, channel_multiplier=1), reads=[ident_f], writes=[ident_f])
    kb.op(V, lambda e: e.tensor_copy(out=ident_b.t[:], in_=ident_f.t[:]), reads=[ident_f], writes=[ident_b])
    kb.op(G, lambda e: e.memset(tri_f.t[:], 1.0), writes=[tri_f])
    kb.op(G, lambda e: e.affine_select(out=tri_f.t[:], in_=tri_f.t[:], pattern=[[1, 128]], compare_op=ALU.is_ge,
                                      fill=0.0, base=0, channel_multiplier=-1), reads=[tri_f], writes=[tri_f])
    kb.op(G, lambda e: e.memset(mgt_f.t[:], 1.0), writes=[mgt_f])
    kb.op(G, lambda e: e.affine_select(out=mgt_f.t[:], in_=mgt_f.t[:], pattern=[[-1, 128]], compare_op=ALU.is_gt,
                                      fill=0.0, base=0, channel_multiplier=1), reads=[mgt_f], writes=[mgt_f])
    kb.op(G, lambda e: e.memset(ones_f.t[:], 1.0), writes=[ones_f])
    kb.op(G, lambda e: e.memset(S_f.t[:], 0.0), writes=[S_f])
    kb.op(G, lambda e: e.memset(uhalo.t[:], 0.0), writes=[uhalo])
    kb.op(G, lambda e: e.memset(fhalo.t[:], 0.0), writes=[fhalo])

    if STOP == "c0":
        kb.finish(); es.close(); return nc

    def ld(buf, src, view=None):
        kb.dma(G, buf.t[:] if view is None else view, src, buf, writes=[buf])

    ld(mcw, mcw_d.rearrange("p (c k) -> p c k", k=4)); ld(mcb, mcb_d)
    ld(fcw, fcw_d.rearrange("p (l c k) -> p l c k", l=2, k=3)); ld(fcb, fcb_d.rearrange("p (l c) -> p l c", l=2))
    ld(lng, lng_d.rearrange("p (l c) -> p l c", l=4)); ld(lnb, lnb_d.rearrange("p (l c) -> p l c", l=4))
    ld(dtb, dtb_d); ld(arow, arow_d); ld(drow, drow_d)
    kb.op(A, lambda e: e.activation(out=arow.t[:], in_=arow.t[:], func=AF.Exp), reads=[arow], writes=[arow])
    kb.op(V, lambda e: e.tensor_scalar(out=arow.t[:], in0=arow.t[:], scalar1=-1.0, scalar2=None, op0=ALU.mult), reads=[arow], writes=[arow])
    kb.op(G, lambda e: e.memset(kmT.t[:], 0.0), writes=[kmT])
    ld(albias, albias_d.rearrange("p (h j) -> p h j", h=HEADS))
    ld(fq, fq_d); ld(vbias, vbias_d.rearrange("p (i c) -> p i c", i=8))
    ld(esel, esel_d.rearrange("p (n k) -> p n k", n=16))

    if STOP == "c1":
        kb.finish(); es.close(); return nc
    def convert(names):
        for n in names:
            src, dst, wbuf = W[n]
            K = src.shape[0]
            rb = 256
            for r0 in range(0, K, rb):
                kb.dma(G, dst[r0:r0 + rb, :], src[r0:r0 + rb, :], wbuf, writes=[wbuf])
            kb.skip_ds.add(wbuf.ds)
    if STOP not in ("x0", "x1", "x2"):
        convert(["w_in"])

    wctr = [0]

    def linT(wname, K, blocks, rhs_fn, rhs_bufs, epi, ps_banks=(0, 1, 2, 3)):
        _, wdst, wbuf = W[wname]
        nkb = (K + 2047) // 2048
        for bi, segs in enumerate(blocks):
            ncols = sum(s[1] for s in segs)
            nj = ncols // 128 if ncols >= 128 else 1
            banks = [PS[ps_banks[(2 * (bi % 2) + j) % len(ps_banks)]] for j in range(nj)]
            for kbi in range(nkb):
                k0 = kbi * 2048
                kcs = min(16, (K - k0) // 128)
                wt = wb[wctr[0] % NWB]
                wctr[0] += 1
                c = 0
                for (c0, n) in segs:
                    kb.dma('sp', wt.t[:, 0:kcs, c:c + n],
                           wdst[k0:k0 + kcs * 128, c0:c0 + n].rearrange("(kc p) n -> p kc n", p=128),
                           wt, reads=[wbuf], writes=[wt])
                    c += n
                for j in range(nj):
                    m = min(128, ncols)
                    for kc in range(kcs):
                        first = (kbi == 0 and kc == 0)
                        last = (kbi == nkb - 1 and kc == kcs - 1)
                        kb.mm(banks[j], banks[j].t[0:m, :], lhsT=wt.t[:, kc, j * 128:j * 128 + m], rhs=rhs_fn(k0 // 128 + kc),
                              start=first, stop=last, reads=[wt] + list(rhs_bufs), sig=(kc == kcs - 1))
            for j in range(nj):
                epi(bi, j, banks[j])

    def layer_norm(li):
        ps_sum, ps_sq = PS[6], PS[7]
        for j in range(KC):
            sq = sq_s[j % 2]
            kb.op(A, lambda e: e.activation(out=sq.t[:], in_=hT_f.t[:, j, :], func=AF.Square), reads=[hT_f], writes=[sq])
            kb.mm(ps_sum, ps_sum.t[:], lhsT=ones_f.t[:], rhs=hT_f.t[:, j, :], start=(j == 0), stop=(j == KC - 1), reads=[ones_f, hT_f])
            kb.mm(ps_sq, ps_sq.t[:], lhsT=ones_f.t[:], rhs=sq.t[:], start=(j == 0), stop=(j == KC - 1), reads=[ones_f, sq], sig=True)
        kb.op(V, lambda e: e.tensor_scalar(out=mean_s.t[:], in0=ps_sum.t[:], scalar1=1.0 / D, scalar2=None, op0=ALU.mult),
              reads=[ps_sum], writes=[mean_s])
        t0 = lnt[0]
        kb.op(V, lambda e: e.tensor_tensor(out=t0.t[:], in0=mean_s.t[:], in1=mean_s.t[:], op=ALU.mult), reads=[mean_s], writes=[t0])
        kb.op(V, lambda e: e.scalar_tensor_tensor(out=rstd_s.t[:], in0=ps_sq.t[:], scalar=1.0 / D, in1=t0.t[:],
                                                  op0=ALU.mult, op1=ALU.subtract), reads=[ps_sq, t0], writes=[rstd_s])
        kb.op(V, lambda e: e.tensor_scalar(out=rstd_s.t[:], in0=rstd_s.t[:], scalar1=EPS, scalar2=None, op0=ALU.add),
              reads=[rstd_s], writes=[rstd_s])
        kb.op(A, lambda e: e.activation(out=rstd_s.t[:], in_=rstd_s.t[:], func=AF.Sqrt), reads=[rstd_s], writes=[rstd_s])
        kb.op(V, lambda e: e.reciprocal(out=rstd_s.t[:], in_=rstd_s.t[:]), reads=[rstd_s], writes=[rstd_s])
        for j in range(KC):
            t = lnt[j % 2]
            kb.op(V, lambda e: e.tensor_tensor(out=t.t[:], in0=hT_f.t[:, j, :], in1=mean_s.t[:], op=ALU.subtract),
                  reads=[hT_f, mean_s], writes=[t])
            kb.op(V, lambda e: e.tensor_tensor(out=t.t[:], in0=t.t[:], in1=rstd_s.t[:], op=ALU.mult), reads=[t, rstd_s], writes=[t])
            kb.op(V, lambda e: e.tensor_scalar(out=hT_f.t[:, j, :], in0=t.t[:], scalar1=lng.t[:, li, j:j + 1], scalar2=lnb.t[:, li, j:j + 1],
                                               op0=ALU.mult, op1=ALU.add), reads=[t, lng, lnb], writes=[hT_f])
            kb.op(A, lambda e: e.activation(out=hT_b.t[:, j, :], in_=hT_f.t[:, j, :], func=AF.Copy), reads=[hT_f], writes=[hT_b])

    def resid_epi(bi, j, ps):
        ch = bi * 2 + j
        kb.op(V, lambda e: e.scalar_tensor_tensor(out=hT_f.t[:, ch, :], in0=hT_f.t[:, ch, :], scalar=ALPHA, in1=ps.t[:],
                                                  op0=ALU.mult, op1=ALU.add), reads=[hT_f, ps], writes=[hT_f])

    blocks_dm = [[(c0, 256)] for c0 in range(0, D, 256)]

    def ffn(layer, pes):
        act = kb.sb("ffn_act", [128, FC, TT], BF16, pes)
        fu = [kb.sb("ffn_u%d" % i, [128, 2 + TT], F32, pes) for i in range(2)]
        facc = [kb.sb("ffn_acc%d" % i, [128, TT], F32, pes) for i in range(2)]
        gact = [kb.sb("ffn_g%d" % i, [128, TT], BF16, pes) for i in range(2)]
        wn = "up%d" % layer
        blocks = []
        for c in range(0, FC, 2):
            blocks.append([(c * 128, 256)])
            blocks.append([(DFF + c * 128, 256)])
        ctr = [0]

        def epi(bi, j, ps):
            isval = bi % 2
            c = (bi // 2) * 2 + j
            ch = isval * FC + c
            u = fu[ctr[0] % 2]; acc = facc[ctr[0] % 2]; ctr[0] += 1
            kb.op(V, lambda e: e.tensor_copy(out=u.t[:, 0:2], in_=fhalo.t[:, layer, ch, :]), reads=[fhalo], writes=[u])
            kb.op(A, lambda e: e.activation(out=u.t[:, 2:2 + TT], in_=ps.t[:], func=AF.Copy), reads=[ps], writes=[u])
            kb.op(V, lambda e: e.tensor_copy(out=fhalo.t[:, layer, ch, :], in_=u.t[:, TT:TT + 2]), reads=[u], writes=[fhalo])
            kb.op(V, lambda e: e.tensor_scalar(out=acc.t[:], in0=u.t[:, 0:TT], scalar1=fcw.t[:, layer, ch, 0:1], scalar2=None, op0=ALU.mult),
                  reads=[u, fcw], writes=[acc])
            for k in (1, 2):
                kb.op(V, lambda e: e.scalar_tensor_tensor(out=acc.t[:], in0=u.t[:, k:k + TT], scalar=fcw.t[:, layer, ch, k:k + 1], in1=acc.t[:],
                                                          op0=ALU.mult, op1=ALU.add), reads=[u, fcw, acc], writes=[acc])
            if not isval:
                g = gact[j]
                kb.op(A, lambda e: e.activation(out=g.t[:], in_=acc.t[:], func=AF.Silu, bias=fcb.t[:, layer, ch:ch + 1], scale=1.0),
                      reads=[acc, fcb], writes=[g])
            else:
                g = gact[j]
                kb.op(V, lambda e: e.scalar_tensor_tensor(out=act.t[:, c, :], in0=acc.t[:], scalar=fcb.t[:, layer, ch:ch + 1], in1=g.t[:],
                                                          op0=ALU.add, op1=ALU.mult), reads=[acc, fcb, g], writes=[act])

        linT(wn, D, blocks, lambda kc: hT_b.t[:, kc, :], [hT_b], epi)
        linT("dn%d" % layer, DFF, blocks_dm, lambda kc: act.t[:, kc, :], [act], resid_epi)

    for it in range(NT):
        tok0 = it * TT
        with ExitStack() as pes:
            xin = [kb.sb("xin%d" % i, [128, D], F32, pes) for i in range(2)]
            for s in range(4):
                xt = xin[s % 2]
                kb.dma(G, xt.t[:], x_d[tok0 + s * 128: tok0 + (s + 1) * 128, :], xt, writes=[xt])
                for b4 in range(4 if STOP != "x2" else 0):
                    ps = PS[4 + (b4 % 2)]
                    for j in range(4):
                        kc = b4 * 4 + j
                        kb.mm(ps, ps.t[:, j * 128:(j + 1) * 128], xt.t[:, kc * 128:(kc + 1) * 128], ident_f.t[:], start=True, stop=True,
                              reads=[xt, ident_f], sig=(j == 3), transpose=True)
                    src = ps.t[:].rearrange("p (j t) -> p j t", j=4)
                    kb.op(A, lambda e: e.activation(out=hT_f.t[:, b4 * 4:(b4 + 1) * 4, s * 128:(s + 1) * 128], in_=src, func=AF.Copy),
                          reads=[ps], writes=[hT_f])
                    kb.op(V, lambda e: e.tensor_copy(out=hT_b.t[:, b4 * 4:(b4 + 1) * 4, s * 128:(s + 1) * 128], in_=src),
                          reads=[ps], writes=[hT_b])
            kb.barrier()
        if STOP in ("x1", "x2"):
            kb.finish(); es.close(); return nc
        if STOP in ("x", "x0"):
            break
        if it == 0:
            convert(["w_out", "up0", "dn0", "w_kv", "w_q", "w_o", "up1", "dn1"])

        with ExitStack() as pes:
            u_s = [kb.sb("m_u%d" % i, [128, 3 + TT], F32, pes) for i in range(2)]
            acc_s = [kb.sb("m_acc%d" % i, [128, TT], F32, pes) for i in range(2)]
            sT = [kb.sb("m_sT%d" % i, [128, TT], BF16, pes) for i in range(2)]
            xg = kb.sb("m_xg", [128, 4, 512], BF16, pes); zg = kb.sb("m_zg", [128, 4, 512], BF16, pes)
            Btok = kb.sb("m_Btok", [128, 4, 128], BF16, pes)
            BT = kb.sb("m_BT", [128, TT], BF16, pes); CT = kb.sb("m_CT", [128, TT], BF16, pes)
            dtT = kb.sb("m_dtT", [64, TT], F32, pes)
            dt_t = kb.sb("m_dt", [128, 4, 64], F32, pes); da_t = kb.sb("m_da", [128, 4, 64], F32, pes)
            E_t = kb.sb("m_E", [128, 4, 64], F32, pes); dtx_t = kb.sb("m_dtx", [128, 4, 64], F32, pes)
            Dt_t = kb.sb("m_Dt", [128, 4, 64], F32, pes)
            S_b = kb.sb("m_Sb", [128, 512], BF16, pes)
            lt = [kb.sb("m_lt%d" % i, [128, 128], F32, pes) for i in range(2)]
            Lx = [kb.sb("m_Lx%d" % i, [128, 512], F32, pes) for i in range(2)]
            CBm = kb.sb("m_CBm", [128, 128], F32, pes)
            MT = [kb.sb("m_MT%d" % i, [128, 512], BF16, pes) for i in range(2)]
            xdt = kb.sb("m_xdt", [128, 512], BF16, pes); xdte = kb.sb("m_xdte", [128, 512], BF16, pes)
            y1 = kb.sb("m_y1", [128, 512], F32, pes); y2 = kb.sb("m_y2", [128, 512], F32, pes)
            junk = kb.sb("m_junk", [128, 512], F32, pes)
            ss = kb.sb("m_ss", [128, 1], F32, pes); rs = kb.sb("m_rs", [128, 1], F32, pes)
            yn = kb.sb("m_yn", [128, 512], BF16, pes)
            nrm_g = [kb.sb("m_nrm%d" % i, [128, 512], F32, pes) for i in range(2)]

            def dt_epi(bi, j, ps):
                kb.op(A, lambda e: e.activation(out=dtT.t[:], in_=ps.t[0:64, :], func=AF.Exp, bias=dtb.t[:, 0:1], scale=1.0),
                      reads=[ps, dtb], writes=[dtT])
                kb.op(A, lambda e: e.activation(out=dtT.t[:], in_=dtT.t[:], func=AF.Ln, bias=1.0, scale=1.0), reads=[dtT], writes=[dtT])
            linT("w_in", D, [[(10240, 64)]], lambda kc: hT_b.t[:, kc, :], [hT_b], dt_epi)
            psd = PS[4]
            for s in range(4):
                kb.mm(psd, psd.t[:, s * 64:(s + 1) * 64], dtT.t[:, s * 128:(s + 1) * 128], ident_f.t[0:64, 0:64], start=True, stop=True,
                      reads=[dtT, ident_f], sig=(s == 3), transpose=True)
            kb.op(V, lambda e: e.tensor_copy(out=dt_t.t[:], in_=psd.t[:, 0:256].rearrange("p (s h) -> p s h", s=4)), reads=[psd], writes=[dt_t])
            kb.op(V, lambda e: e.tensor_tensor(out=da_t.t[:], in0=dt_t.t[:], in1=arow.t[:].unsqueeze(1).to_broadcast([128, 4, 64]), op=ALU.mult),
                  reads=[dt_t, arow], writes=[da_t])
            pc, ptot = PS[5], PS[6]
            for s in range(4):
                kb.mm(pc, pc.t[:, s * 64:(s + 1) * 64], lhsT=tri_f.t[:], rhs=da_t.t[:, s, :], start=True, stop=True, reads=[tri_f, da_t], sig=(s == 3))
            for s in range(4):
                kb.mm(ptot, ptot.t[:, s * 64:(s + 1) * 64], lhsT=ones_f.t[:], rhs=da_t.t[:, s, :], start=True, stop=True, reads=[ones_f, da_t], sig=(s == 3))
            v4 = lambda b: b.t[:].rearrange("p s h -> p (s h)")
            kb.op(A, lambda e: e.activation(out=v4(E_t), in_=pc.t[:, 0:256], func=AF.Exp), reads=[pc], writes=[E_t])
            kb.op(A, lambda e: e.activation(out=v4(Dt_t), in_=ptot.t[:, 0:256], func=AF.Exp), reads=[ptot], writes=[Dt_t])
            kb.op(V, lambda e: e.tensor_copy(out=v4(dtx_t), in_=pc.t[:, 0:256]), reads=[pc], writes=[dtx_t])
            kb.op(V, lambda e: e.tensor_tensor(out=v4(dtx_t), in0=ptot.t[:, 0:256], in1=v4(dtx_t), op=ALU.subtract),
                  reads=[ptot, dtx_t], writes=[dtx_t])
            kb.op(A, lambda e: e.activation(out=v4(dtx_t), in_=v4(dtx_t), func=AF.Exp), reads=[dtx_t], writes=[dtx_t])
            kb.op(V, lambda e: e.tensor_tensor(out=v4(dtx_t), in0=v4(dtx_t), in1=v4(dt_t), op=ALU.mult), reads=[dtx_t, dt_t], writes=[dtx_t])

            cctr = [0]

            def conv_silu(ps, ch, dst):
                u = u_s[cctr[0] % 2]; acc = acc_s[cctr[0] % 2]; cctr[0] += 1
                kb.op(V, lambda e: e.tensor_copy(out=u.t[:, 0:3], in_=uhalo.t[:, ch, :]), reads=[uhalo], writes=[u])
                kb.op(A, lambda e: e.activation(out=u.t[:, 3:3 + TT], in_=ps.t[:], func=AF.Copy), reads=[ps], writes=[u])
                kb.op(V, lambda e: e.tensor_copy(out=uhalo.t[:, ch, :], in_=u.t[:, TT:TT + 3]), reads=[u], writes=[uhalo])
                kb.op(V, lambda e: e.tensor_scalar(out=acc.t[:], in0=u.t[:, 0:TT], scalar1=mcw.t[:, ch, 0:1], scalar2=None, op0=ALU.mult),
                      reads=[u, mcw], writes=[acc])
                for k in (1, 2, 3):
                    kb.op(V, lambda e: e.scalar_tensor_tensor(out=acc.t[:], in0=u.t[:, k:k + TT], scalar=mcw.t[:, ch, k:k + 1], in1=acc.t[:],
                                                              op0=ALU.mult, op1=ALU.add), reads=[u, mcw, acc], writes=[acc])
                kb.op(A, lambda e: e.activation(out=dst.t[:], in_=acc.t[:], func=AF.Silu, bias=mcb.t[:, ch:ch + 1], scale=1.0),
                      reads=[acc, mcb], writes=[dst])

            tctr = [0]

            def to_tokmajor(src, dst, col0):
                bank = 6 + (tctr[0] % 2); tctr[0] += 1
                ps = PS[bank]
                pv = psb(bank)
                for s in range(4):
                    kb.mm(ps, pv[:, s * 128:(s + 1) * 128], src.t[:, s * 128:(s + 1) * 128], ident_b.t[:], start=True, stop=True,
                          reads=[src, ident_b], sig=(s == 3), transpose=True)
                kb.op(A, lambda e: e.activation(out=dst.t[:, :, col0:col0 + 128], in_=pv[:, 0:512].rearrange("p (s c) -> p s c", s=4), func=AF.Copy),
                      reads=[ps], writes=[dst])

            for g in range(NG):
                nrmw = nrm_g[g % 2]
                kb.dma(G, nrmw.t[:], nrmw_d[:, g * 512:(g + 1) * 512], nrmw, writes=[nrmw])
                def x_epi(bi, j, ps, g=g):
                    c = bi * 2 + j
                    st = sT[c % 2]
                    conv_silu(ps, g * 4 + c, st)
                    to_tokmajor(st, xg, c * 128)
                linT("w_in", D, [[(DI + g * 512, 256)], [(DI + g * 512 + 256, 256)]], lambda kc: hT_b.t[:, kc, :], [hT_b], x_epi)

                def bc_epi(bi, j, ps, g=g):
                    if j == 0:
                        conv_silu(ps, 32 + g, BT)
                        bank = 6 + (tctr[0] % 2); tctr[0] += 1
                        pst = PS[bank]; pv = psb(bank)
                        for s in range(4):
                            kb.mm(pst, pv[:, s * 128:(s + 1) * 128], BT.t[:, s * 128:(s + 1) * 128], ident_b.t[:], start=True, stop=True,
                                  reads=[BT, ident_b], sig=(s == 3), transpose=True)
                        kb.op(A, lambda e: e.activation(out=Btok.t[:], in_=pv[:, 0:512].rearrange("p (s c) -> p s c", s=4), func=AF.Copy),
                              reads=[pst], writes=[Btok])
                    else:
                        conv_silu(ps, 40 + g, CT)
                linT("w_in", D, [[(2 * DI + g * 128, 128), (2 * DI + 1024 + g * 128, 128)]], lambda kc: hT_b.t[:, kc, :], [hT_b], bc_epi)

                def z_epi(bi, j, ps, g=g):
                    c = bi * 2 + j
                    st = sT[c % 2]
                    kb.op(A, lambda e: e.activation(out=st.t[:], in_=ps.t[:], func=AF.Silu), reads=[ps], writes=[st])
                    to_tokmajor(st, zg, c * 128)
                linT("w_in", D, [[(g * 512, 256)], [(g * 512 + 256, 256)]], lambda kc: hT_b.t[:, kc, :], [hT_b], z_epi)

                kb.op(A, lambda e: e.activation(out=S_b.t[:], in_=S_f.t[:, g, :], func=AF.Copy), reads=[S_f], writes=[S_b])
                hs = slice(g * 8, (g + 1) * 8)
                for s in range(4):
                    cs = slice(s * 128, (s + 1) * 128)
                    pcb = PS[2]
                    kb.mm(pcb, pcb.t[:, 0:128], lhsT=BT.t[:, cs], rhs=CT.t[:, cs], start=True, stop=True, reads=[BT, CT])
                    kb.op(V, lambda e: e.tensor_tensor(out=CBm.t[:], in0=pcb.t[:, 0:128], in1=tri_f.t[:], op=ALU.mult),
                          reads=[pcb, tri_f], writes=[CBm])
                    kb.op(V, lambda e: e.tensor_tensor(out=xdt.t[:].rearrange("p (h d) -> p h d", h=8), in0=xg.t[:, s, :].rearrange("p (h d) -> p h d", h=8),
                                                       in1=dt_t.t[:, s, hs].unsqueeze(2).to_broadcast([128, 8, 64]), op=ALU.mult),
                          reads=[xg, dt_t], writes=[xdt])
                    kb.op(V, lambda e: e.tensor_tensor(out=xdte.t[:].rearrange("p (h d) -> p h d", h=8), in0=xg.t[:, s, :].rearrange("p (h d) -> p h d", h=8),
                                                       in1=dtx_t.t[:, s, hs].unsqueeze(2).to_broadcast([128, 8, 64]), op=ALU.mult),
                          reads=[xg, dtx_t], writes=[xdte])
                    py = PS[3]
                    for hh in range(2):
                        pseg = PS[hh]
                        for q in range(4):
                            h = g * 8 + hh * 4 + q
                            l = lt[q % 2]
                            kb.op(V, lambda e: e.tensor_scalar(out=l.t[:], in0=mgt_f.t[:], scalar1=da_t.t[:, s, h:h + 1], scalar2=None, op0=ALU.mult),
                                  reads=[mgt_f, da_t], writes=[l])
                            kb.mm(pseg, pseg.t[:, q * 128:(q + 1) * 128], lhsT=l.t[:], rhs=tri_f.t[:], start=True, stop=True,
                                  reads=[l, tri_f], sig=True)
                        L = Lx[hh]; M = MT[hh]
                        kb.op(A, lambda e: e.activation(out=L.t[:], in_=pseg.t[:], func=AF.Exp), reads=[pseg], writes=[L])
                        kb.op(V, lambda e: e.tensor_tensor(out=M.t[:].rearrange("p (q l) -> p q l", q=4), in0=L.t[:].rearrange("p (q l) -> p q l", q=4),
                                                           in1=CBm.t[:].unsqueeze(1).to_broadcast([128, 4, 128]), op=ALU.mult),
                              reads=[L, CBm], writes=[M])
                        for q in range(4):
                            hl = hh * 4 + q
                            kb.mm(py, py.t[:, hl * 64:(hl + 1) * 64], lhsT=M.t[:, q * 128:(q + 1) * 128], rhs=xdt.t[:, hl * 64:(hl + 1) * 64],
                                  start=True, stop=True, reads=[M, xdt], sig=(q == 3))
                    poff = PS[4]
                    kb.mm(poff, poff.t[:], lhsT=CT.t[:, cs], rhs=S_b.t[:], start=True, stop=True, reads=[CT, S_b])
                    v3 = lambda ap: ap.rearrange("p (h d) -> p h d", h=8)
                    kb.op(V, lambda e: e.tensor_tensor(out=v3(y1.t[:]), in0=v3(poff.t[:]), in1=E_t.t[:, s, hs].unsqueeze(2).to_broadcast([128, 8, 64]), op=ALU.mult),
                          reads=[poff, E_t], writes=[y1])
                    kb.op(V, lambda e: e.tensor_tensor(out=y1.t[:], in0=y1.t[:], in1=py.t[:], op=ALU.add), reads=[y1, py], writes=[y1])
                    kb.op(V, lambda e: e.tensor_tensor(out=v3(y2.t[:]), in0=v3(xg.t[:, s, :]), in1=drow.t[:, hs].unsqueeze(2).to_broadcast([128, 8, 64]), op=ALU.mult),
                          reads=[xg, drow], writes=[y2])
                    kb.op(V, lambda e: e.tensor_tensor(out=y1.t[:], in0=y1.t[:], in1=y2.t[:], op=ALU.add), reads=[y1, y2], writes=[y1])
                    kb.op(V, lambda e: e.tensor_tensor(out=y1.t[:], in0=y1.t[:], in1=zg.t[:, s, :], op=ALU.mult), reads=[y1, zg], writes=[y1])
                    kb.op(A, lambda e: e.activation(out=junk.t[:], in_=y1.t[:], func=AF.Square), reads=[y1], writes=[junk])
                    kb.op(V, lambda e: e.reduce_sum(out=ss.t[:], in_=junk.t[:], axis=AX.X), reads=[junk], writes=[ss])
                    kb.op(V, lambda e: e.tensor_scalar(out=rs.t[:], in0=ss.t[:], scalar1=1.0 / 512, scalar2=EPS, op0=ALU.mult, op1=ALU.add),
                          reads=[ss], writes=[rs])
                    kb.op(A, lambda e: e.activation(out=rs.t[:], in_=rs.t[:], func=AF.Sqrt), reads=[rs], writes=[rs])
                    kb.op(V, lambda e: e.reciprocal(out=rs.t[:], in_=rs.t[:]), reads=[rs], writes=[rs])
                    kb.op(V, lambda e: e.scalar_tensor_tensor(out=yn.t[:], in0=y1.t[:], scalar=rs.t[:, 0:1], in1=nrmw.t[:],
                                                              op0=ALU.mult, op1=ALU.mult), reads=[y1, rs, nrmw], writes=[yn])
                    bank = 6 + (tctr[0] % 2); tctr[0] += 1
                    pst = PS[bank]; pv = psb(bank)
                    for q in range(4):
                        kb.mm(pst, pv[:, q * 128:(q + 1) * 128], yn.t[:, q * 128:(q + 1) * 128], ident_b.t[:], start=True, stop=True,
                              reads=[yn, ident_b], sig=(q == 3), transpose=True)
                    kb.op(A, lambda e: e.activation(out=bigT.t[:, g * 4:(g + 1) * 4, cs], in_=pv[:, 0:512].rearrange("p (q t) -> p q t", q=4), func=AF.Copy),
                          reads=[pst], writes=[bigT])
                    pstt = PS[5]
                    kb.mm(pstt, pstt.t[:], lhsT=Btok.t[:, s, :], rhs=xdte.t[:], start=True, stop=True, reads=[Btok, xdte])
                    kb.op(V, lambda e: e.tensor_tensor(out=v3(S_f.t[:, g, :]), in0=v3(S_f.t[:, g, :]), in1=Dt_t.t[:, s, hs].unsqueeze(2).to_broadcast([128, 8, 64]), op=ALU.mult),
                          reads=[S_f, Dt_t], writes=[S_f])
                    kb.op(V, lambda e: e.tensor_tensor(out=S_f.t[:, g, :], in0=S_f.t[:, g, :], in1=pstt.t[:], op=ALU.add), reads=[S_f, pstt], writes=[S_f])
                    kb.op(A, lambda e: e.activation(out=S_b.t[:], in_=S_f.t[:, g, :], func=AF.Copy), reads=[S_f], writes=[S_b])
            kb.barrier()

        linT("w_out", DI, blocks_dm, lambda kc: bigT.t[:, kc, :], [bigT], resid_epi)
        layer_norm(0)
        if STOP == "hmid0":
            break
        kb.barrier()
        with ExitStack() as pes:
            ffn(0, pes)
            layer_norm(1)
            kb.barrier()
        if STOP == "h1":
            break

        with ExitStack() as pes:
            kst = [kb.sb("kv_k%d" % i, [128, TT], BF16, pes) for i in range(2)]
            vT = [kb.sb("kv_vT%d" % i, [128, TT], BF16, pes) for i in range(2)]
            vst = [kb.sb("kv_v%d" % i, [128, 4, 129], BF16, pes) for i in range(2)]
            kms = kb.sb("kv_kms", [128, 2], F32, pes)
            for b in vst:
                kb.op(G, lambda e: e.memset(b.t[:], 1.0), writes=[b])

            def kv_epi(bi, j, ps):
                ch = bi * 2 + j
                if ch < 16:
                    h = ch
                    k = kst[h % 2]
                    kb.op(A, lambda e: e.activation(out=k.t[:], in_=ps.t[:], func=AF.Copy), reads=[ps], writes=[k])
                    kb.dma(G, KT_d[h, :, tok0:tok0 + TT], k.t[:], k, reads=[k])
                    kb.op(V, lambda e: e.tensor_reduce(out=kms.t[:], in_=ps.t[:].rearrange("p (b t) -> p b t", b=2), axis=AX.X, op=ALU.add),
                          reads=[ps], writes=[kms])
                    kb.op(V, lambda e: e.tensor_scalar(out=kmT.t[:, h, 2 * it:2 * it + 2], in0=kms.t[:], scalar1=1.0 / 256, scalar2=None, op0=ALU.mult),
                          reads=[kms], writes=[kmT])
                else:
                    h = ch - 16
                    vt = vT[h % 2]; vs = vst[h % 2]
                    kb.op(A, lambda e: e.activation(out=vt.t[:], in_=ps.t[:], func=AF.Copy), reads=[ps], writes=[vt])
                    bank = 6 + (h % 2)
                    pst = PS[bank]; pv = psb(bank)
                    for s in range(4):
                        kb.mm(pst, pv[:, s * 128:(s + 1) * 128], vt.t[:, s * 128:(s + 1) * 128], ident_b.t[:], start=True, stop=True,
                              reads=[vt, ident_b], sig=(s == 3), transpose=True)
                    kb.op(V, lambda e: e.tensor_copy(out=vs.t[:, :, 0:128], in_=pv[:, 0:512].rearrange("p (s c) -> p s c", s=4)), reads=[pst], writes=[vs])
                    kb.dma(G, V_d[h, 4 * it:4 * it + 4, :, :].rearrange("s p c -> p s c"), vs.t[:], vs, reads=[vs])
            linT("w_kv", D, [[(c0, 256)] for c0 in range(0, 2 * D, 256)], lambda kc: hT_b.t[:, kc, :], [hT_b], kv_epi)
            kb.barrier()

        with ExitStack() as pes:
            nk = (it + 1) * TT
            nkt = nk // 128
            qT = kb.sb("a_qT", [128, HEADS, TT], BF16, pes)
            kbuf = [kb.sb("a_k%d" % i, [128, S], BF16, pes) for i in range(1)]
            vbuf = [kb.sb("a_v%d" % i, [128, S // 128, 129], BF16, pes) for i in range(1)]
            gm = kb.sb("a_gm", [128, 4, 16], F32, pes); mx8 = kb.sb("a_mx8", [128, 8], F32, pes)
            mb = kb.sb("a_mb", [128, 4, 16], BF16, pes); mbf = kb.sb("a_mbf", [128, 4, 16], F32, pes)
            negm = kb.sb("a_negm", [16, TT], BF16, pes)
            PT = [kb.sb("a_PT%d" % i, [128, TT], BF16, pes) for i in range(3)]
            dtmp = kb.sb("a_dtmp", [128, 128], F32, pes)
            num = kb.sb("a_num", [128, 129], F32, pes); rec = kb.sb("a_rec", [128, 1], F32, pes)
            otok = kb.sb("a_otok", [128, 4, 128], BF16, pes)
            dmat = kb.sb("a_dmat", [128, HEADS, 128], F32, pes)
            kb.dma(G, dmat.t[:], dmat_d.rearrange("p (h q) -> p h q", h=HEADS), dmat, writes=[dmat])

            def q_epi(bi, j, ps):
                h = bi * 2 + j
                kb.op(A, lambda e: e.activation(out=qT.t[:, h, :], in_=ps.t[:], func=AF.Copy), reads=[ps], writes=[qT])
            linT("w_q", D, blocks_dm, lambda kc: hT_b.t[:, kc, :], [hT_b], q_epi)

            pctr = [0]
            for h in range(HEADS):
                kbf = kbuf[0]; vbf = vbuf[0]
                kb.dma(G, kbf.t[:, 0:nk], KT_d[h, :, 0:nk], kbf, writes=[kbf])
                kb.dma(G, vbf.t[:, 0:nkt, :], V_d[h, 0:nkt, :, :].rearrange("s p c -> p s c"), vbf, writes=[vbf])
                pg = PS[7]
                for s in range(4):
                    kb.mm(pg, pg.t[:, s * 16:(s + 1) * 16], lhsT=qT.t[:, h, s * 128:(s + 1) * 128], rhs=kmT.t[:, h, :], start=True, stop=True,
                          reads=[qT, kmT], sig=(s == 3))
                kb.op(V, lambda e: e.tensor_tensor(out=gm.t[:].rearrange("p s n -> p (s n)"), in0=pg.t[:, 0:64], in1=vbias.t[:, it, :], op=ALU.add),
                      reads=[pg, vbias], writes=[gm])
                for s in range(4):
                    kb.op(V, lambda e: e.max(out=mx8.t[:], in_=gm.t[:, s, :]), reads=[gm], writes=[mx8])
                    kb.op(V, lambda e: e.tensor_scalar(out=mbf.t[:, s, :], in0=gm.t[:, s, :], scalar1=mx8.t[:, 2:3], scalar2=-NEG, op0=ALU.is_ge, op1=ALU.mult),
                          reads=[gm, mx8], writes=[mbf])
                    kb.op(V, lambda e: e.scalar_tensor_tensor(out=mb.t[:, s, :], in0=mbf.t[:, s, :], scalar=NEG, in1=vbias.t[:, it, s * 16:(s + 1) * 16],
                                                              op0=ALU.add, op1=ALU.add), reads=[mbf, vbias], writes=[mb])
                pm = PS[7]; pmv = psb(7)
                for s in range(4):
                    kb.mm(pm, pmv[0:16, s * 128:(s + 1) * 128], mb.t[:, s, :], ident_b.t[:], start=True, stop=True, reads=[mb, ident_b], sig=(s == 3), transpose=True)
                kb.op(V, lambda e: e.tensor_copy(out=negm.t[:], in_=pmv[0:16, 0:512]), reads=[pm], writes=[negm])

                poffs = [PS[2], PS[3], PS[4], PS[5]]
                pdiag = [PS[6], PS[6], PS[6], PS[6]]
                ocol = [0, 0, 0, 0]
                n_off = [4 * it + s for s in range(4)]
                done_off = [0, 0, 0, 0]

                def pv_mm(s, kt, P, diag):
                    if diag:
                        ps = pdiag[s]
                        kb.mm(ps, ps.t[:, ocol[s]:ocol[s] + 129], lhsT=P, rhs=vbf.t[:, kt, :], start=True, stop=True, reads=[vbf] + [PTcur[0]])
                    else:
                        ps = poffs[s]
                        first = done_off[s] == 0
                        done_off[s] += 1
                        last = done_off[s] == n_off[s]
                        kb.mm(ps, ps.t[:, ocol[s]:ocol[s] + 129], lhsT=P, rhs=vbf.t[:, kt, :], start=first, stop=last, reads=[vbf] + [PTcur[0]], sig=True)

                PTcur = [None]
                for kt in range(4 * it):
                    n = kt // 2
                    pss = PS[pctr[0] % 2]; P = PT[pctr[0] % 3]; pctr[0] += 1
                    PTcur[0] = P
                    kb.mm(pss, pss.t[:], lhsT=kbf.t[:, kt * 128:(kt + 1) * 128], rhs=qT.t[:, h, :], start=True, stop=False, reads=[kbf, qT], sig=False)
                    kb.mm(pss, pss.t[:], lhsT=esel.t[:, n, :], rhs=negm.t[:], start=False, stop=True, reads=[esel, negm], sig=True)
                    for s in range(4):
                        jj = 4 * it + s - kt
                        kb.op(A, lambda e: e.activation(out=P.t[:, s * 128:(s + 1) * 128], in_=pss.t[:, s * 128:(s + 1) * 128], func=AF.Exp,
                                                        bias=albias.t[:, h, jj:jj + 1], scale=SCALE), reads=[pss, albias], writes=[P])
                    for s in range(4):
                        pv_mm(s, kt, P.t[:, s * 128:(s + 1) * 128], False)
                for s in range(4):
                    own = 2 * it + s // 2
                    for sp in range(s + 1):
                        kt = 4 * it + sp
                        n = kt // 2
                        pss = PS[pctr[0] % 2]; P = PT[pctr[0] % 3]; pctr[0] += 1
                        PTcur[0] = P
                        qs = slice(s * 128, (s + 1) * 128)
                        need_mask = (n < own)
                        kb.mm(pss, pss.t[:, 0:128], lhsT=kbf.t[:, kt * 128:(kt + 1) * 128], rhs=qT.t[:, h, qs], start=True, stop=not need_mask,
                              reads=[kbf, qT], sig=not need_mask)
                        if need_mask:
                            kb.mm(pss, pss.t[:, 0:128], lhsT=esel.t[:, n, :], rhs=negm.t[:, qs], start=False, stop=True, reads=[esel, negm], sig=True)
                        if sp < s:
                            jj = s - sp
                            kb.op(A, lambda e: e.activation(out=P.t[:, 0:128], in_=pss.t[:, 0:128], func=AF.Exp, bias=albias.t[:, h, jj:jj + 1], scale=SCALE),
                                  reads=[pss, albias], writes=[P])
                            pv_mm(s, kt, P.t[:, 0:128], False)
                        else:
                            kb.op(V, lambda e: e.scalar_tensor_tensor(out=dtmp.t[:], in0=pss.t[:, 0:128], scalar=SCALE, in1=dmat.t[:, h, :],
                                                                      op0=ALU.mult, op1=ALU.add), reads=[pss, dmat], writes=[dtmp])
                            kb.op(A, lambda e: e.activation(out=P.t[:, 0:128], in_=dtmp.t[:], func=AF.Exp), reads=[dtmp], writes=[P])
                            pv_mm(s, kt, P.t[:, 0:128], True)
                    pd = pdiag[s]; po = poffs[s]; oc = ocol[s]
                    kb.op(V, lambda e: e.tensor_copy(out=num.t[:], in_=pd.t[:, oc:oc + 129]), reads=[pd], writes=[num])
                    if n_off[s] > 0:
                        kb.op(V, lambda e: e.scalar_tensor_tensor(out=num.t[:], in0=po.t[:, oc:oc + 129], scalar=fq.t[:, h:h + 1], in1=num.t[:],
                                                                  op0=ALU.mult, op1=ALU.add), reads=[po, num, fq], writes=[num])
                    kb.op(V, lambda e: e.reciprocal(out=rec.t[:], in_=num.t[:, 128:129]), reads=[num], writes=[rec])
                    kb.op(V, lambda e: e.tensor_scalar(out=otok.t[:, s, :], in0=num.t[:, 0:128], scalar1=rec.t[:, 0:1], scalar2=None, op0=ALU.mult),
                          reads=[num, rec], writes=[otok])
                pt = PS[7]; ptv = psb(7)
                for s in range(4):
                    kb.mm(pt, ptv[:, s * 128:(s + 1) * 128], otok.t[:, s, :], ident_b.t[:], start=True, stop=True, reads=[otok, ident_b], sig=(s == 3), transpose=True)
                kb.op(A, lambda e: e.activation(out=bigT.t[:, h, :], in_=ptv[:, 0:512], func=AF.Copy), reads=[pt], writes=[bigT])
            kb.barrier()
        linT("w_o", D, blocks_dm, lambda kc: bigT.t[:, kc, :], [bigT], resid_epi)
        layer_norm(2)
        if STOP == "hmid1":
            break
        kb.barrier()
        with ExitStack() as pes:
            ffn(1, pes)
            layer_norm(3)
            kb.barrier()
        with ExitStack() as pes:
            ost = [kb.sb("o_st%d" % i, [128, D], F32, pes) for i in range(2)]
            for s in range(4):
                o = ost[s % 2]
                for b4 in range(4):
                    ps = PS[4 + (b4 % 2)]
                    for j in range(4):
                        kc = b4 * 4 + j
                        kb.mm(ps, ps.t[:, j * 128:(j + 1) * 128], hT_f.t[:, kc, s * 128:(s + 1) * 128], ident_f.t[:], start=True, stop=True,
                              reads=[hT_f, ident_f], sig=(j == 3), transpose=True)
                    kb.op(A if b4 % 2 else V, (lambda e: e.activation(out=o.t[:, b4 * 512:(b4 + 1) * 512], in_=ps.t[:], func=AF.Copy)) if b4 % 2 else
                          (lambda e: e.tensor_copy(out=o.t[:, b4 * 512:(b4 + 1) * 512], in_=ps.t[:])), reads=[ps], writes=[o])
                kb.dma(G, out_d[tok0 + s * 128: tok0 + (s + 1) * 128, :], o.t[:], o, reads=[o])
            kb.barrier()

    if STOP is not None:
        with ExitStack() as pes:
            ost = [kb.sb("dbg_st%d" % i, [128, D], F32, pes) for i in range(2)]
            for s in range(4):
                o = ost[s % 2]
                for b4 in range(4):
                    ps = PS[4 + (b4 % 2)]
                    for j in range(4):
                        kc = b4 * 4 + j
                        kb.mm(ps, ps.t[:, j * 128:(j + 1) * 128], hT_f.t[:, kc, s * 128:(s + 1) * 128], ident_f.t[:], start=True, stop=True,
                              reads=[hT_f, ident_f], sig=(j == 3), transpose=True)
                    kb.op(V, lambda e: e.tensor_copy(out=o.t[:, b4 * 512:(b4 + 1) * 512], in_=ps.t[:]), reads=[ps], writes=[o])
                kb.dma(G, out_d[tok0 + s * 128: tok0 + (s + 1) * 128, :], o.t[:], o, reads=[o])
    kb.finish()
    es.close()
    return nc


def host_consts(inp):
    f = np.float32
    c = {}

    def percol(v, nch):
        return np.ascontiguousarray(v.reshape(nch, 128).T).astype(f)
    mcw = inp["mamba_conv_w"][0]
    c["mcw"] = np.ascontiguousarray(mcw.reshape(4, 48, 128).transpose(2, 1, 0)).reshape(128, 48 * 4).astype(f)
    c["mcb"] = percol(inp["mamba_conv_b"][0], 48)
    fw = inp["ffn_conv_w"]
    c["fcw"] = np.ascontiguousarray(fw.reshape(2, 3, 88, 128).transpose(3, 0, 2, 1)).reshape(128, 2 * 88 * 3).astype(f)
    c["fcb"] = np.ascontiguousarray(inp["ffn_conv_b"].reshape(2, 88, 128).transpose(2, 0, 1)).reshape(128, 2 * 88).astype(f)
    g = np.stack([inp["ln_mix_g"][0], inp["ln_ffn_g"][0], inp["ln_mix_g"][1], inp["ln_ffn_g"][1]])
    b = np.stack([inp["ln_mix_b"][0], inp["ln_ffn_b"][0], inp["ln_mix_b"][1], inp["ln_ffn_b"][1]])
    c["lng"] = np.ascontiguousarray(g.reshape(4, 16, 128).transpose(2, 0, 1)).reshape(128, 64).astype(f)
    c["lnb"] = np.ascontiguousarray(b.reshape(4, 16, 128).transpose(2, 0, 1)).reshape(128, 64).astype(f)
    c["dtb"] = np.ascontiguousarray(inp["mamba_dt_bias"][0].reshape(64, 1)).astype(f)
    return c


def kernel(**inputs):
    return _run(inputs)


_CACHE = {}


def _run(inputs, NT=8, STOP=None, NCORES=8):
    inp = {k: np.asarray(v) for k, v in inputs.items()}
    f = np.float32
    c = host_consts(inp)
    c["drow"] = np.ascontiguousarray(np.broadcast_to(inp["mamba_d"][0][None, :], (128, 64))).astype(f)
    c["nrmw"] = np.ascontiguousarray(np.broadcast_to(inp["mamba_norm_w"][0][None, :], (128, DI))).astype(f)
    alog = np.ascontiguousarray(np.broadcast_to(inp["mamba_a_log"][0][None, :], (128, 64))).astype(f)
    slopes = (2.0 ** (-8.0 * np.arange(1, HEADS + 1, dtype=np.float64) / HEADS))
    kk = np.arange(128, dtype=np.float64)
    alb = np.zeros((128, HEADS, 36), f)
    for j in range(1, 36):
        alb[:, :, j] = (-(slopes[None, :]) * (128.0 * j - kk[:, None])).astype(f)
    dm = np.zeros((128, HEADS, 128), f)
    qq = np.arange(128, dtype=np.float64)
    dist = qq[None, :] - kk[:, None]
    for h in range(HEADS):
        dm[:, h, :] = np.where(dist >= 0, -slopes[h] * dist, -1.0e4).astype(f)
    fqt = np.exp(-slopes[None, :] * qq[:, None]).astype(f)
    vb = np.zeros((8, 4, 16), f)
    for it in range(8):
        for s in range(4):
            own = 2 * it + s // 2
            vb[it, s, :] = np.where(np.arange(16) < own, 0.0, NEG)
    vbias = np.ascontiguousarray(np.broadcast_to(vb.reshape(1, 8 * 64), (128, 512))).astype(f)
    es_ = np.zeros((16, 16, 128), f)
    for n in range(16):
        es_[n, n, :] = 1.0
    key = (NT, STOP)
    if key not in _CACHE:
        _CACHE[key] = build(NT, STOP)
    nc = _CACHE[key]
    common = dict(
        w_in=inp["mamba_w_in"][0], w_out=inp["mamba_w_out"][0], up0=inp["ffn_w_up"][0], dn0=inp["ffn_w_down"][0],
        w_kv=inp["w_kv"], w_q=inp["attn_w_q"][0], w_o=inp["attn_w_o"][0], up1=inp["ffn_w_up"][1], dn1=inp["ffn_w_down"][1],
        mcw=c["mcw"], mcb=c["mcb"], fcw=c["fcw"], fcb=c["fcb"], lng=c["lng"], lnb=c["lnb"], dtb=c["dtb"],
        arow=alog, drow=c["drow"], nrmw=c["nrmw"], albias=alb.reshape(128, -1), dmat=dm.reshape(128, -1), fq=fqt,
        vbias=vbias, esel=es_.reshape(16, -1))
    common = {k: np.ascontiguousarray(v, dtype=f) for k, v in common.items()}
    in_maps = []
    for core in range(NCORES):
        m = dict(common)
        m["x"] = np.ascontiguousarray(inp["x"][core // 2], dtype=f)
        in_maps.append(m)
    res = run_bass_kernel_spmd(nc, in_maps, core_ids=list(range(NCORES)))
    if NCORES < 8:
        return res.results[0]["out"]
    out = np.stack([res.results[2 * b]["out"] for b in range(4)], axis=0)
    return out.astype(np.float32)
```

```python
import numpy as np
from contextlib import ExitStack
import concourse.bass as bass
import concourse.mybir as mybir
from concourse.bass_utils import run_bass_kernel_spmd

F32, BF16 = mybir.dt.float32, mybir.dt.bfloat16
AF = mybir.ActivationFunctionType
ALU = mybir.AluOpType
AX = mybir.AxisListType

D = 2048
S = 4096
TT = 512
KC = D // 128
DI = 4096
NH = 64
NG = 8
DFF = 5632
FC = DFF // 128
INP = 10304
HEADS = 16
ALPHA = 4.0 ** 0.25
EPS = 1e-5
SCALE = 128 ** -0.5
NEG = -30000.0


class Buf:
    __slots__ = ("name", "t", "w", "r", "ds", "ps")

    def __init__(self, name, t=None, ps=False):
        self.name, self.t, self.w, self.r, self.ds, self.ps = name, t, [], [], None, ps


def _key(ev):
    return (ev[0], ev[1])


class KB:
    def __init__(self, nc, es):
        self.nc, self.es = nc, es
        self.E = dict(pe=nc.tensor, act=nc.scalar, dve=nc.vector, pool=nc.gpsimd, sp=nc.sync)
        self.esem = {k: es.enter_context(nc.semaphore("e_" + k)) for k in self.E}
        self.cnt = {k: 0 for k in self.E}
        self.seen = {k: {} for k in self.E}
        self.dsems, self.dcnt = [], []
        self.skip_ds = set()
        self.ds_by_name = {}
        self.uid = 0

    def sb(self, name, shape, dt, es=None):
        self.uid += 1
        tname = name if es is None else "%s_u%d" % (name, self.uid)
        t = (es or self.es).enter_context(self.nc.sbuf_tensor(tname, shape, dt))
        return Buf(name, t)

    def _wait(self, eng, evs):
        for ev in evs:
            if ev[0] == 'e':
                if eng == 'pe' and ev[1] == 'pe':
                    continue
                key, val, sem = ev[1], ev[2], self.esem[ev[1]]
            else:
                key, val, sem = ('d', ev[1]), self.dcnt[ev[1]], self.dsems[ev[1]]
            if self.seen[eng].get(key, 0) < val:
                self.E[eng].wait_ge(sem, val)
                self.seen[eng][key] = val

    @staticmethod
    def _merge(lst, ev):
        k = _key(ev)
        return [e for e in lst if _key(e) != k] + [ev]

    def _deps(self, reads, writes):
        evs = []
        for b in reads:
            evs += b.w
            if b.ps:
                evs += b.r
        for b in writes:
            evs += b.w
            evs += b.r
        return evs

    def _commit(self, ev, reads, writes):
        for b in reads:
            b.r = self._merge(b.r, ev)
        for b in writes:
            b.w = self._merge(b.w, ev)
            b.r = []

    def op(self, eng, fn, reads=(), writes=()):
        self._wait(eng, self._deps(reads, writes))
        ins = fn(self.E[eng])
        self.cnt[eng] += 1
        ins.then_inc(self.esem[eng], 1)
        ev = ('e', eng, self.cnt[eng])
        self._commit(ev, reads, writes)
        return ev

    def mm(self, ps, out, lhsT, rhs, start, stop, reads=(), sig=None, transpose=False):
        if sig is None:
            sig = stop
        evs = self._deps(reads, ())
        if start:
            evs += ps.w + ps.r
        self._wait('pe', evs)
        if transpose:
            ins = self.nc.tensor.transpose(out, lhsT, rhs)
        else:
            ins = self.nc.tensor.matmul(out, lhsT=lhsT, rhs=rhs, start=start, stop=stop)
        ev = ('e', 'pe', self.cnt['pe'] + 1)
        if sig:
            self.cnt['pe'] += 1
            ins.then_inc(self.esem['pe'], 1)
        for b in reads:
            b.r = self._merge(b.r, ev)
        if stop:
            ps.w = self._merge(ps.w, ev)
            ps.r = []
        return ev

    def dma(self, q, out, in_, sem_buf, reads=(), writes=()):
        self._wait(q, self._deps(reads, writes))
        if sem_buf.ds is None:
            if sem_buf.name in self.ds_by_name:
                sem_buf.ds = self.ds_by_name[sem_buf.name]
            else:
                sem_buf.ds = len(self.dsems)
                self.ds_by_name[sem_buf.name] = sem_buf.ds
                self.dsems.append(self.es.enter_context(self.nc.semaphore("d_%d" % sem_buf.ds)))
                self.dcnt.append(0)
        i = sem_buf.ds
        ins = self.E[q].dma_start(out=out, in_=in_)
        self.dcnt[i] += 16
        ins.then_inc(self.dsems[i], 16)
        ev = ('d', i, self.dcnt[i])
        self._commit(ev, reads, writes)
        return ev

    def barrier(self, engs=('pe', 'act', 'dve', 'pool')):
        evs = [('e', k, self.cnt[k]) for k in self.E if self.cnt[k] > 0]
        for nm, i in self.ds_by_name.items():
            if nm.startswith("wb"):
                self.skip_ds.add(i)
        evs += [('d', i, 0) for i in range(len(self.dsems)) if i not in self.skip_ds]
        for e in engs:
            self._wait(e, evs)

    def finish(self):
        evs = [('e', k, self.cnt[k]) for k in self.E if self.cnt[k] > 0]
        evs += [('d', i, 0) for i in range(len(self.dsems))]
        self._wait('sp', evs)


def build(NT=8, STOP=None):
    nc = bass.Bass("TRN2", target_bir_lowering=False)
    es = ExitStack()
    kb = KB(nc, es)

    def din(name, shape, dt=F32):
        return nc.dram_tensor(name, list(shape), dt, kind="ExternalInput").ap()

    def dscr(name, shape, dt):
        return nc.dram_tensor(name, list(shape), dt, kind="Internal").ap()

    x_d = din("x", [S, D])
    W = {}
    wshapes = dict(w_in=[D, INP], w_out=[DI, D], up0=[D, 2 * DFF], dn0=[DFF, D], w_kv=[D, 2 * D],
                   w_q=[D, D], w_o=[D, D], up1=[D, 2 * DFF], dn1=[DFF, D])
    for n, shp in wshapes.items():
        W[n] = (din(n, shp), dscr(n + "_b", shp, BF16), Buf("wd_" + n))
    mcw_d = din("mcw", [128, 48 * 4]); mcb_d = din("mcb", [128, 48])
    fcw_d = din("fcw", [128, 2 * 88 * 3]); fcb_d = din("fcb", [128, 2 * 88])
    lng_d = din("lng", [128, 4 * 16]); lnb_d = din("lnb", [128, 4 * 16])
    dtb_d = din("dtb", [64, 1]); arow_d = din("arow", [128, 64]); drow_d = din("drow", [128, 64])
    nrmw_d = din("nrmw", [128, DI])
    albias_d = din("albias", [128, HEADS * 36]); dmat_d = din("dmat", [128, HEADS * 128]); fq_d = din("fq", [128, HEADS])
    vbias_d = din("vbias", [128, 8 * 64]); esel_d = din("esel", [16, 16 * 128])
    out_d = nc.dram_tensor("out", [S, D], F32, kind="ExternalOutput").ap()
    KT_d = dscr("KT", [HEADS, 128, S], BF16)
    V_d = dscr("V", [HEADS, S // 128, 128, 129], BF16)

    ident_f = kb.sb("ident_f", [128, 128], F32); ident_b = kb.sb("ident_b", [128, 128], BF16)
    tri_f = kb.sb("tri_f", [128, 128], F32); mgt_f = kb.sb("mgt_f", [128, 128], F32); ones_f = kb.sb("ones_f", [128, 128], F32)
    hT_f = kb.sb("hT_f", [128, KC, TT], F32); hT_b = kb.sb("hT_b", [128, KC, TT], BF16)
    NWB = 3
    wb = [kb.sb("wb%d" % i, [128, 16, 256], BF16) for i in range(NWB)]
    S_f = kb.sb("S_f", [128, NG, 512], F32)
    uhalo = kb.sb("uhalo", [128, 48, 3], F32); fhalo = kb.sb("fhalo", [128, 2, 88, 2], F32)
    mcw = kb.sb("mcw_s", [128, 48, 4], F32); mcb = kb.sb("mcb_s", [128, 48], F32)
    fcw = kb.sb("fcw_s", [128, 2, 88, 3], F32); fcb = kb.sb("fcb_s", [128, 2, 88], F32)
    lng = kb.sb("lng_s", [128, 4, 16], F32); lnb = kb.sb("lnb_s", [128, 4, 16], F32)
    dtb = kb.sb("dtb_s", [64, 1], F32); arow = kb.sb("arow_s", [128, 64], F32); drow = kb.sb("drow_s", [128, 64], F32)
    albias = kb.sb("albias_s", [128, HEADS, 36], F32)
    fq = kb.sb("fq_s", [128, HEADS], F32); vbias = kb.sb("vbias_s", [128, 8, 64], F32)
    esel = kb.sb("esel_s", [16, 16, 128], BF16)
    kmT = kb.sb("kmT", [128, HEADS, 16], BF16)
    bigT = kb.sb("bigT", [128, 32, TT], BF16)
    mean_s = kb.sb("mean_s", [128, TT], F32); rstd_s = kb.sb("rstd_s", [128, TT], F32)
    sq_s = [kb.sb("sq0", [128, TT], F32)] * 2
    lnt = [kb.sb("lnt%d" % i, [128, TT], F32) for i in range(2)]
    PS = []
    for i in range(8):
        t = es.enter_context(nc.psum_tensor("ps%d" % i, [128, 512], F32))
        PS.append(Buf("ps%d" % i, t, ps=True))

    def psb(i):
        return PS[i].t[:].bitcast(BF16)

    V, A, G = 'dve', 'act', 'pool'

    kb.op(G, lambda e: e.memset(ident_f.t[:], 1.0), writes=[ident_f])
    kb.op(G, lambda e: e.affine_select(out=ident_f.t[:], in_=ident_f.t[:], pattern=[[-1, 128]], compare_op=ALU.is_equal,
                                      fill=0.0, base=0, channel_multiplier=1), reads=[ident_f], writes=[ident_f])
    kb.op(V, lambda e: e.tensor_copy(out=ident_b.t[:], in_=ident_f.t[:]), reads=[ident_f], writes=[ident_b])
    kb.op(G, lambda e: e.memset(tri_f.t[:], 1.0), writes=[tri_f])
    kb.op(G, lambda e: e.affine_select(out=tri_f.t[:], in_=tri_f.t[:], pattern=[[1, 128]], compare_op=ALU.is_ge,
                                      fill=0.0, base=0, channel_multiplier=-1), reads=[tri_f], writes=[tri_f])
    kb.op(G, lambda e: e.memset(mgt_f.t[:], 1.0), writes=[mgt_f])
    kb.op(G, lambda e: e.affine_select(out=mgt_f.t[:], in_=mgt_f.t[:], pattern=[[-1, 128]], compare_op=ALU.is_gt,
                                      fill=0.0, base=0, channel_multiplier=1), reads=[mgt_f], writes=[mgt_f])
    kb.op(G, lambda e: e.memset(ones_f.t[:], 1.0), writes=[ones_f])
    kb.op(G, lambda e: e.memset(S_f.t[:], 0.0), writes=[S_f])
    kb.op(G, lambda e: e.memset(uhalo.t[:], 0.0), writes=[uhalo])
    kb.op(G, lambda e: e.memset(fhalo.t[:], 0.0), writes=[fhalo])

    if STOP == "c0":
        kb.finish(); es.close(); return nc

    def ld(buf, src, view=None):
        kb.dma(G, buf.t[:] if view is None else view, src, buf, writes=[buf])

    ld(mcw, mcw_d.rearrange("p (c k) -> p c k", k=4)); ld(mcb, mcb_d)
    ld(fcw, fcw_d.rearrange("p (l c k) -> p l c k", l=2, k=3)); ld(fcb, fcb_d.rearrange("p (l c) -> p l c", l=2))
    ld(lng, lng_d.rearrange("p (l c) -> p l c", l=4)); ld(lnb, lnb_d.rearrange("p (l c) -> p l c", l=4))
    ld(dtb, dtb_d); ld(arow, arow_d); ld(drow, drow_d)
    kb.op(A, lambda e: e.activation(out=arow.t[:], in_=arow.t[:], func=AF.Exp), reads=[arow], writes=[arow])
    kb.op(V, lambda e: e.tensor_scalar(out=arow.t[:], in0=arow.t[:], scalar1=-1.0, scalar2=None, op0=ALU.mult), reads=[arow], writes=[arow])
    kb.op(G, lambda e: e.memset(kmT.t[:], 0.0), writes=[kmT])
    ld(albias, albias_d.rearrange("p (h j) -> p h j", h=HEADS))
    ld(fq, fq_d); ld(vbias, vbias_d.rearrange("p (i c) -> p i c", i=8))
    ld(esel, esel_d.rearrange("p (n k) -> p n k", n=16))

    if STOP == "c1":
        kb.finish(); es.close(); return nc
    def convert(names):
        for n in names:
            src, dst, wbuf = W[n]
            K = src.shape[0]
            rb = 256
            for r0 in range(0, K, rb):
                kb.dma(G, dst[r0:r0 + rb, :], src[r0:r0 + rb, :], wbuf, writes=[wbuf])
            kb.skip_ds.add(wbuf.ds)
    if STOP not in ("x0", "x1", "x2"):
        convert(["w_in"])

    wctr = [0]

    def linT(wname, K, blocks, rhs_fn, rhs_bufs, epi, ps_banks=(0, 1, 2, 3)):
        _, wdst, wbuf = W[wname]
        nkb = (K + 2047) // 2048
        pending = []
        for bi, segs in enumerate(blocks):
            ncols = sum(s[1] for s in segs)
            nj = ncols // 128 if ncols >= 128 else 1
            banks = [PS[ps_banks[(2 * (bi % 2) + j) % len(ps_banks)]] for j in range(nj)]
            for kbi in range(nkb):
                k0 = kbi * 2048
                kcs = min(16, (K - k0) // 128)
                wt = wb[wctr[0] % NWB]
                wctr[0] += 1
                c = 0
                for (c0, n) in segs:
                    kb.dma('sp', wt.t[:, 0:kcs, c:c + n],
                           wdst[k0:k0 + kcs * 128, c0:c0 + n].rearrange("(kc p) n -> p kc n", p=128),
                           wt, reads=[wbuf], writes=[wt])
                    c += n
                for j in range(nj):
                    m = min(128, ncols)
                    for kc in range(kcs):
                        first = (kbi == 0 and kc == 0)
                        last = (kbi == nkb - 1 and kc == kcs - 1)
                        kb.mm(banks[j], banks[j].t[0:m, :], lhsT=wt.t[:, kc, j * 128:j * 128 + m], rhs=rhs_fn(k0 // 128 + kc),
                              start=first, stop=last, reads=[wt] + list(rhs_bufs), sig=(kc == kcs - 1))
            for f in pending:
                f()
            pending = [(lambda bi=bi, j=j, b=banks[j]: epi(bi, j, b)) for j in range(nj)]
        for f in pending:
            f()

    def layer_norm(li):
        ps_sum, ps_sq = PS[6], PS[7]
        for j in range(KC):
            sq = sq_s[j % 2]
            kb.op(A, lambda e: e.activation(out=sq.t[:], in_=hT_f.t[:, j, :], func=AF.Square), reads=[hT_f], writes=[sq])
            kb.mm(ps_sum, ps_sum.t[:], lhsT=ones_f.t[:], rhs=hT_f.t[:, j, :], start=(j == 0), stop=(j == KC - 1), reads=[ones_f, hT_f])
            kb.mm(ps_sq, ps_sq.t[:], lhsT=ones_f.t[:], rhs=sq.t[:], start=(j == 0), stop=(j == KC - 1), reads=[ones_f, sq], sig=True)
        kb.op(V, lambda e: e.tensor_scalar(out=mean_s.t[:], in0=ps_sum.t[:], scalar1=1.0 / D, scalar2=None, op0=ALU.mult),
              reads=[ps_sum], writes=[mean_s])
        t0 = lnt[0]
        kb.op(V, lambda e: e.tensor_tensor(out=t0.t[:], in0=mean_s.t[:], in1=mean_s.t[:], op=ALU.mult), reads=[mean_s], writes=[t0])
        kb.op(V, lambda e: e.scalar_tensor_tensor(out=rstd_s.t[:], in0=ps_sq.t[:], scalar=1.0 / D, in1=t0.t[:],
                                                  op0=ALU.mult, op1=ALU.subtract), reads=[ps_sq, t0], writes=[rstd_s])
        kb.op(V, lambda e: e.tensor_scalar(out=rstd_s.t[:], in0=rstd_s.t[:], scalar1=EPS, scalar2=None, op0=ALU.add),
              reads=[rstd_s], writes=[rstd_s])
        kb.op(A, lambda e: e.activation(out=rstd_s.t[:], in_=rstd_s.t[:], func=AF.Sqrt), reads=[rstd_s], writes=[rstd_s])
        kb.op(V, lambda e: e.reciprocal(out=rstd_s.t[:], in_=rstd_s.t[:]), reads=[rstd_s], writes=[rstd_s])
        for j in range(KC):
            t = lnt[j % 2]
            kb.op(V, lambda e: e.tensor_tensor(out=t.t[:], in0=hT_f.t[:, j, :], in1=mean_s.t[:], op=ALU.subtract),
                  reads=[hT_f, mean_s], writes=[t])
            kb.op(V, lambda e: e.tensor_tensor(out=t.t[:], in0=t.t[:], in1=rstd_s.t[:], op=ALU.mult), reads=[t, rstd_s], writes=[t])
            kb.op(V, lambda e: e.tensor_scalar(out=hT_f.t[:, j, :], in0=t.t[:], scalar1=lng.t[:, li, j:j + 1], scalar2=lnb.t[:, li, j:j + 1],
                                               op0=ALU.mult, op1=ALU.add), reads=[t, lng, lnb], writes=[hT_f])
            kb.op(A, lambda e: e.activation(out=hT_b.t[:, j, :], in_=hT_f.t[:, j, :], func=AF.Copy), reads=[hT_f], writes=[hT_b])

    def resid_epi(bi, j, ps):
        ch = bi * 2 + j
        kb.op(V, lambda e: e.scalar_tensor_tensor(out=hT_f.t[:, ch, :], in0=hT_f.t[:, ch, :], scalar=ALPHA, in1=ps.t[:],
                                                  op0=ALU.mult, op1=ALU.add), reads=[hT_f, ps], writes=[hT_f])

    blocks_dm = [[(c0, 256)] for c0 in range(0, D, 256)]

    def ffn(layer, pes):
        act = kb.sb("ffn_act", [128, FC, TT], BF16, pes)
        fu = [kb.sb("ffn_u%d" % i, [128, 2 + TT], F32, pes) for i in range(2)]
        facc = [kb.sb("ffn_acc%d" % i, [128, TT], F32, pes) for i in range(2)]
        gact = [kb.sb("ffn_g%d" % i, [128, TT], BF16, pes) for i in range(2)]
        wn = "up%d" % layer
        blocks = []
        for c in range(0, FC, 2):
            blocks.append([(c * 128, 256)])
            blocks.append([(DFF + c * 128, 256)])
        ctr = [0]

        def epi(bi, j, ps):
            isval = bi % 2
            c = (bi // 2) * 2 + j
            ch = isval * FC + c
            u = fu[ctr[0] % 2]; acc = facc[ctr[0] % 2]; ctr[0] += 1
            kb.op(V, lambda e: e.tensor_copy(out=u.t[:, 0:2], in_=fhalo.t[:, layer, ch, :]), reads=[fhalo], writes=[u])
            kb.op(A, lambda e: e.activation(out=u.t[:, 2:2 + TT], in_=ps.t[:], func=AF.Copy), reads=[ps], writes=[u])
            kb.op(V, lambda e: e.tensor_copy(out=fhalo.t[:, layer, ch, :], in_=u.t[:, TT:TT + 2]), reads=[u], writes=[fhalo])
            kb.op(V, lambda e: e.tensor_scalar(out=acc.t[:], in0=u.t[:, 0:TT], scalar1=fcw.t[:, layer, ch, 0:1], scalar2=None, op0=ALU.mult),
                  reads=[u, fcw], writes=[acc])
            for k in (1, 2):
                kb.op(V, lambda e: e.scalar_tensor_tensor(out=acc.t[:], in0=u.t[:, k:k + TT], scalar=fcw.t[:, layer, ch, k:k + 1], in1=acc.t[:],
                                                          op0=ALU.mult, op1=ALU.add), reads=[u, fcw, acc], writes=[acc])
            if not isval:
                g = gact[j]
                kb.op(A, lambda e: e.activation(out=g.t[:], in_=acc.t[:], func=AF.Silu, bias=fcb.t[:, layer, ch:ch + 1], scale=1.0),
                      reads=[acc, fcb], writes=[g])
            else:
                g = gact[j]
                kb.op(V, lambda e: e.scalar_tensor_tensor(out=act.t[:, c, :], in0=acc.t[:], scalar=fcb.t[:, layer, ch:ch + 1], in1=g.t[:],
                                                          op0=ALU.add, op1=ALU.mult), reads=[acc, fcb, g], writes=[act])

        linT(wn, D, blocks, lambda kc: hT_b.t[:, kc, :], [hT_b], epi)
        linT("dn%d" % layer, DFF, blocks_dm, lambda kc: act.t[:, kc, :], [act], resid_epi)

    for it in range(NT):
        tok0 = it * TT
        with ExitStack() as pes:
            xin = [kb.sb("xin%d" % i, [128, D], F32, pes) for i in range(2)]
            for s in range(4):
                xt = xin[s % 2]
                kb.dma(G, xt.t[:], x_d[tok0 + s * 128: tok0 + (s + 1) * 128, :], xt, writes=[xt])
                for b4 in range(4 if STOP != "x2" else 0):
                    ps = PS[4 + (b4 % 2)]
                    for j in range(4):
                        kc = b4 * 4 + j
                        kb.mm(ps, ps.t[:, j * 128:(j + 1) * 128], xt.t[:, kc * 128:(kc + 1) * 128], ident_f.t[:], start=True, stop=True,
                              reads=[xt, ident_f], sig=(j == 3), transpose=True)
                    src = ps.t[:].rearrange("p (j t) -> p j t", j=4)
                    kb.op(A, lambda e: e.activation(out=hT_f.t[:, b4 * 4:(b4 + 1) * 4, s * 128:(s + 1) * 128], in_=src, func=AF.Copy),
                          reads=[ps], writes=[hT_f])
                    kb.op(V, lambda e: e.tensor_copy(out=hT_b.t[:, b4 * 4:(b4 + 1) * 4, s * 128:(s + 1) * 128], in_=src),
                          reads=[ps], writes=[hT_b])
            kb.barrier()
        if STOP in ("x1", "x2"):
            kb.finish(); es.close(); return nc
        if STOP in ("x", "x0"):
            break
        if it == 0:
            convert(["w_out", "up0", "dn0", "w_kv", "w_q", "w_o", "up1", "dn1"])

        with ExitStack() as pes:
            u_s = [kb.sb("m_u%d" % i, [128, 3 + TT], F32, pes) for i in range(2)]
            acc_s = [kb.sb("m_acc%d" % i, [128, TT], F32, pes) for i in range(2)]
            sT = [kb.sb("m_sT%d" % i, [128, TT], BF16, pes) for i in range(2)]
            xg = kb.sb("m_xg", [128, 4, 512], BF16, pes); zg = kb.sb("m_zg", [128, 4, 512], BF16, pes)
            Btok = kb.sb("m_Btok", [128, 4, 128], BF16, pes)
            BT = kb.sb("m_BT", [128, TT], BF16, pes); CT = kb.sb("m_CT", [128, TT], BF16, pes)
            dtT = kb.sb("m_dtT", [64, TT], F32, pes)
            dt_t = kb.sb("m_dt", [128, 4, 64], F32, pes); da_t = kb.sb("m_da", [128, 4, 64], F32, pes)
            E_t = kb.sb("m_E", [128, 4, 64], F32, pes); dtx_t = kb.sb("m_dtx", [128, 4, 64], F32, pes)
            Dt_t = kb.sb("m_Dt", [128, 4, 64], F32, pes)
            S_b = kb.sb("m_Sb", [128, 512], BF16, pes)
            sr = [kb.sb("m_sr%d" % i, [128, 8, 128], F32, pes) for i in range(2)]
            Lg = [kb.sb("m_Lg%d" % i, [128, 1024], BF16, pes) for i in range(4)]
            CBm = kb.sb("m_CBm", [128, 128], F32, pes)
            MT = [kb.sb("m_MT%d" % i, [128, 512], BF16, pes) for i in range(2)]
            xdt = kb.sb("m_xdt", [128, 512], BF16, pes); xdte = kb.sb("m_xdte", [128, 512], BF16, pes)
            y1 = kb.sb("m_y1", [128, 512], F32, pes); y2 = kb.sb("m_y2", [128, 512], F32, pes)
            junk = kb.sb("m_junk", [128, 512], F32, pes)
            ss = kb.sb("m_ss", [128, 1], F32, pes); rs = kb.sb("m_rs", [128, 1], F32, pes)
            yn = kb.sb("m_yn", [128, 512], BF16, pes)
            nrm_g = [kb.sb("m_nrm0", [128, 512], F32, pes)] * 2

            def dt_epi(bi, j, ps):
                kb.op(A, lambda e: e.activation(out=dtT.t[:], in_=ps.t[0:64, :], func=AF.Exp, bias=dtb.t[:, 0:1], scale=1.0),
                      reads=[ps, dtb], writes=[dtT])
                kb.op(A, lambda e: e.activation(out=dtT.t[:], in_=dtT.t[:], func=AF.Ln, bias=1.0, scale=1.0), reads=[dtT], writes=[dtT])
            linT("w_in", D, [[(10240, 64)]], lambda kc: hT_b.t[:, kc, :], [hT_b], dt_epi)
            psd = PS[4]
            for s in range(4):
                kb.mm(psd, psd.t[:, s * 64:(s + 1) * 64], dtT.t[:, s * 128:(s + 1) * 128], ident_f.t[0:64, 0:64], start=True, stop=True,
                      reads=[dtT, ident_f], sig=(s == 3), transpose=True)
            kb.op(V, lambda e: e.tensor_copy(out=dt_t.t[:], in_=psd.t[:, 0:256].rearrange("p (s h) -> p s h", s=4)), reads=[psd], writes=[dt_t])
            kb.op(V, lambda e: e.tensor_tensor(out=da_t.t[:], in0=dt_t.t[:], in1=arow.t[:].unsqueeze(1).to_broadcast([128, 4, 64]), op=ALU.mult),
                  reads=[dt_t, arow], writes=[da_t])
            pc, ptot = PS[5], PS[6]
            for s in range(4):
                kb.mm(pc, pc.t[:, s * 64:(s + 1) * 64], lhsT=tri_f.t[:], rhs=da_t.t[:, s, :], start=True, stop=True, reads=[tri_f, da_t], sig=(s == 3))
            for s in range(4):
                kb.mm(ptot, ptot.t[:, s * 64:(s + 1) * 64], lhsT=ones_f.t[:], rhs=da_t.t[:, s, :], start=True, stop=True, reads=[ones_f, da_t], sig=(s == 3))
            v4 = lambda b: b.t[:].rearrange("p s h -> p (s h)")
            kb.op(A, lambda e: e.activation(out=v4(E_t), in_=pc.t[:, 0:256], func=AF.Exp), reads=[pc], writes=[E_t])
            kb.op(A, lambda e: e.activation(out=v4(Dt_t), in_=ptot.t[:, 0:256], func=AF.Exp), reads=[ptot], writes=[Dt_t])
            kb.op(V, lambda e: e.tensor_copy(out=v4(dtx_t), in_=pc.t[:, 0:256]), reads=[pc], writes=[dtx_t])
            kb.op(V, lambda e: e.tensor_tensor(out=v4(dtx_t), in0=ptot.t[:, 0:256], in1=v4(dtx_t), op=ALU.subtract),
                  reads=[ptot, dtx_t], writes=[dtx_t])
            kb.op(A, lambda e: e.activation(out=v4(dtx_t), in_=v4(dtx_t), func=AF.Exp), reads=[dtx_t], writes=[dtx_t])
            kb.op(V, lambda e: e.tensor_tensor(out=v4(dtx_t), in0=v4(dtx_t), in1=v4(dt_t), op=ALU.mult), reads=[dtx_t, dt_t], writes=[dtx_t])

            cctr = [0]

            def conv_silu(ps, ch, dst):
                u = u_s[cctr[0] % 2]; acc = acc_s[cctr[0] % 2]; cctr[0] += 1
                kb.op(V, lambda e: e.tensor_copy(out=u.t[:, 0:3], in_=uhalo.t[:, ch, :]), reads=[uhalo], writes=[u])
                kb.op(A, lambda e: e.activation(out=u.t[:, 3:3 + TT], in_=ps.t[:], func=AF.Copy), reads=[ps], writes=[u])
                kb.op(V, lambda e: e.tensor_copy(out=uhalo.t[:, ch, :], in_=u.t[:, TT:TT + 3]), reads=[u], writes=[uhalo])
                kb.op(V, lambda e: e.tensor_scalar(out=acc.t[:], in0=u.t[:, 0:TT], scalar1=mcw.t[:, ch, 0:1], scalar2=None, op0=ALU.mult),
                      reads=[u, mcw], writes=[acc])
                for k in (1, 2, 3):
                    kb.op(V, lambda e: e.scalar_tensor_tensor(out=acc.t[:], in0=u.t[:, k:k + TT], scalar=mcw.t[:, ch, k:k + 1], in1=acc.t[:],
                                                              op0=ALU.mult, op1=ALU.add), reads=[u, mcw, acc], writes=[acc])
                kb.op(A, lambda e: e.activation(out=dst.t[:], in_=acc.t[:], func=AF.Silu, bias=mcb.t[:, ch:ch + 1], scale=1.0),
                      reads=[acc, mcb], writes=[dst])

            tctr = [0]

            def to_tokmajor(src, dst, col0):
                bank = 6 + (tctr[0] % 2); tctr[0] += 1
                ps = PS[bank]
                pv = psb(bank)
                for s in range(4):
                    kb.mm(ps, pv[:, s * 128:(s + 1) * 128], src.t[:, s * 128:(s + 1) * 128], ident_b.t[:], start=True, stop=True,
                          reads=[src, ident_b], sig=(s == 3), transpose=True)
                kb.op(A, lambda e: e.activation(out=dst.t[:, :, col0:col0 + 128], in_=pv[:, 0:512].rearrange("p (s c) -> p s c", s=4), func=AF.Copy),
                      reads=[ps], writes=[dst])

            for g in range(NG):
                nrmw = nrm_g[g % 2]
                kb.dma(G, nrmw.t[:], nrmw_d[:, g * 512:(g + 1) * 512], nrmw, writes=[nrmw])
                hs = slice(g * 8, (g + 1) * 8)
                for s in range(4):
                    srb = sr[s % 2]
                    kb.op(V, lambda e: e.tensor_tensor(out=srb.t[:], in0=tri_f.t[:].unsqueeze(1).to_broadcast([128, 8, 128]),
                                                       in1=da_t.t[:, s, hs].unsqueeze(2).to_broadcast([128, 8, 128]), op=ALU.mult),
                          reads=[tri_f, da_t], writes=[srb])
                    for hh in range(2):
                        pseg = PS[hh]
                        kb.mm(pseg, pseg.t[:], lhsT=mgt_f.t[:], rhs=srb.t[:, hh * 4:(hh + 1) * 4, :].rearrange("p q l -> p (q l)"), start=True, stop=True,
                              reads=[mgt_f, srb])
                        kb.op(A, lambda e: e.activation(out=Lg[s].t[:, hh * 512:(hh + 1) * 512], in_=pseg.t[:], func=AF.Exp), reads=[pseg], writes=[Lg[s]])
                def x_epi(bi, j, ps, g=g):
                    c = bi * 2 + j
                    st = sT[c % 2]
                    conv_silu(ps, g * 4 + c, st)
                    to_tokmajor(st, xg, c * 128)
                linT("w_in", D, [[(DI + g * 512, 256)], [(DI + g * 512 + 256, 256)]], lambda kc: hT_b.t[:, kc, :], [hT_b], x_epi)

                def bc_epi(bi, j, ps, g=g):
                    if j == 0:
                        conv_silu(ps, 32 + g, BT)
                        bank = 6 + (tctr[0] % 2); tctr[0] += 1
                        pst = PS[bank]; pv = psb(bank)
                        for s in range(4):
                            kb.mm(pst, pv[:, s * 128:(s + 1) * 128], BT.t[:, s * 128:(s + 1) * 128], ident_b.t[:], start=True, stop=True,
                                  reads=[BT, ident_b], sig=(s == 3), transpose=True)
                        kb.op(A, lambda e: e.activation(out=Btok.t[:], in_=pv[:, 0:512].rearrange("p (s c) -> p s c", s=4), func=AF.Copy),
                              reads=[pst], writes=[Btok])
                    else:
                        conv_silu(ps, 40 + g, CT)
                linT("w_in", D, [[(2 * DI + g * 128, 128), (2 * DI + 1024 + g * 128, 128)]], lambda kc: hT_b.t[:, kc, :], [hT_b], bc_epi)

                def z_epi(bi, j, ps, g=g):
                    c = bi * 2 + j
                    st = sT[c % 2]
                    kb.op(A, lambda e: e.activation(out=st.t[:], in_=ps.t[:], func=AF.Silu), reads=[ps], writes=[st])
                    to_tokmajor(st, zg, c * 128)
                linT("w_in", D, [[(g * 512, 256)], [(g * 512 + 256, 256)]], lambda kc: hT_b.t[:, kc, :], [hT_b], z_epi)

                kb.op(A, lambda e: e.activation(out=S_b.t[:], in_=S_f.t[:, g, :], func=AF.Copy), reads=[S_f], writes=[S_b])
                hs = slice(g * 8, (g + 1) * 8)
                v3 = lambda ap: ap.rearrange("p (h d) -> p h d", h=8)

                def ssd_early(s):
                    cs = slice(s * 128, (s + 1) * 128)
                    kb.op(G, lambda e: e.tensor_tensor(out=v3(xdt.t[:]), in0=v3(xg.t[:, s, :]),
                                                       in1=dt_t.t[:, s, hs].unsqueeze(2).to_broadcast([128, 8, 64]), op=ALU.mult),
                          reads=[xg, dt_t], writes=[xdt])
                    kb.op(G, lambda e: e.tensor_tensor(out=v3(xdte.t[:]), in0=v3(xg.t[:, s, :]),
                                                       in1=dtx_t.t[:, s, hs].unsqueeze(2).to_broadcast([128, 8, 64]), op=ALU.mult),
                          reads=[xg, dtx_t], writes=[xdte])
                    kb.op(G, lambda e: e.tensor_tensor(out=v3(y2.t[:]), in0=v3(xg.t[:, s, :]), in1=drow.t[:, hs].unsqueeze(2).to_broadcast([128, 8, 64]), op=ALU.mult),
                          reads=[xg, drow], writes=[y2])
                    poff = PS[4]
                    kb.mm(poff, poff.t[:], lhsT=CT.t[:, cs], rhs=S_b.t[:], start=True, stop=True, reads=[CT, S_b])
                    pstt = PS[5]
                    kb.mm(pstt, pstt.t[:], lhsT=Btok.t[:, s, :], rhs=xdte.t[:], start=True, stop=True, reads=[Btok, xdte])
                    pcb = PS[2]
                    kb.mm(pcb, pcb.t[:, 0:128], lhsT=BT.t[:, cs], rhs=CT.t[:, cs], start=True, stop=True, reads=[BT, CT])
                    kb.op(V, lambda e: e.tensor_tensor(out=CBm.t[:], in0=pcb.t[:, 0:128], in1=tri_f.t[:], op=ALU.mult),
                          reads=[pcb, tri_f], writes=[CBm])
                    kb.op(G, lambda e: e.tensor_tensor(out=v3(S_f.t[:, g, :]), in0=v3(S_f.t[:, g, :]), in1=Dt_t.t[:, s, hs].unsqueeze(2).to_broadcast([128, 8, 64]), op=ALU.mult),
                          reads=[S_f, Dt_t], writes=[S_f])
                    kb.op(V, lambda e: e.tensor_tensor(out=S_f.t[:, g, :], in0=S_f.t[:, g, :], in1=pstt.t[:], op=ALU.add), reads=[S_f, pstt], writes=[S_f])
                    kb.op(A, lambda e: e.activation(out=S_b.t[:], in_=S_f.t[:, g, :], func=AF.Copy), reads=[S_f], writes=[S_b])

                def ssd_late(s):
                    py = PS[3]; poff = PS[4]
                    for hh in range(2):
                        L = Lg[s]; M = MT[hh]
                        kb.op(V, lambda e: e.tensor_tensor(out=M.t[:].rearrange("p (q l) -> p q l", q=4), in0=L.t[:, hh * 512:(hh + 1) * 512].rearrange("p (q l) -> p q l", q=4),
                                                           in1=CBm.t[:].unsqueeze(1).to_broadcast([128, 4, 128]), op=ALU.mult),
                              reads=[L, CBm], writes=[M])
                        for q in range(4):
                            hl = hh * 4 + q
                            kb.mm(py, py.t[:, hl * 64:(hl + 1) * 64], lhsT=M.t[:, q * 128:(q + 1) * 128], rhs=xdt.t[:, hl * 64:(hl + 1) * 64],
                                  start=True, stop=True, reads=[M, xdt], sig=(q == 3))
                    kb.op(V, lambda e: e.tensor_tensor(out=v3(y1.t[:]), in0=v3(poff.t[:]), in1=E_t.t[:, s, hs].unsqueeze(2).to_broadcast([128, 8, 64]), op=ALU.mult),
                          reads=[poff, E_t], writes=[y1])
                    kb.op(V, lambda e: e.tensor_tensor(out=y1.t[:], in0=y1.t[:], in1=y2.t[:], op=ALU.add), reads=[y1, y2], writes=[y1])
                    kb.op(V, lambda e: e.tensor_tensor(out=y1.t[:], in0=y1.t[:], in1=py.t[:], op=ALU.add), reads=[y1, py], writes=[y1])
                    kb.op(V, lambda e: e.tensor_tensor(out=y1.t[:], in0=y1.t[:], in1=zg.t[:, s, :], op=ALU.mult), reads=[y1, zg], writes=[y1])
                    kb.op(V, lambda e: e.tensor_tensor(out=junk.t[:], in0=y1.t[:], in1=y1.t[:], op=ALU.mult), reads=[y1], writes=[junk])
                    kb.op(V, lambda e: e.reduce_sum(out=ss.t[:], in_=junk.t[:], axis=AX.X), reads=[junk], writes=[ss])
                    kb.op(V, lambda e: e.tensor_scalar(out=rs.t[:], in0=ss.t[:], scalar1=1.0 / 512, scalar2=EPS, op0=ALU.mult, op1=ALU.add),
                          reads=[ss], writes=[rs])
                    kb.op(A, lambda e: e.activation(out=rs.t[:], in_=rs.t[:], func=AF.Ln), reads=[rs], writes=[rs])
                    kb.op(A, lambda e: e.activation(out=rs.t[:], in_=rs.t[:], func=AF.Exp, scale=-0.5), reads=[rs], writes=[rs])
                    kb.op(V, lambda e: e.scalar_tensor_tensor(out=yn.t[:], in0=y1.t[:], scalar=rs.t[:, 0:1], in1=nrmw.t[:],
                                                              op0=ALU.mult, op1=ALU.mult), reads=[y1, rs, nrmw], writes=[yn])

                def ssd_trans(s):
                    cs = slice(s * 128, (s + 1) * 128)
                    bank = 6 + (tctr[0] % 2); tctr[0] += 1
                    pst = PS[bank]; pv = psb(bank)
                    for q in range(4):
                        kb.mm(pst, pv[:, q * 128:(q + 1) * 128], yn.t[:, q * 128:(q + 1) * 128], ident_b.t[:], start=True, stop=True,
                              reads=[yn, ident_b], sig=(q == 3), transpose=True)
                    kb.op(A, lambda e: e.activation(out=bigT.t[:, g * 4:(g + 1) * 4, cs], in_=pv[:, 0:512].rearrange("p (q t) -> p q t", q=4), func=AF.Copy),
                          reads=[pst], writes=[bigT])

                for s in range(4):
                    ssd_early(s)
                    if s > 0:
                        ssd_trans(s - 1)
                    ssd_late(s)
                ssd_trans(3)
            kb.barrier()

        linT("w_out", DI, blocks_dm, lambda kc: bigT.t[:, kc, :], [bigT], resid_epi)
        layer_norm(0)
        if STOP == "hmid0":
            break
        kb.barrier()
        with ExitStack() as pes:
            ffn(0, pes)
            layer_norm(1)
            kb.barrier()
        if STOP == "h1":
            break

        with ExitStack() as pes:
            kst = [kb.sb("kv_k%d" % i, [128, TT], BF16, pes) for i in range(2)]
            vT = [kb.sb("kv_vT%d" % i, [128, TT], BF16, pes) for i in range(2)]
            vst = [kb.sb("kv_v%d" % i, [128, 4, 129], BF16, pes) for i in range(2)]
            kms = kb.sb("kv_kms", [128, 2], F32, pes)
            for b in vst:
                kb.op(G, lambda e: e.memset(b.t[:], 1.0), writes=[b])

            def kv_epi(bi, j, ps):
                ch = bi * 2 + j
                if ch < 16:
                    h = ch
                    k = kst[h % 2]
                    kb.op(A, lambda e: e.activation(out=k.t[:], in_=ps.t[:], func=AF.Copy), reads=[ps], writes=[k])
                    kb.dma(G, KT_d[h, :, tok0:tok0 + TT], k.t[:], k, reads=[k])
                    kb.op(V, lambda e: e.tensor_reduce(out=kms.t[:], in_=ps.t[:].rearrange("p (b t) -> p b t", b=2), axis=AX.X, op=ALU.add),
                          reads=[ps], writes=[kms])
                    kb.op(V, lambda e: e.tensor_scalar(out=kmT.t[:, h, 2 * it:2 * it + 2], in0=kms.t[:], scalar1=1.0 / 256, scalar2=None, op0=ALU.mult),
                          reads=[kms], writes=[kmT])
                else:
                    h = ch - 16
                    vt = vT[h % 2]; vs = vst[h % 2]
                    kb.op(A, lambda e: e.activation(out=vt.t[:], in_=ps.t[:], func=AF.Copy), reads=[ps], writes=[vt])
                    bank = 6 + (h % 2)
                    pst = PS[bank]; pv = psb(bank)
                    for s in range(4):
                        kb.mm(pst, pv[:, s * 128:(s + 1) * 128], vt.t[:, s * 128:(s + 1) * 128], ident_b.t[:], start=True, stop=True,
                              reads=[vt, ident_b], sig=(s == 3), transpose=True)
                    kb.op(V, lambda e: e.tensor_copy(out=vs.t[:, :, 0:128], in_=pv[:, 0:512].rearrange("p (s c) -> p s c", s=4)), reads=[pst], writes=[vs])
                    kb.dma(G, V_d[h, 4 * it:4 * it + 4, :, :].rearrange("s p c -> p s c"), vs.t[:], vs, reads=[vs])
            linT("w_kv", D, [[(c0, 256)] for c0 in range(0, 2 * D, 256)], lambda kc: hT_b.t[:, kc, :], [hT_b], kv_epi)
            kb.barrier()

        with ExitStack() as pes:
            nk = (it + 1) * TT
            nkt = nk // 128
            qT = kb.sb("a_qT", [128, HEADS, TT], BF16, pes)
            kbuf = [kb.sb("a_k%d" % i, [128, S], BF16, pes) for i in range(1)]
            vbuf = [kb.sb("a_v%d" % i, [128, S // 128, 129], BF16, pes) for i in range(1)]
            gm = kb.sb("a_gm", [128, 4, 16], F32, pes); mx8 = kb.sb("a_mx8", [128, 8], F32, pes)
            mb = kb.sb("a_mb", [128, 4, 16], BF16, pes); mbf = kb.sb("a_mbf", [128, 4, 16], F32, pes)
            negm = kb.sb("a_negm", [16, TT], BF16, pes)
            PT = [kb.sb("a_PT%d" % i, [128, TT], BF16, pes) for i in range(3)]
            dtmp = kb.sb("a_dtmp", [128, 128], F32, pes)
            num = kb.sb("a_num", [128, 129], F32, pes); rec = kb.sb("a_rec", [128, 1], F32, pes)
            otok = kb.sb("a_otok", [128, 4, 128], BF16, pes)
            dmat = kb.sb("a_dmat", [128, HEADS, 128], F32, pes)
            kb.dma(G, dmat.t[:], dmat_d.rearrange("p (h q) -> p h q", h=HEADS), dmat, writes=[dmat])

            def q_epi(bi, j, ps):
                h = bi * 2 + j
                kb.op(A, lambda e: e.activation(out=qT.t[:, h, :], in_=ps.t[:], func=AF.Copy), reads=[ps], writes=[qT])
            linT("w_q", D, blocks_dm, lambda kc: hT_b.t[:, kc, :], [hT_b], q_epi)

            pctr = [0]
            for h in range(HEADS):
                kbf = kbuf[0]; vbf = vbuf[0]
                kb.dma(G, kbf.t[:, 0:nk], KT_d[h, :, 0:nk], kbf, writes=[kbf])
                kb.dma(G, vbf.t[:, 0:nkt, :], V_d[h, 0:nkt, :, :].rearrange("s p c -> p s c"), vbf, writes=[vbf])
                pg = PS[7]
                for s in range(4):
                    kb.mm(pg, pg.t[:, s * 16:(s + 1) * 16], lhsT=qT.t[:, h, s * 128:(s + 1) * 128], rhs=kmT.t[:, h, :], start=True, stop=True,
                          reads=[qT, kmT], sig=(s == 3))
                kb.op(V, lambda e: e.tensor_tensor(out=gm.t[:].rearrange("p s n -> p (s n)"), in0=pg.t[:, 0:64], in1=vbias.t[:, it, :], op=ALU.add),
                      reads=[pg, vbias], writes=[gm])
                for s in range(4):
                    kb.op(V, lambda e: e.max(out=mx8.t[:], in_=gm.t[:, s, :]), reads=[gm], writes=[mx8])
                    kb.op(V, lambda e: e.tensor_scalar(out=mbf.t[:, s, :], in0=gm.t[:, s, :], scalar1=mx8.t[:, 2:3], scalar2=-NEG, op0=ALU.is_ge, op1=ALU.mult),
                          reads=[gm, mx8], writes=[mbf])
                    kb.op(V, lambda e: e.scalar_tensor_tensor(out=mb.t[:, s, :], in0=mbf.t[:, s, :], scalar=NEG, in1=vbias.t[:, it, s * 16:(s + 1) * 16],
                                                              op0=ALU.add, op1=ALU.add), reads=[mbf, vbias], writes=[mb])
                pm = PS[7]; pmv = psb(7)
                for s in range(4):
                    kb.mm(pm, pmv[0:16, s * 128:(s + 1) * 128], mb.t[:, s, :], ident_b.t[:], start=True, stop=True, reads=[mb, ident_b], sig=(s == 3), transpose=True)
                kb.op(V, lambda e: e.tensor_copy(out=negm.t[:], in_=pmv[0:16, 0:512]), reads=[pm], writes=[negm])

                poffs = [PS[2], PS[3], PS[4], PS[5]]
                pdiag = [PS[6], PS[6], PS[6], PS[6]]
                ocol = [0, 0, 0, 0]
                n_off = [4 * it + s for s in range(4)]
                done_off = [0, 0, 0, 0]

                def pv_mm(s, kt, P, diag):
                    if diag:
                        ps = pdiag[s]
                        kb.mm(ps, ps.t[:, ocol[s]:ocol[s] + 129], lhsT=P, rhs=vbf.t[:, kt, :], start=True, stop=True, reads=[vbf] + [PTcur[0]])
                    else:
                        ps = poffs[s]
                        first = done_off[s] == 0
                        done_off[s] += 1
                        last = done_off[s] == n_off[s]
                        kb.mm(ps, ps.t[:, ocol[s]:ocol[s] + 129], lhsT=P, rhs=vbf.t[:, kt, :], start=first, stop=last, reads=[vbf] + [PTcur[0]], sig=True)

                PTcur = [None]
                for kt in range(4 * it):
                    n = kt // 2
                    pss = PS[pctr[0] % 2]; P = PT[pctr[0] % 3]; pctr[0] += 1
                    PTcur[0] = P
                    kb.mm(pss, pss.t[:], lhsT=kbf.t[:, kt * 128:(kt + 1) * 128], rhs=qT.t[:, h, :], start=True, stop=False, reads=[kbf, qT], sig=False)
                    kb.mm(pss, pss.t[:], lhsT=esel.t[:, n, :], rhs=negm.t[:], start=False, stop=True, reads=[esel, negm], sig=True)
                    for s in range(4):
                        jj = 4 * it + s - kt
                        kb.op(A, lambda e: e.activation(out=P.t[:, s * 128:(s + 1) * 128], in_=pss.t[:, s * 128:(s + 1) * 128], func=AF.Exp,
                                                        bias=albias.t[:, h, jj:jj + 1], scale=SCALE), reads=[pss, albias], writes=[P])
                    for s in range(4):
                        pv_mm(s, kt, P.t[:, s * 128:(s + 1) * 128], False)
                for s in range(4):
                    own = 2 * it + s // 2
                    for sp in range(s + 1):
                        kt = 4 * it + sp
                        n = kt // 2
                        pss = PS[pctr[0] % 2]; P = PT[pctr[0] % 3]; pctr[0] += 1
                        PTcur[0] = P
                        qs = slice(s * 128, (s + 1) * 128)
                        need_mask = (n < own)
                        kb.mm(pss, pss.t[:, 0:128], lhsT=kbf.t[:, kt * 128:(kt + 1) * 128], rhs=qT.t[:, h, qs], start=True, stop=not need_mask,
                              reads=[kbf, qT], sig=not need_mask)
                        if need_mask:
                            kb.mm(pss, pss.t[:, 0:128], lhsT=esel.t[:, n, :], rhs=negm.t[:, qs], start=False, stop=True, reads=[esel, negm], sig=True)
                        if sp < s:
                            jj = s - sp
                            kb.op(A, lambda e: e.activation(out=P.t[:, 0:128], in_=pss.t[:, 0:128], func=AF.Exp, bias=albias.t[:, h, jj:jj + 1], scale=SCALE),
                                  reads=[pss, albias], writes=[P])
                            pv_mm(s, kt, P.t[:, 0:128], False)
                        else:
                            kb.op(V, lambda e: e.scalar_tensor_tensor(out=dtmp.t[:], in0=pss.t[:, 0:128], scalar=SCALE, in1=dmat.t[:, h, :],
                                                                      op0=ALU.mult, op1=ALU.add), reads=[pss, dmat], writes=[dtmp])
                            kb.op(A, lambda e: e.activation(out=P.t[:, 0:128], in_=dtmp.t[:], func=AF.Exp), reads=[dtmp], writes=[P])
                            pv_mm(s, kt, P.t[:, 0:128], True)
                    pd = pdiag[s]; po = poffs[s]; oc = ocol[s]
                    kb.op(V, lambda e: e.tensor_copy(out=num.t[:], in_=pd.t[:, oc:oc + 129]), reads=[pd], writes=[num])
                    if n_off[s] > 0:
                        kb.op(V, lambda e: e.scalar_tensor_tensor(out=num.t[:], in0=po.t[:, oc:oc + 129], scalar=fq.t[:, h:h + 1], in1=num.t[:],
                                                                  op0=ALU.mult, op1=ALU.add), reads=[po, num, fq], writes=[num])
                    kb.op(V, lambda e: e.reciprocal(out=rec.t[:], in_=num.t[:, 128:129]), reads=[num], writes=[rec])
                    kb.op(V, lambda e: e.tensor_scalar(out=otok.t[:, s, :], in0=num.t[:, 0:128], scalar1=rec.t[:, 0:1], scalar2=None, op0=ALU.mult),
                          reads=[num, rec], writes=[otok])
                pt = PS[7]; ptv = psb(7)
                for s in range(4):
                    kb.mm(pt, ptv[:, s * 128:(s + 1) * 128], otok.t[:, s, :], ident_b.t[:], start=True, stop=True, reads=[otok, ident_b], sig=(s == 3), transpose=True)
                kb.op(A, lambda e: e.activation(out=bigT.t[:, h, :], in_=ptv[:, 0:512], func=AF.Copy), reads=[pt], writes=[bigT])
            kb.barrier()
        linT("w_o", D, blocks_dm, lambda kc: bigT.t[:, kc, :], [bigT], resid_epi)
        layer_norm(2)
        if STOP == "hmid1":
            break
        kb.barrier()
        with ExitStack() as pes:
            ffn(1, pes)
            layer_norm(3)
            kb.barrier()
        with ExitStack() as pes:
            ost = [kb.sb("o_st%d" % i, [128, D], F32, pes) for i in range(2)]
            for s in range(4):
                o = ost[s % 2]
                for b4 in range(4):
                    ps = PS[4 + (b4 % 2)]
                    for j in range(4):
                        kc = b4 * 4 + j
                        kb.mm(ps, ps.t[:, j * 128:(j + 1) * 128], hT_f.t[:, kc, s * 128:(s + 1) * 128], ident_f.t[:], start=True, stop=True,
                              reads=[hT_f, ident_f], sig=(j == 3), transpose=True)
                    kb.op(A if b4 % 2 else V, (lambda e: e.activation(out=o.t[:, b4 * 512:(b4 + 1) * 512], in_=ps.t[:], func=AF.Copy)) if b4 % 2 else
                          (lambda e: e.tensor_copy(out=o.t[:, b4 * 512:(b4 + 1) * 512], in_=ps.t[:])), reads=[ps], writes=[o])
                kb.dma(G, out_d[tok0 + s * 128: tok0 + (s + 1) * 128, :], o.t[:], o, reads=[o])
            kb.barrier()

    if STOP is not None:
        with ExitStack() as pes:
            ost = [kb.sb("dbg_st%d" % i, [128, D], F32, pes) for i in range(2)]
            for s in range(4):
                o = ost[s % 2]
                for b4 in range(4):
                    ps = PS[4 + (b4 % 2)]
                    for j in range(4):
                        kc = b4 * 4 + j
                        kb.mm(ps, ps.t[:, j * 128:(j + 1) * 128], hT_f.t[:, kc, s * 128:(s + 1) * 128], ident_f.t[:], start=True, stop=True,
                              reads=[hT_f, ident_f], sig=(j == 3), transpose=True)
                    kb.op(V, lambda e: e.tensor_copy(out=o.t[:, b4 * 512:(b4 + 1) * 512], in_=ps.t[:]), reads=[ps], writes=[o])
                kb.dma(G, out_d[tok0 + s * 128: tok0 + (s + 1) * 128, :], o.t[:], o, reads=[o])
    kb.finish()
    es.close()
    return nc


def host_consts(inp):
    f = np.float32
    c = {}

    def percol(v, nch):
        return np.ascontiguousarray(v.reshape(nch, 128).T).astype(f)
    mcw = inp["mamba_conv_w"][0]
    c["mcw"] = np.ascontiguousarray(mcw.reshape(4, 48, 128).transpose(2, 1, 0)).reshape(128, 48 * 4).astype(f)
    c["mcb"] = percol(inp["mamba_conv_b"][0], 48)
    fw = inp["ffn_conv_w"]
    c["fcw"] = np.ascontiguousarray(fw.reshape(2, 3, 88, 128).transpose(3, 0, 2, 1)).reshape(128, 2 * 88 * 3).astype(f)
    c["fcb"] = np.ascontiguousarray(inp["ffn_conv_b"].reshape(2, 88, 128).transpose(2, 0, 1)).reshape(128, 2 * 88).astype(f)
    g = np.stack([inp["ln_mix_g"][0], inp["ln_ffn_g"][0], inp["ln_mix_g"][1], inp["ln_ffn_g"][1]])
    b = np.stack([inp["ln_mix_b"][0], inp["ln_ffn_b"][0], inp["ln_mix_b"][1], inp["ln_ffn_b"][1]])
    c["lng"] = np.ascontiguousarray(g.reshape(4, 16, 128).transpose(2, 0, 1)).reshape(128, 64).astype(f)
    c["lnb"] = np.ascontiguousarray(b.reshape(4, 16, 128).transpose(2, 0, 1)).reshape(128, 64).astype(f)
    c["dtb"] = np.ascontiguousarray(inp["mamba_dt_bias"][0].reshape(64, 1)).astype(f)
    return c


def kernel(**inputs):
    return _run(inputs)


_CACHE = {}


def _run(inputs, NT=8, STOP=None, NCORES=8, TRACE=False):
    inp = {k: np.asarray(v) for k, v in inputs.items()}
    f = np.float32
    c = host_consts(inp)
    c["drow"] = np.ascontiguousarray(np.broadcast_to(inp["mamba_d"][0][None, :], (128, 64))).astype(f)
    c["nrmw"] = np.ascontiguousarray(np.broadcast_to(inp["mamba_norm_w"][0][None, :], (128, DI))).astype(f)
    alog = np.ascontiguousarray(np.broadcast_to(inp["mamba_a_log"][0][None, :], (128, 64))).astype(f)
    slopes = (2.0 ** (-8.0 * np.arange(1, HEADS + 1, dtype=np.float64) / HEADS))
    kk = np.arange(128, dtype=np.float64)
    alb = np.zeros((128, HEADS, 36), f)
    for j in range(1, 36):
        alb[:, :, j] = (-(slopes[None, :]) * (128.0 * j - kk[:, None])).astype(f)
    dm = np.zeros((128, HEADS, 128), f)
    qq = np.arange(128, dtype=np.float64)
    dist = qq[None, :] - kk[:, None]
    for h in range(HEADS):
        dm[:, h, :] = np.where(dist >= 0, -slopes[h] * dist, -1.0e4).astype(f)
    fqt = np.exp(-slopes[None, :] * qq[:, None]).astype(f)
    vb = np.zeros((8, 4, 16), f)
    for it in range(8):
        for s in range(4):
            own = 2 * it + s // 2
            vb[it, s, :] = np.where(np.arange(16) < own, 0.0, NEG)
    vbias = np.ascontiguousarray(np.broadcast_to(vb.reshape(1, 8 * 64), (128, 512))).astype(f)
    es_ = np.zeros((16, 16, 128), f)
    for n in range(16):
        es_[n, n, :] = 1.0
    key = (NT, STOP)
    if key not in _CACHE:
        _CACHE[key] = build(NT, STOP)
    nc = _CACHE[key]
    common = dict(
        w_in=inp["mamba_w_in"][0], w_out=inp["mamba_w_out"][0], up0=inp["ffn_w_up"][0], dn0=inp["ffn_w_down"][0],
        w_kv=inp["w_kv"], w_q=inp["attn_w_q"][0], w_o=inp["attn_w_o"][0], up1=inp["ffn_w_up"][1], dn1=inp["ffn_w_down"][1],
        mcw=c["mcw"], mcb=c["mcb"], fcw=c["fcw"], fcb=c["fcb"], lng=c["lng"], lnb=c["lnb"], dtb=c["dtb"],
        arow=alog, drow=c["drow"], nrmw=c["nrmw"], albias=alb.reshape(128, -1), dmat=dm.reshape(128, -1), fq=fqt,
        vbias=vbias, esel=es_.reshape(16, -1))
    common = {k: np.ascontiguousarray(v, dtype=f) for k, v in common.items()}
    in_maps = []
    for core in range(NCORES):
        m = dict(common)
        m["x"] = np.ascontiguousarray(inp["x"][core // 2], dtype=f)
        in_maps.append(m)
    res = run_bass_kernel_spmd(nc, in_maps, core_ids=list(range(NCORES)), **({"trace": True} if TRACE else {}))
    if NCORES < 8:
        _CACHE["last_res"] = res
        return res.results[0]["out"]
    out = np.stack([res.results[2 * b]["out"] for b in range(4)], axis=0)
    return out.astype(np.float32)
```

```python
import numpy as np
from contextlib import ExitStack
import concourse.bass as bass
import concourse.mybir as mybir
from concourse.bass_utils import run_bass_kernel_spmd

F32, BF16 = mybir.dt.float32, mybir.dt.bfloat16
AF = mybir.ActivationFunctionType
ALU = mybir.AluOpType
AX = mybir.AxisListType

D = 2048
S = 4096
TT = 512
KC = D // 128
DI = 4096
NH = 64
NG = 8
DFF = 5632
FC = DFF // 128
INP = 10304
HEADS = 16
ALPHA = 4.0 ** 0.25
EPS = 1e-5
SCALE = 128 ** -0.5
NEG = -30000.0


class Buf:
    __slots__ = ("name", "t", "w", "r", "ds", "ps")

    def __init__(self, name, t=None, ps=False):
        self.name, self.t, self.w, self.r, self.ds, self.ps = name, t, [], [], None, ps


def _key(ev):
    return (ev[0], ev[1])


class KB:
    def __init__(self, nc, es):
        self.nc, self.es = nc, es
        self.E = dict(pe=nc.tensor, act=nc.scalar, dve=nc.vector, pool=nc.gpsimd, sp=nc.sync)
        self.esem = {k: es.enter_context(nc.semaphore("e_" + k)) for k in self.E}
        self.cnt = {k: 0 for k in self.E}
        self.seen = {k: {} for k in self.E}
        self.dsems, self.dcnt = [], []
        self.skip_ds = set()
        self.ds_by_name = {}
        self.uid = 0

    def sb(self, name, shape, dt, es=None):
        self.uid += 1
        tname = name if es is None else "%s_u%d" % (name, self.uid)
        t = (es or self.es).enter_context(self.nc.sbuf_tensor(tname, shape, dt))
        return Buf(name, t)

    def _wait(self, eng, evs):
        for ev in evs:
            if ev[0] == 'e':
                if eng == 'pe' and ev[1] == 'pe':
                    continue
                key, val, sem = ev[1], ev[2], self.esem[ev[1]]
            else:
                key, val, sem = ('d', ev[1]), self.dcnt[ev[1]], self.dsems[ev[1]]
            if self.seen[eng].get(key, 0) < val:
                self.E[eng].wait_ge(sem, val)
                self.seen[eng][key] = val

    @staticmethod
    def _merge(lst, ev):
        k = _key(ev)
        return [e for e in lst if _key(e) != k] + [ev]

    def _deps(self, reads, writes):
        evs = []
        for b in reads:
            evs += b.w
            if b.ps:
                evs += b.r
        for b in writes:
            evs += b.w
            evs += b.r
        return evs

    def _commit(self, ev, reads, writes):
        for b in reads:
            b.r = self._merge(b.r, ev)
        for b in writes:
            b.w = self._merge(b.w, ev)
            b.r = []

    def op(self, eng, fn, reads=(), writes=()):
        self._wait(eng, self._deps(reads, writes))
        ins = fn(self.E[eng])
        self.cnt[eng] += 1
        ins.then_inc(self.esem[eng], 1)
        ev = ('e', eng, self.cnt[eng])
        self._commit(ev, reads, writes)
        return ev

    def mm(self, ps, out, lhsT, rhs, start, stop, reads=(), sig=None, transpose=False):
        if sig is None:
            sig = stop
        evs = self._deps(reads, ())
        if start:
            evs += ps.w + ps.r
        self._wait('pe', evs)
        if transpose:
            ins = self.nc.tensor.transpose(out, lhsT, rhs)
        else:
            ins = self.nc.tensor.matmul(out, lhsT=lhsT, rhs=rhs, start=start, stop=stop)
        ev = ('e', 'pe', self.cnt['pe'] + 1)
        if sig:
            self.cnt['pe'] += 1
            ins.then_inc(self.esem['pe'], 1)
        for b in reads:
            b.r = self._merge(b.r, ev)
        if stop:
            ps.w = self._merge(ps.w, ev)
            ps.r = []
        return ev

    def dma(self, q, out, in_, sem_buf, reads=(), writes=()):
        self._wait(q, self._deps(reads, writes))
        if sem_buf.ds is None:
            if sem_buf.name in self.ds_by_name:
                sem_buf.ds = self.ds_by_name[sem_buf.name]
            else:
                sem_buf.ds = len(self.dsems)
                self.ds_by_name[sem_buf.name] = sem_buf.ds
                self.dsems.append(self.es.enter_context(self.nc.semaphore("d_%d" % sem_buf.ds)))
                self.dcnt.append(0)
        i = sem_buf.ds
        ins = self.E[q].dma_start(out=out, in_=in_)
        self.dcnt[i] += 16
        ins.then_inc(self.dsems[i], 16)
        ev = ('d', i, self.dcnt[i])
        self._commit(ev, reads, writes)
        return ev

    def barrier(self, engs=('pe', 'act', 'dve', 'pool')):
        evs = [('e', k, self.cnt[k]) for k in self.E if self.cnt[k] > 0]
        for nm, i in self.ds_by_name.items():
            if nm.startswith("wb"):
                self.skip_ds.add(i)
        evs += [('d', i, 0) for i in range(len(self.dsems)) if i not in self.skip_ds]
        for e in engs:
            self._wait(e, evs)

    def finish(self):
        evs = [('e', k, self.cnt[k]) for k in self.E if self.cnt[k] > 0]
        evs += [('d', i, 0) for i in range(len(self.dsems))]
        self._wait('sp', evs)


def build(NT=8, STOP=None):
    nc = bass.Bass("TRN2", target_bir_lowering=False)
    es = ExitStack()
    kb = KB(nc, es)

    def din(name, shape, dt=F32):
        return nc.dram_tensor(name, list(shape), dt, kind="ExternalInput").ap()

    def dscr(name, shape, dt):
        return nc.dram_tensor(name, list(shape), dt, kind="Internal").ap()

    x_d = din("x", [S, D])
    W = {}
    wshapes = dict(w_in=[D, INP], w_out=[DI, D], up0=[D, 2 * DFF], dn0=[DFF, D], w_kv=[D, 2 * D],
                   w_q=[D, D], w_o=[D, D], up1=[D, 2 * DFF], dn1=[DFF, D])
    for n, shp in wshapes.items():
        W[n] = (din(n, shp), dscr(n + "_b", shp, BF16), Buf("wd_" + n))
    mcw_d = din("mcw", [128, 48 * 4]); mcb_d = din("mcb", [128, 48])
    fcw_d = din("fcw", [128, 2 * 88 * 3]); fcb_d = din("fcb", [128, 2 * 88])
    lng_d = din("lng", [128, 4 * 16]); lnb_d = din("lnb", [128, 4 * 16])
    dtb_d = din("dtb", [64, 1]); arow_d = din("arow", [128, 64]); drow_d = din("drow", [128, 64])
    nrmw_d = din("nrmw", [128, DI])
    albias_d = din("albias", [128, HEADS * 36]); dmat_d = din("dmat", [128, HEADS * 128]); fq_d = din("fq", [128, HEADS]); fqa_d = din("fqa", [128, 4 * HEADS])
    vbias_d = din("vbias", [128, 8 * 64]); esel_d = din("esel", [16, 16 * 128])
    out_d = nc.dram_tensor("out", [S, D], F32, kind="ExternalOutput").ap()
    KT_d = dscr("KT", [HEADS, 128, S], BF16)
    V_d = dscr("V", [HEADS, S // 128, 128, 129], BF16)

    ident_f = kb.sb("ident_f", [128, 128], F32); ident_b = kb.sb("ident_b", [128, 128], BF16)
    tri_f = kb.sb("tri_f", [128, 128], F32); mgt_f = kb.sb("mgt_f", [128, 128], F32); ones_f = kb.sb("ones_f", [128, 128], F32)
    hT_f = kb.sb("hT_f", [128, KC, TT], F32); hT_b = kb.sb("hT_b", [128, KC, TT], BF16)
    NWB = 3
    wb = [kb.sb("wb%d" % i, [128, 16, 256], BF16) for i in range(NWB)]
    S_f = kb.sb("S_f", [128, NG, 512], F32)
    uhalo = kb.sb("uhalo", [128, 48, 3], F32); fhalo = kb.sb("fhalo", [128, 2, 88, 2], F32)
    mcw = kb.sb("mcw_s", [128, 48, 4], F32); mcb = kb.sb("mcb_s", [128, 48], F32)
    fcw = kb.sb("fcw_s", [128, 2, 88, 3], F32); fcb = kb.sb("fcb_s", [128, 2, 88], F32)
    lng = kb.sb("lng_s", [128, 4, 16], F32); lnb = kb.sb("lnb_s", [128, 4, 16], F32)
    dtb = kb.sb("dtb_s", [64, 1], F32); arow = kb.sb("arow_s", [128, 64], F32); drow = kb.sb("drow_s", [128, 64], F32)
    albias = kb.sb("albias_s", [128, HEADS, 36], F32)
    fqa = kb.sb("fqa_s", [128, 4, HEADS], F32)
    fq = kb.sb("fq_s", [128, HEADS], F32); vbias = kb.sb("vbias_s", [128, 8, 64], F32)
    esel = kb.sb("esel_s", [16, 16, 128], BF16)
    kmT = kb.sb("kmT", [128, HEADS, 16], BF16)
    bigT = kb.sb("bigT", [128, 32, TT], BF16)
    mean_s = kb.sb("mean_s", [128, TT], F32); rstd_s = kb.sb("rstd_s", [128, TT], F32)
    sq_s = [kb.sb("sq0", [128, TT], F32)] * 2
    lnt = [kb.sb("lnt%d" % i, [128, TT], F32) for i in range(2)]
    PS = []
    for i in range(8):
        t = es.enter_context(nc.psum_tensor("ps%d" % i, [128, 512], F32))
        PS.append(Buf("ps%d" % i, t, ps=True))

    def psb(i):
        return PS[i].t[:].bitcast(BF16)

    V, A, G = 'dve', 'act', 'pool'

    kb.op(G, lambda e: e.memset(ident_f.t[:], 1.0), writes=[ident_f])
    kb.op(G, lambda e: e.affine_select(out=ident_f.t[:], in_=ident_f.t[:], pattern=[[-1, 128]], compare_op=ALU.is_equal,
                                      fill=0.0, base=0, channel_multiplier=1), reads=[ident_f], writes=[ident_f])
    kb.op(V, lambda e: e.tensor_copy(out=ident_b.t[:], in_=ident_f.t[:]), reads=[ident_f], writes=[ident_b])
    kb.op(G, lambda e: e.memset(tri_f.t[:], 1.0), writes=[tri_f])
    kb.op(G, lambda e: e.affine_select(out=tri_f.t[:], in_=tri_f.t[:], pattern=[[1, 128]], compare_op=ALU.is_ge,
                                      fill=0.0, base=0, channel_multiplier=-1), reads=[tri_f], writes=[tri_f])
    kb.op(G, lambda e: e.memset(mgt_f.t[:], 1.0), writes=[mgt_f])
    kb.op(G, lambda e: e.affine_select(out=mgt_f.t[:], in_=mgt_f.t[:], pattern=[[-1, 128]], compare_op=ALU.is_gt,
                                      fill=0.0, base=0, channel_multiplier=1), reads=[mgt_f], writes=[mgt_f])
    kb.op(G, lambda e: e.memset(ones_f.t[:], 1.0), writes=[ones_f])
    kb.op(G, lambda e: e.memset(S_f.t[:], 0.0), writes=[S_f])
    kb.op(G, lambda e: e.memset(uhalo.t[:], 0.0), writes=[uhalo])
    kb.op(G, lambda e: e.memset(fhalo.t[:], 0.0), writes=[fhalo])

    if STOP == "c0":
        kb.finish(); es.close(); return nc

    def ld(buf, src, view=None):
        kb.dma(G, buf.t[:] if view is None else view, src, buf, writes=[buf])

    ld(mcw, mcw_d.rearrange("p (c k) -> p c k", k=4)); ld(mcb, mcb_d)
    ld(fcw, fcw_d.rearrange("p (l c k) -> p l c k", l=2, k=3)); ld(fcb, fcb_d.rearrange("p (l c) -> p l c", l=2))
    ld(lng, lng_d.rearrange("p (l c) -> p l c", l=4)); ld(lnb, lnb_d.rearrange("p (l c) -> p l c", l=4))
    ld(dtb, dtb_d); ld(arow, arow_d); ld(drow, drow_d)
    kb.op(A, lambda e: e.activation(out=arow.t[:], in_=arow.t[:], func=AF.Exp), reads=[arow], writes=[arow])
    kb.op(V, lambda e: e.tensor_scalar(out=arow.t[:], in0=arow.t[:], scalar1=-1.0, scalar2=None, op0=ALU.mult), reads=[arow], writes=[arow])
    kb.op(G, lambda e: e.memset(kmT.t[:], 0.0), writes=[kmT])
    ld(albias, albias_d.rearrange("p (h j) -> p h j", h=HEADS))
    ld(fqa, fqa_d.rearrange("p (s h) -> p s h", s=4)); ld(fq, fq_d); ld(vbias, vbias_d.rearrange("p (i c) -> p i c", i=8))
    ld(esel, esel_d.rearrange("p (n k) -> p n k", n=16))

    if STOP == "c1":
        kb.finish(); es.close(); return nc
    def convert(names):
        for n in names:
            src, dst, wbuf = W[n]
            K = src.shape[0]
            rb = 256
            for r0 in range(0, K, rb):
                kb.dma(G, dst[r0:r0 + rb, :], src[r0:r0 + rb, :], wbuf, writes=[wbuf])
            kb.skip_ds.add(wbuf.ds)
    if STOP not in ("x0", "x1", "x2"):
        convert(["w_in"])

    wctr = [0]

    def linT(wname, K, blocks, rhs_fn, rhs_bufs, epi, ps_banks=(0, 1, 2, 3)):
        _, wdst, wbuf = W[wname]
        nkb = (K + 2047) // 2048
        pending = []
        for bi, segs in enumerate(blocks):
            ncols = sum(s[1] for s in segs)
            nj = ncols // 128 if ncols >= 128 else 1
            banks = [PS[ps_banks[(2 * (bi % 2) + j) % len(ps_banks)]] for j in range(nj)]
            for kbi in range(nkb):
                k0 = kbi * 2048
                kcs = min(16, (K - k0) // 128)
                wt = wb[wctr[0] % NWB]
                wctr[0] += 1
                c = 0
                for (c0, n) in segs:
                    kb.dma('sp', wt.t[:, 0:kcs, c:c + n],
                           wdst[k0:k0 + kcs * 128, c0:c0 + n].rearrange("(kc p) n -> p kc n", p=128),
                           wt, reads=[wbuf], writes=[wt])
                    c += n
                for j in range(nj):
                    m = min(128, ncols)
                    for kc in range(kcs):
                        first = (kbi == 0 and kc == 0)
                        last = (kbi == nkb - 1 and kc == kcs - 1)
                        kb.mm(banks[j], banks[j].t[0:m, :], lhsT=wt.t[:, kc, j * 128:j * 128 + m], rhs=rhs_fn(k0 // 128 + kc),
                              start=first, stop=last, reads=[wt] + list(rhs_bufs), sig=(kc == kcs - 1))
            for f in pending:
                f()
            pending = [(lambda bi=bi, j=j, b=banks[j]: epi(bi, j, b)) for j in range(nj)]
        for f in pending:
            f()

    def layer_norm(li):
        ps_sum, ps_sq = PS[6], PS[7]
        for j in range(KC):
            sq = sq_s[j % 2]
            kb.op(A, lambda e: e.activation(out=sq.t[:], in_=hT_f.t[:, j, :], func=AF.Square), reads=[hT_f], writes=[sq])
            kb.mm(ps_sum, ps_sum.t[:], lhsT=ones_f.t[:], rhs=hT_f.t[:, j, :], start=(j == 0), stop=(j == KC - 1), reads=[ones_f, hT_f])
            kb.mm(ps_sq, ps_sq.t[:], lhsT=ones_f.t[:], rhs=sq.t[:], start=(j == 0), stop=(j == KC - 1), reads=[ones_f, sq], sig=True)
        kb.op(V, lambda e: e.tensor_scalar(out=mean_s.t[:], in0=ps_sum.t[:], scalar1=1.0 / D, scalar2=None, op0=ALU.mult),
              reads=[ps_sum], writes=[mean_s])
        t0 = lnt[0]
        kb.op(V, lambda e: e.tensor_tensor(out=t0.t[:], in0=mean_s.t[:], in1=mean_s.t[:], op=ALU.mult), reads=[mean_s], writes=[t0])
        kb.op(V, lambda e: e.scalar_tensor_tensor(out=rstd_s.t[:], in0=ps_sq.t[:], scalar=1.0 / D, in1=t0.t[:],
                                                  op0=ALU.mult, op1=ALU.subtract), reads=[ps_sq, t0], writes=[rstd_s])
        kb.op(V, lambda e: e.tensor_scalar(out=rstd_s.t[:], in0=rstd_s.t[:], scalar1=EPS, scalar2=None, op0=ALU.add),
              reads=[rstd_s], writes=[rstd_s])
        kb.op(A, lambda e: e.activation(out=rstd_s.t[:], in_=rstd_s.t[:], func=AF.Sqrt), reads=[rstd_s], writes=[rstd_s])
        kb.op(V, lambda e: e.reciprocal(out=rstd_s.t[:], in_=rstd_s.t[:]), reads=[rstd_s], writes=[rstd_s])
        for j in range(KC):
            t = lnt[j % 2]
            kb.op(V, lambda e: e.tensor_tensor(out=t.t[:], in0=hT_f.t[:, j, :], in1=mean_s.t[:], op=ALU.subtract),
                  reads=[hT_f, mean_s], writes=[t])
            kb.op(V, lambda e: e.tensor_tensor(out=t.t[:], in0=t.t[:], in1=rstd_s.t[:], op=ALU.mult), reads=[t, rstd_s], writes=[t])
            kb.op(V, lambda e: e.tensor_scalar(out=hT_f.t[:, j, :], in0=t.t[:], scalar1=lng.t[:, li, j:j + 1], scalar2=lnb.t[:, li, j:j + 1],
                                               op0=ALU.mult, op1=ALU.add), reads=[t, lng, lnb], writes=[hT_f])
            kb.op(A, lambda e: e.activation(out=hT_b.t[:, j, :], in_=hT_f.t[:, j, :], func=AF.Copy), reads=[hT_f], writes=[hT_b])

    def resid_epi(bi, j, ps):
        ch = bi * 2 + j
        kb.op(V, lambda e: e.scalar_tensor_tensor(out=hT_f.t[:, ch, :], in0=hT_f.t[:, ch, :], scalar=ALPHA, in1=ps.t[:],
                                                  op0=ALU.mult, op1=ALU.add), reads=[hT_f, ps], writes=[hT_f])

    blocks_dm = [[(c0, 256)] for c0 in range(0, D, 256)]

    def ffn(layer, pes):
        act = kb.sb("ffn_act", [128, FC, TT], BF16, pes)
        fu = [kb.sb("ffn_u%d" % i, [128, 2 + TT], F32, pes) for i in range(2)]
        facc = [kb.sb("ffn_acc%d" % i, [128, TT], F32, pes) for i in range(2)]
        gact = [kb.sb("ffn_g%d" % i, [128, TT], BF16, pes) for i in range(2)]
        wn = "up%d" % layer
        blocks = []
        for c in range(0, FC, 2):
            blocks.append([(c * 128, 256)])
            blocks.append([(DFF + c * 128, 256)])
        ctr = [0]

        def epi(bi, j, ps):
            isval = bi % 2
            c = (bi // 2) * 2 + j
            ch = isval * FC + c
            u = fu[ctr[0] % 2]; acc = facc[ctr[0] % 2]; ctr[0] += 1
            kb.op(V, lambda e: e.tensor_copy(out=u.t[:, 0:2], in_=fhalo.t[:, layer, ch, :]), reads=[fhalo], writes=[u])
            kb.op(A, lambda e: e.activation(out=u.t[:, 2:2 + TT], in_=ps.t[:], func=AF.Copy), reads=[ps], writes=[u])
            kb.op(V, lambda e: e.tensor_copy(out=fhalo.t[:, layer, ch, :], in_=u.t[:, TT:TT + 2]), reads=[u], writes=[fhalo])
            kb.op(V, lambda e: e.tensor_scalar(out=acc.t[:], in0=u.t[:, 0:TT], scalar1=fcw.t[:, layer, ch, 0:1], scalar2=None, op0=ALU.mult),
                  reads=[u, fcw], writes=[acc])
            for k in (1, 2):
                kb.op(V, lambda e: e.scalar_tensor_tensor(out=acc.t[:], in0=u.t[:, k:k + TT], scalar=fcw.t[:, layer, ch, k:k + 1], in1=acc.t[:],
                                                          op0=ALU.mult, op1=ALU.add), reads=[u, fcw, acc], writes=[acc])
            if not isval:
                g = gact[j]
                kb.op(A, lambda e: e.activation(out=g.t[:], in_=acc.t[:], func=AF.Silu, bias=fcb.t[:, layer, ch:ch + 1], scale=1.0),
                      reads=[acc, fcb], writes=[g])
            else:
                g = gact[j]
                kb.op(V, lambda e: e.scalar_tensor_tensor(out=act.t[:, c, :], in0=acc.t[:], scalar=fcb.t[:, layer, ch:ch + 1], in1=g.t[:],
                                                          op0=ALU.add, op1=ALU.mult), reads=[acc, fcb, g], writes=[act])

        linT(wn, D, blocks, lambda kc: hT_b.t[:, kc, :], [hT_b], epi)
        linT("dn%d" % layer, DFF, blocks_dm, lambda kc: act.t[:, kc, :], [act], resid_epi)

    for it in range(NT):
        tok0 = it * TT
        with ExitStack() as pes:
            xin = [kb.sb("xin%d" % i, [128, D], F32, pes) for i in range(2)]
            for s in range(4):
                xt = xin[s % 2]
                kb.dma(G, xt.t[:], x_d[tok0 + s * 128: tok0 + (s + 1) * 128, :], xt, writes=[xt])
                for b4 in range(4 if STOP != "x2" else 0):
                    ps = PS[4 + (b4 % 2)]
                    for j in range(4):
                        kc = b4 * 4 + j
                        kb.mm(ps, ps.t[:, j * 128:(j + 1) * 128], xt.t[:, kc * 128:(kc + 1) * 128], ident_f.t[:], start=True, stop=True,
                              reads=[xt, ident_f], sig=(j == 3), transpose=True)
                    src = ps.t[:].rearrange("p (j t) -> p j t", j=4)
                    kb.op(A, lambda e: e.activation(out=hT_f.t[:, b4 * 4:(b4 + 1) * 4, s * 128:(s + 1) * 128], in_=src, func=AF.Copy),
                          reads=[ps], writes=[hT_f])
                    kb.op(V, lambda e: e.tensor_copy(out=hT_b.t[:, b4 * 4:(b4 + 1) * 4, s * 128:(s + 1) * 128], in_=src),
                          reads=[ps], writes=[hT_b])
            kb.barrier()
        if STOP in ("x1", "x2"):
            kb.finish(); es.close(); return nc
        if STOP in ("x", "x0"):
            break
        if it == 0:
            convert(["w_out", "up0", "dn0", "w_kv", "w_q", "w_o", "up1", "dn1"])

        with ExitStack() as pes:
            u_s = [kb.sb("m_u%d" % i, [128, 3 + TT], F32, pes) for i in range(2)]
            acc_s = [kb.sb("m_acc%d" % i, [128, TT], F32, pes) for i in range(2)]
            sT = [kb.sb("m_sT%d" % i, [128, TT], BF16, pes) for i in range(2)]
            xg = kb.sb("m_xg", [128, 4, 512], BF16, pes); zg = kb.sb("m_zg", [128, 4, 512], BF16, pes)
            Btok = kb.sb("m_Btok", [128, 4, 128], BF16, pes)
            BT = kb.sb("m_BT", [128, TT], BF16, pes); CT = kb.sb("m_CT", [128, TT], BF16, pes)
            dtT = kb.sb("m_dtT", [64, TT], F32, pes)
            dt_t = kb.sb("m_dt", [128, 4, 64], F32, pes); da_t = kb.sb("m_da", [128, 4, 64], F32, pes)
            E_t = kb.sb("m_E", [128, 4, 64], F32, pes); dtx_t = kb.sb("m_dtx", [128, 4, 64], F32, pes)
            Dt_t = kb.sb("m_Dt", [128, 4, 64], F32, pes)
            S_b = kb.sb("m_Sb", [128, 512], BF16, pes)
            sr = [kb.sb("m_sr%d" % i, [128, 8, 128], F32, pes) for i in range(2)]
            Lg = [kb.sb("m_Lg%d" % i, [128, 1024], BF16, pes) for i in range(4)]
            CBm = kb.sb("m_CBm", [128, 128], F32, pes)
            MT = [kb.sb("m_MT%d" % i, [128, 512], BF16, pes) for i in range(2)]
            xdt = kb.sb("m_xdt", [128, 512], BF16, pes); xdte = kb.sb("m_xdte", [128, 512], BF16, pes)
            y1 = kb.sb("m_y1", [128, 512], F32, pes); y2 = kb.sb("m_y2", [128, 512], F32, pes)
            junk = kb.sb("m_junk", [128, 512], F32, pes)
            ss = kb.sb("m_ss", [128, 1], F32, pes); rs = kb.sb("m_rs", [128, 1], F32, pes)
            yn = kb.sb("m_yn", [128, 512], BF16, pes)
            nrm_g = [kb.sb("m_nrm0", [128, 512], F32, pes)] * 2

            def dt_epi(bi, j, ps):
                kb.op(A, lambda e: e.activation(out=dtT.t[:], in_=ps.t[0:64, :], func=AF.Exp, bias=dtb.t[:, 0:1], scale=1.0),
                      reads=[ps, dtb], writes=[dtT])
                kb.op(A, lambda e: e.activation(out=dtT.t[:], in_=dtT.t[:], func=AF.Ln, bias=1.0, scale=1.0), reads=[dtT], writes=[dtT])
            linT("w_in", D, [[(10240, 64)]], lambda kc: hT_b.t[:, kc, :], [hT_b], dt_epi)
            psd = PS[4]
            for s in range(4):
                kb.mm(psd, psd.t[:, s * 64:(s + 1) * 64], dtT.t[:, s * 128:(s + 1) * 128], ident_f.t[0:64, 0:64], start=True, stop=True,
                      reads=[dtT, ident_f], sig=(s == 3), transpose=True)
            kb.op(V, lambda e: e.tensor_copy(out=dt_t.t[:], in_=psd.t[:, 0:256].rearrange("p (s h) -> p s h", s=4)), reads=[psd], writes=[dt_t])
            kb.op(V, lambda e: e.tensor_tensor(out=da_t.t[:], in0=dt_t.t[:], in1=arow.t[:].unsqueeze(1).to_broadcast([128, 4, 64]), op=ALU.mult),
                  reads=[dt_t, arow], writes=[da_t])
            pc, ptot = PS[5], PS[6]
            for s in range(4):
                kb.mm(pc, pc.t[:, s * 64:(s + 1) * 64], lhsT=tri_f.t[:], rhs=da_t.t[:, s, :], start=True, stop=True, reads=[tri_f, da_t], sig=(s == 3))
            for s in range(4):
                kb.mm(ptot, ptot.t[:, s * 64:(s + 1) * 64], lhsT=ones_f.t[:], rhs=da_t.t[:, s, :], start=True, stop=True, reads=[ones_f, da_t], sig=(s == 3))
            v4 = lambda b: b.t[:].rearrange("p s h -> p (s h)")
            kb.op(A, lambda e: e.activation(out=v4(E_t), in_=pc.t[:, 0:256], func=AF.Exp), reads=[pc], writes=[E_t])
            kb.op(A, lambda e: e.activation(out=v4(Dt_t), in_=ptot.t[:, 0:256], func=AF.Exp), reads=[ptot], writes=[Dt_t])
            kb.op(V, lambda e: e.tensor_copy(out=v4(dtx_t), in_=pc.t[:, 0:256]), reads=[pc], writes=[dtx_t])
            kb.op(V, lambda e: e.tensor_tensor(out=v4(dtx_t), in0=ptot.t[:, 0:256], in1=v4(dtx_t), op=ALU.subtract),
                  reads=[ptot, dtx_t], writes=[dtx_t])
            kb.op(A, lambda e: e.activation(out=v4(dtx_t), in_=v4(dtx_t), func=AF.Exp), reads=[dtx_t], writes=[dtx_t])
            kb.op(V, lambda e: e.tensor_tensor(out=v4(dtx_t), in0=v4(dtx_t), in1=v4(dt_t), op=ALU.mult), reads=[dtx_t, dt_t], writes=[dtx_t])

            cctr = [0]

            def conv_silu(ps, ch, dst):
                u = u_s[cctr[0] % 2]; acc = acc_s[cctr[0] % 2]; cctr[0] += 1
                kb.op(V, lambda e: e.tensor_copy(out=u.t[:, 0:3], in_=uhalo.t[:, ch, :]), reads=[uhalo], writes=[u])
                kb.op(A, lambda e: e.activation(out=u.t[:, 3:3 + TT], in_=ps.t[:], func=AF.Copy), reads=[ps], writes=[u])
                kb.op(V, lambda e: e.tensor_copy(out=uhalo.t[:, ch, :], in_=u.t[:, TT:TT + 3]), reads=[u], writes=[uhalo])
                kb.op(V, lambda e: e.tensor_scalar(out=acc.t[:], in0=u.t[:, 0:TT], scalar1=mcw.t[:, ch, 0:1], scalar2=None, op0=ALU.mult),
                      reads=[u, mcw], writes=[acc])
                for k in (1, 2, 3):
                    kb.op(V, lambda e: e.scalar_tensor_tensor(out=acc.t[:], in0=u.t[:, k:k + TT], scalar=mcw.t[:, ch, k:k + 1], in1=acc.t[:],
                                                              op0=ALU.mult, op1=ALU.add), reads=[u, mcw, acc], writes=[acc])
                kb.op(A, lambda e: e.activation(out=dst.t[:], in_=acc.t[:], func=AF.Silu, bias=mcb.t[:, ch:ch + 1], scale=1.0),
                      reads=[acc, mcb], writes=[dst])

            tctr = [0]

            def to_tokmajor(src, dst, col0):
                bank = 6 + (tctr[0] % 2); tctr[0] += 1
                ps = PS[bank]
                pv = psb(bank)
                for s in range(4):
                    kb.mm(ps, pv[:, s * 128:(s + 1) * 128], src.t[:, s * 128:(s + 1) * 128], ident_b.t[:], start=True, stop=True,
                          reads=[src, ident_b], sig=(s == 3), transpose=True)
                kb.op(A, lambda e: e.activation(out=dst.t[:, :, col0:col0 + 128], in_=pv[:, 0:512].rearrange("p (s c) -> p s c", s=4), func=AF.Copy),
                      reads=[ps], writes=[dst])

            for g in range(NG):
                nrmw = nrm_g[g % 2]
                kb.dma(G, nrmw.t[:], nrmw_d[:, g * 512:(g + 1) * 512], nrmw, writes=[nrmw])
                hs = slice(g * 8, (g + 1) * 8)
                for s in range(4):
                    srb = sr[s % 2]
                    kb.op(V, lambda e: e.tensor_tensor(out=srb.t[:], in0=tri_f.t[:].unsqueeze(1).to_broadcast([128, 8, 128]),
                                                       in1=da_t.t[:, s, hs].unsqueeze(2).to_broadcast([128, 8, 128]), op=ALU.mult),
                          reads=[tri_f, da_t], writes=[srb])
                    for hh in range(2):
                        pseg = PS[hh]
                        kb.mm(pseg, pseg.t[:], lhsT=mgt_f.t[:], rhs=srb.t[:, hh * 4:(hh + 1) * 4, :].rearrange("p q l -> p (q l)"), start=True, stop=True,
                              reads=[mgt_f, srb])
                        kb.op(A, lambda e: e.activation(out=Lg[s].t[:, hh * 512:(hh + 1) * 512], in_=pseg.t[:], func=AF.Exp), reads=[pseg], writes=[Lg[s]])
                def x_epi(bi, j, ps, g=g):
                    c = bi * 2 + j
                    st = sT[c % 2]
                    conv_silu(ps, g * 4 + c, st)
                    to_tokmajor(st, xg, c * 128)
                linT("w_in", D, [[(DI + g * 512, 256)], [(DI + g * 512 + 256, 256)]], lambda kc: hT_b.t[:, kc, :], [hT_b], x_epi)

                def bc_epi(bi, j, ps, g=g):
                    if j == 0:
                        conv_silu(ps, 32 + g, BT)
                        bank = 6 + (tctr[0] % 2); tctr[0] += 1
                        pst = PS[bank]; pv = psb(bank)
                        for s in range(4):
                            kb.mm(pst, pv[:, s * 128:(s + 1) * 128], BT.t[:, s * 128:(s + 1) * 128], ident_b.t[:], start=True, stop=True,
                                  reads=[BT, ident_b], sig=(s == 3), transpose=True)
                        kb.op(A, lambda e: e.activation(out=Btok.t[:], in_=pv[:, 0:512].rearrange("p (s c) -> p s c", s=4), func=AF.Copy),
                              reads=[pst], writes=[Btok])
                    else:
                        conv_silu(ps, 40 + g, CT)
                linT("w_in", D, [[(2 * DI + g * 128, 128), (2 * DI + 1024 + g * 128, 128)]], lambda kc: hT_b.t[:, kc, :], [hT_b], bc_epi)

                def z_epi(bi, j, ps, g=g):
                    c = bi * 2 + j
                    st = sT[c % 2]
                    kb.op(A, lambda e: e.activation(out=st.t[:], in_=ps.t[:], func=AF.Silu), reads=[ps], writes=[st])
                    to_tokmajor(st, zg, c * 128)
                linT("w_in", D, [[(g * 512, 256)], [(g * 512 + 256, 256)]], lambda kc: hT_b.t[:, kc, :], [hT_b], z_epi)

                kb.op(A, lambda e: e.activation(out=S_b.t[:], in_=S_f.t[:, g, :], func=AF.Copy), reads=[S_f], writes=[S_b])
                hs = slice(g * 8, (g + 1) * 8)
                v3 = lambda ap: ap.rearrange("p (h d) -> p h d", h=8)

                def ssd_early(s):
                    cs = slice(s * 128, (s + 1) * 128)
                    kb.op(G, lambda e: e.tensor_tensor(out=v3(xdt.t[:]), in0=v3(xg.t[:, s, :]),
                                                       in1=dt_t.t[:, s, hs].unsqueeze(2).to_broadcast([128, 8, 64]), op=ALU.mult),
                          reads=[xg, dt_t], writes=[xdt])
                    kb.op(G, lambda e: e.tensor_tensor(out=v3(xdte.t[:]), in0=v3(xg.t[:, s, :]),
                                                       in1=dtx_t.t[:, s, hs].unsqueeze(2).to_broadcast([128, 8, 64]), op=ALU.mult),
                          reads=[xg, dtx_t], writes=[xdte])
                    kb.op(G, lambda e: e.tensor_tensor(out=v3(y2.t[:]), in0=v3(xg.t[:, s, :]), in1=drow.t[:, hs].unsqueeze(2).to_broadcast([128, 8, 64]), op=ALU.mult),
                          reads=[xg, drow], writes=[y2])
                    poff = PS[4]
                    kb.mm(poff, poff.t[:], lhsT=CT.t[:, cs], rhs=S_b.t[:], start=True, stop=True, reads=[CT, S_b])
                    pstt = PS[5]
                    kb.mm(pstt, pstt.t[:], lhsT=Btok.t[:, s, :], rhs=xdte.t[:], start=True, stop=True, reads=[Btok, xdte])
                    pcb = PS[2]
                    kb.mm(pcb, pcb.t[:, 0:128], lhsT=BT.t[:, cs], rhs=CT.t[:, cs], start=True, stop=True, reads=[BT, CT])
                    kb.op(V, lambda e: e.tensor_tensor(out=CBm.t[:], in0=pcb.t[:, 0:128], in1=tri_f.t[:], op=ALU.mult),
                          reads=[pcb, tri_f], writes=[CBm])
                    kb.op(G, lambda e: e.tensor_tensor(out=v3(S_f.t[:, g, :]), in0=v3(S_f.t[:, g, :]), in1=Dt_t.t[:, s, hs].unsqueeze(2).to_broadcast([128, 8, 64]), op=ALU.mult),
                          reads=[S_f, Dt_t], writes=[S_f])
                    kb.op(V, lambda e: e.tensor_tensor(out=S_f.t[:, g, :], in0=S_f.t[:, g, :], in1=pstt.t[:], op=ALU.add), reads=[S_f, pstt], writes=[S_f])
                    kb.op(A, lambda e: e.activation(out=S_b.t[:], in_=S_f.t[:, g, :], func=AF.Copy), reads=[S_f], writes=[S_b])

                def ssd_late(s):
                    py = PS[3]; poff = PS[4]
                    for hh in range(2):
                        L = Lg[s]; M = MT[hh]
                        kb.op(V, lambda e: e.tensor_tensor(out=M.t[:].rearrange("p (q l) -> p q l", q=4), in0=L.t[:, hh * 512:(hh + 1) * 512].rearrange("p (q l) -> p q l", q=4),
                                                           in1=CBm.t[:].unsqueeze(1).to_broadcast([128, 4, 128]), op=ALU.mult),
                              reads=[L, CBm], writes=[M])
                        for q in range(4):
                            hl = hh * 4 + q
                            kb.mm(py, py.t[:, hl * 64:(hl + 1) * 64], lhsT=M.t[:, q * 128:(q + 1) * 128], rhs=xdt.t[:, hl * 64:(hl + 1) * 64],
                                  start=True, stop=True, reads=[M, xdt], sig=(q == 3))
                    kb.op(V, lambda e: e.tensor_tensor(out=v3(y1.t[:]), in0=v3(poff.t[:]), in1=E_t.t[:, s, hs].unsqueeze(2).to_broadcast([128, 8, 64]), op=ALU.mult),
                          reads=[poff, E_t], writes=[y1])
                    kb.op(V, lambda e: e.tensor_tensor(out=y1.t[:], in0=y1.t[:], in1=y2.t[:], op=ALU.add), reads=[y1, y2], writes=[y1])
                    kb.op(V, lambda e: e.tensor_tensor(out=y1.t[:], in0=y1.t[:], in1=py.t[:], op=ALU.add), reads=[y1, py], writes=[y1])
                    kb.op(V, lambda e: e.tensor_tensor(out=y1.t[:], in0=y1.t[:], in1=zg.t[:, s, :], op=ALU.mult), reads=[y1, zg], writes=[y1])
                    kb.op(V, lambda e: e.tensor_tensor(out=junk.t[:], in0=y1.t[:], in1=y1.t[:], op=ALU.mult), reads=[y1], writes=[junk])
                    kb.op(V, lambda e: e.reduce_sum(out=ss.t[:], in_=junk.t[:], axis=AX.X), reads=[junk], writes=[ss])
                    kb.op(V, lambda e: e.tensor_scalar(out=rs.t[:], in0=ss.t[:], scalar1=1.0 / 512, scalar2=EPS, op0=ALU.mult, op1=ALU.add),
                          reads=[ss], writes=[rs])
                    kb.op(A, lambda e: e.activation(out=rs.t[:], in_=rs.t[:], func=AF.Ln), reads=[rs], writes=[rs])
                    kb.op(A, lambda e: e.activation(out=rs.t[:], in_=rs.t[:], func=AF.Exp, scale=-0.5), reads=[rs], writes=[rs])
                    kb.op(V, lambda e: e.scalar_tensor_tensor(out=yn.t[:], in0=y1.t[:], scalar=rs.t[:, 0:1], in1=nrmw.t[:],
                                                              op0=ALU.mult, op1=ALU.mult), reads=[y1, rs, nrmw], writes=[yn])

                def ssd_trans(s):
                    cs = slice(s * 128, (s + 1) * 128)
                    bank = 6 + (tctr[0] % 2); tctr[0] += 1
                    pst = PS[bank]; pv = psb(bank)
                    for q in range(4):
                        kb.mm(pst, pv[:, q * 128:(q + 1) * 128], yn.t[:, q * 128:(q + 1) * 128], ident_b.t[:], start=True, stop=True,
                              reads=[yn, ident_b], sig=(q == 3), transpose=True)
                    kb.op(A, lambda e: e.activation(out=bigT.t[:, g * 4:(g + 1) * 4, cs], in_=pv[:, 0:512].rearrange("p (q t) -> p q t", q=4), func=AF.Copy),
                          reads=[pst], writes=[bigT])

                for s in range(4):
                    ssd_early(s)
                    if s > 0:
                        ssd_trans(s - 1)
                    ssd_late(s)
                ssd_trans(3)
            kb.barrier()

        linT("w_out", DI, blocks_dm, lambda kc: bigT.t[:, kc, :], [bigT], resid_epi)
        layer_norm(0)
        if STOP == "hmid0":
            break
        kb.barrier()
        with ExitStack() as pes:
            ffn(0, pes)
            layer_norm(1)
            kb.barrier()
        if STOP == "h1":
            break

        with ExitStack() as pes:
            kst = [kb.sb("kv_k%d" % i, [128, TT], BF16, pes) for i in range(2)]
            vT = [kb.sb("kv_vT%d" % i, [128, TT], BF16, pes) for i in range(2)]
            vst = [kb.sb("kv_v%d" % i, [128, 4, 129], BF16, pes) for i in range(2)]
            kms = kb.sb("kv_kms", [128, 2], F32, pes)
            for b in vst:
                kb.op(G, lambda e: e.memset(b.t[:], 1.0), writes=[b])

            def kv_epi(bi, j, ps):
                ch = bi * 2 + j
                if ch < 16:
                    h = ch
                    k = kst[h % 2]
                    kb.op(A, lambda e: e.activation(out=k.t[:], in_=ps.t[:], func=AF.Copy), reads=[ps], writes=[k])
                    kb.dma(G, KT_d[h, :, tok0:tok0 + TT], k.t[:], k, reads=[k])
                    kb.op(V, lambda e: e.tensor_reduce(out=kms.t[:], in_=ps.t[:].rearrange("p (b t) -> p b t", b=2), axis=AX.X, op=ALU.add),
                          reads=[ps], writes=[kms])
                    kb.op(V, lambda e: e.tensor_scalar(out=kmT.t[:, h, 2 * it:2 * it + 2], in0=kms.t[:], scalar1=1.0 / 256, scalar2=None, op0=ALU.mult),
                          reads=[kms], writes=[kmT])
                else:
                    h = ch - 16
                    vt = vT[h % 2]; vs = vst[h % 2]
                    kb.op(A, lambda e: e.activation(out=vt.t[:], in_=ps.t[:], func=AF.Copy), reads=[ps], writes=[vt])
                    bank = 6 + (h % 2)
                    pst = PS[bank]; pv = psb(bank)
                    for s in range(4):
                        kb.mm(pst, pv[:, s * 128:(s + 1) * 128], vt.t[:, s * 128:(s + 1) * 128], ident_b.t[:], start=True, stop=True,
                              reads=[vt, ident_b], sig=(s == 3), transpose=True)
                    kb.op(V, lambda e: e.tensor_copy(out=vs.t[:, :, 0:128], in_=pv[:, 0:512].rearrange("p (s c) -> p s c", s=4)), reads=[pst], writes=[vs])
                    kb.dma(G, V_d[h, 4 * it:4 * it + 4, :, :].rearrange("s p c -> p s c"), vs.t[:], vs, reads=[vs])
            linT("w_kv", D, [[(c0, 256)] for c0 in range(0, 2 * D, 256)], lambda kc: hT_b.t[:, kc, :], [hT_b], kv_epi)
            kb.barrier()

        with ExitStack() as pes:
            nk = (it + 1) * TT
            nkt = nk // 128
            qT = kb.sb("a_qT", [128, HEADS, TT], BF16, pes)
            kbuf = [kb.sb("a_k%d" % i, [128, S], BF16, pes) for i in range(1)]
            vbuf = [kb.sb("a_v%d" % i, [128, S // 128, 129], BF16, pes) for i in range(1)]
            gm = kb.sb("a_gm", [128, 4, 16], F32, pes); mx8 = kb.sb("a_mx8", [128, 8], F32, pes)
            mb = kb.sb("a_mb", [128, 4, 16], BF16, pes); mbf = kb.sb("a_mbf", [128, 4, 16], F32, pes)
            negm = kb.sb("a_negm", [16, TT], BF16, pes)
            PT = [kb.sb("a_PT%d" % i, [128, TT], BF16, pes) for i in range(3)]
            dtmp = kb.sb("a_dtmp", [128, 128], F32, pes)
            num = kb.sb("a_num", [128, 129], F32, pes); rec = kb.sb("a_rec", [128, 1], F32, pes)
            otok = kb.sb("a_otok", [128, 4, 128], BF16, pes)
            dmat = kb.sb("a_dmat", [128, HEADS, 128], F32, pes)
            kb.dma(G, dmat.t[:], dmat_d.rearrange("p (h q) -> p h q", h=HEADS), dmat, writes=[dmat])

            def q_epi(bi, j, ps):
                h = bi * 2 + j
                kb.op(A, lambda e: e.activation(out=qT.t[:, h, :], in_=ps.t[:], func=AF.Copy), reads=[ps], writes=[qT])
            linT("w_q", D, blocks_dm, lambda kc: hT_b.t[:, kc, :], [hT_b], q_epi)

            pctr = [0]
            for h in range(HEADS):
                kbf = kbuf[0]; vbf = vbuf[0]
                kb.dma(G, kbf.t[:, 0:nk], KT_d[h, :, 0:nk], kbf, writes=[kbf])
                kb.dma(G, vbf.t[:, 0:nkt, :], V_d[h, 0:nkt, :, :].rearrange("s p c -> p s c"), vbf, writes=[vbf])
                pg = PS[7]
                for s in range(4):
                    kb.mm(pg, pg.t[:, s * 16:(s + 1) * 16], lhsT=qT.t[:, h, s * 128:(s + 1) * 128], rhs=kmT.t[:, h, :], start=True, stop=True,
                          reads=[qT, kmT], sig=(s == 3))
                kb.op(V, lambda e: e.tensor_tensor(out=gm.t[:].rearrange("p s n -> p (s n)"), in0=pg.t[:, 0:64], in1=vbias.t[:, it, :], op=ALU.add),
                      reads=[pg, vbias], writes=[gm])
                for s in range(4):
                    kb.op(V, lambda e: e.max(out=mx8.t[:], in_=gm.t[:, s, :]), reads=[gm], writes=[mx8])
                    kb.op(V, lambda e: e.tensor_scalar(out=mbf.t[:, s, :], in0=gm.t[:, s, :], scalar1=mx8.t[:, 2:3], scalar2=-NEG, op0=ALU.is_ge, op1=ALU.mult),
                          reads=[gm, mx8], writes=[mbf])
                    kb.op(V, lambda e: e.scalar_tensor_tensor(out=mb.t[:, s, :], in0=mbf.t[:, s, :], scalar=NEG, in1=vbias.t[:, it, s * 16:(s + 1) * 16],
                                                              op0=ALU.add, op1=ALU.add), reads=[mbf, vbias], writes=[mb])
                pm = PS[7]; pmv = psb(7)
                for s in range(4):
                    kb.mm(pm, pmv[0:16, s * 128:(s + 1) * 128], mb.t[:, s, :], ident_b.t[:], start=True, stop=True, reads=[mb, ident_b], sig=(s == 3), transpose=True)
                kb.op(V, lambda e: e.tensor_copy(out=negm.t[:], in_=pmv[0:16, 0:512]), reads=[pm], writes=[negm])

                poA = [PS[2], PS[2], PS[3], PS[3]]; colA = [0, 129, 0, 129]
                poB = [PS[4], PS[5], PS[4], PS[5]]; colB = [0, 0, 129, 129]
                pdiag = PS[6]
                nA = 4 * it
                nB = [0, 1, 2, 3]
                doneA = [0, 0, 0, 0]; doneB = [0, 0, 0, 0]

                def pv_mm(kind, s, kt, Pbuf, P):
                    if kind == 'D':
                        kb.mm(pdiag, pdiag.t[:, 0:129], lhsT=P, rhs=vbf.t[:, kt, :], start=True, stop=True, reads=[vbf, Pbuf])
                    elif kind == 'A':
                        ps = poA[s]; c = colA[s]
                        first = doneA[s] == 0
                        doneA[s] += 1
                        kb.mm(ps, ps.t[:, c:c + 129], lhsT=P, rhs=vbf.t[:, kt, :], start=first, stop=(doneA[s] == nA), reads=[vbf, Pbuf], sig=True)
                    else:
                        ps = poB[s]; c = colB[s]
                        first = doneB[s] == 0
                        doneB[s] += 1
                        kb.mm(ps, ps.t[:, c:c + 129], lhsT=P, rhs=vbf.t[:, kt, :], start=first, stop=(doneB[s] == nB[s]), reads=[vbf, Pbuf], sig=True)

                for kt in range(4 * it):
                    n = kt // 2
                    pss = PS[pctr[0] % 2]; P = PT[pctr[0] % 3]; pctr[0] += 1
                    kb.mm(pss, pss.t[:], lhsT=kbf.t[:, kt * 128:(kt + 1) * 128], rhs=qT.t[:, h, :], start=True, stop=False, reads=[kbf, qT], sig=False)
                    kb.mm(pss, pss.t[:], lhsT=esel.t[:, n, :], rhs=negm.t[:], start=False, stop=True, reads=[esel, negm], sig=True)
                    jj = 4 * it - kt
                    kb.op(A, lambda e: e.activation(out=P.t[:], in_=pss.t[:], func=AF.Exp, bias=albias.t[:, h, jj:jj + 1], scale=SCALE),
                          reads=[pss, albias], writes=[P])
                    for s in range(4):
                        pv_mm('A', s, kt, P, P.t[:, s * 128:(s + 1) * 128])
                for s in range(4):
                    own = 2 * it + s // 2
                    for sp in range(s + 1):
                        kt = 4 * it + sp
                        n = kt // 2
                        pss = PS[pctr[0] % 2]; P = PT[pctr[0] % 3]; pctr[0] += 1
                        qs = slice(s * 128, (s + 1) * 128)
                        need_mask = (n < own)
                        kb.mm(pss, pss.t[:, 0:128], lhsT=kbf.t[:, kt * 128:(kt + 1) * 128], rhs=qT.t[:, h, qs], start=True, stop=not need_mask,
                              reads=[kbf, qT], sig=not need_mask)
                        if need_mask:
                            kb.mm(pss, pss.t[:, 0:128], lhsT=esel.t[:, n, :], rhs=negm.t[:, qs], start=False, stop=True, reads=[esel, negm], sig=True)
                        if sp < s:
                            jj = s - sp
                            kb.op(A, lambda e: e.activation(out=P.t[:, 0:128], in_=pss.t[:, 0:128], func=AF.Exp, bias=albias.t[:, h, jj:jj + 1], scale=SCALE),
                                  reads=[pss, albias], writes=[P])
                            pv_mm('B', s, kt, P, P.t[:, 0:128])
                        else:
                            kb.op(V, lambda e: e.scalar_tensor_tensor(out=dtmp.t[:], in0=pss.t[:, 0:128], scalar=SCALE, in1=dmat.t[:, h, :],
                                                                      op0=ALU.mult, op1=ALU.add), reads=[pss, dmat], writes=[dtmp])
                            kb.op(A, lambda e: e.activation(out=P.t[:, 0:128], in_=dtmp.t[:], func=AF.Exp), reads=[dtmp], writes=[P])
                            pv_mm('D', s, kt, P, P.t[:, 0:128])
                    kb.op(V, lambda e: e.tensor_copy(out=num.t[:], in_=pdiag.t[:, 0:129]), reads=[pdiag], writes=[num])
                    if nB[s] > 0:
                        pb = poB[s]; c = colB[s]
                        kb.op(V, lambda e: e.scalar_tensor_tensor(out=num.t[:], in0=pb.t[:, c:c + 129], scalar=fq.t[:, h:h + 1], in1=num.t[:],
                                                                  op0=ALU.mult, op1=ALU.add), reads=[pb, num, fq], writes=[num])
                    if nA > 0:
                        pa = poA[s]; c = colA[s]
                        kb.op(V, lambda e: e.scalar_tensor_tensor(out=num.t[:], in0=pa.t[:, c:c + 129], scalar=fqa.t[:, s, h:h + 1], in1=num.t[:],
                                                                  op0=ALU.mult, op1=ALU.add), reads=[pa, num, fqa], writes=[num])
                    kb.op(V, lambda e: e.reciprocal(out=rec.t[:], in_=num.t[:, 128:129]), reads=[num], writes=[rec])
                    kb.op(V, lambda e: e.tensor_scalar(out=otok.t[:, s, :], in0=num.t[:, 0:128], scalar1=rec.t[:, 0:1], scalar2=None, op0=ALU.mult),
                          reads=[num, rec], writes=[otok])
                pt = PS[7]; ptv = psb(7)
                for s in range(4):
                    kb.mm(pt, ptv[:, s * 128:(s + 1) * 128], otok.t[:, s, :], ident_b.t[:], start=True, stop=True, reads=[otok, ident_b], sig=(s == 3), transpose=True)
                kb.op(A, lambda e: e.activation(out=bigT.t[:, h, :], in_=ptv[:, 0:512], func=AF.Copy), reads=[pt], writes=[bigT])
            kb.barrier()
        linT("w_o", D, blocks_dm, lambda kc: bigT.t[:, kc, :], [bigT], resid_epi)
        layer_norm(2)
        if STOP == "hmid1":
            break
        kb.barrier()
        with ExitStack() as pes:
            ffn(1, pes)
            layer_norm(3)
            kb.barrier()
        with ExitStack() as pes:
            ost = [kb.sb("o_st%d" % i, [128, D], F32, pes) for i in range(2)]
            for s in range(4):
                o = ost[s % 2]
                for b4 in range(4):
                    ps = PS[4 + (b4 % 2)]
                    for j in range(4):
                        kc = b4 * 4 + j
                        kb.mm(ps, ps.t[:, j * 128:(j + 1) * 128], hT_f.t[:, kc, s * 128:(s + 1) * 128], ident_f.t[:], start=True, stop=True,
                              reads=[hT_f, ident_f], sig=(j == 3), transpose=True)
                    kb.op(A if b4 % 2 else V, (lambda e: e.activation(out=o.t[:, b4 * 512:(b4 + 1) * 512], in_=ps.t[:], func=AF.Copy)) if b4 % 2 else
                          (lambda e: e.tensor_copy(out=o.t[:, b4 * 512:(b4 + 1) * 512], in_=ps.t[:])), reads=[ps], writes=[o])
                kb.dma(G, out_d[tok0 + s * 128: tok0 + (s + 1) * 128, :], o.t[:], o, reads=[o])
            kb.barrier()

    if STOP is not None:
        with ExitStack() as pes:
            ost = [kb.sb("dbg_st%d" % i, [128, D], F32, pes) for i in range(2)]
            for s in range(4):
                o = ost[s % 2]
                for b4 in range(4):
                    ps = PS[4 + (b4 % 2)]
                    for j in range(4):
                        kc = b4 * 4 + j
                        kb.mm(ps, ps.t[:, j * 128:(j + 1) * 128], hT_f.t[:, kc, s * 128:(s + 1) * 128], ident_f.t[:], start=True, stop=True,
                              reads=[hT_f, ident_f], sig=(j == 3), transpose=True)
                    kb.op(V, lambda e: e.tensor_copy(out=o.t[:, b4 * 512:(b4 + 1) * 512], in_=ps.t[:]), reads=[ps], writes=[o])
                kb.dma(G, out_d[tok0 + s * 128: tok0 + (s + 1) * 128, :], o.t[:], o, reads=[o])
    kb.finish()
    es.close()
    return nc


def host_consts(inp):
    f = np.float32
    c = {}

    def percol(v, nch):
        return np.ascontiguousarray(v.reshape(nch, 128).T).astype(f)
    mcw = inp["mamba_conv_w"][0]
    c["mcw"] = np.ascontiguousarray(mcw.reshape(4, 48, 128).transpose(2, 1, 0)).reshape(128, 48 * 4).astype(f)
    c["mcb"] = percol(inp["mamba_conv_b"][0], 48)
    fw = inp["ffn_conv_w"]
    c["fcw"] = np.ascontiguousarray(fw.reshape(2, 3, 88, 128).transpose(3, 0, 2, 1)).reshape(128, 2 * 88 * 3).astype(f)
    c["fcb"] = np.ascontiguousarray(inp["ffn_conv_b"].reshape(2, 88, 128).transpose(2, 0, 1)).reshape(128, 2 * 88).astype(f)
    g = np.stack([inp["ln_mix_g"][0], inp["ln_ffn_g"][0], inp["ln_mix_g"][1], inp["ln_ffn_g"][1]])
    b = np.stack([inp["ln_mix_b"][0], inp["ln_ffn_b"][0], inp["ln_mix_b"][1], inp["ln_ffn_b"][1]])
    c["lng"] = np.ascontiguousarray(g.reshape(4, 16, 128).transpose(2, 0, 1)).reshape(128, 64).astype(f)
    c["lnb"] = np.ascontiguousarray(b.reshape(4, 16, 128).transpose(2, 0, 1)).reshape(128, 64).astype(f)
    c["dtb"] = np.ascontiguousarray(inp["mamba_dt_bias"][0].reshape(64, 1)).astype(f)
    return c


def kernel(**inputs):
    return _run(inputs)


_CACHE = {}


def _run(inputs, NT=8, STOP=None, NCORES=8, TRACE=False):
    inp = {k: np.asarray(v) for k, v in inputs.items()}
    f = np.float32
    c = host_consts(inp)
    c["drow"] = np.ascontiguousarray(np.broadcast_to(inp["mamba_d"][0][None, :], (128, 64))).astype(f)
    c["nrmw"] = np.ascontiguousarray(np.broadcast_to(inp["mamba_norm_w"][0][None, :], (128, DI))).astype(f)
    alog = np.ascontiguousarray(np.broadcast_to(inp["mamba_a_log"][0][None, :], (128, 64))).astype(f)
    slopes = (2.0 ** (-8.0 * np.arange(1, HEADS + 1, dtype=np.float64) / HEADS))
    kk = np.arange(128, dtype=np.float64)
    alb = np.zeros((128, HEADS, 36), f)
    for j in range(1, 36):
        alb[:, :, j] = (-(slopes[None, :]) * (128.0 * j - kk[:, None])).astype(f)
    dm = np.zeros((128, HEADS, 128), f)
    qq = np.arange(128, dtype=np.float64)
    dist = qq[None, :] - kk[:, None]
    for h in range(HEADS):
        dm[:, h, :] = np.where(dist >= 0, -slopes[h] * dist, -1.0e4).astype(f)
    fqt = np.exp(-slopes[None, :] * qq[:, None]).astype(f)
    fqa_t = np.stack([np.exp(-slopes[None, :] * (128.0 * s_ + qq[:, None])) for s_ in range(4)], axis=1).astype(f)
    vb = np.zeros((8, 4, 16), f)
    for it in range(8):
        for s in range(4):
            own = 2 * it + s // 2
            vb[it, s, :] = np.where(np.arange(16) < own, 0.0, NEG)
    vbias = np.ascontiguousarray(np.broadcast_to(vb.reshape(1, 8 * 64), (128, 512))).astype(f)
    es_ = np.zeros((16, 16, 128), f)
    for n in range(16):
        es_[n, n, :] = 1.0
    key = (NT, STOP)
    if key not in _CACHE:
        _CACHE[key] = build(NT, STOP)
    nc = _CACHE[key]
    common = dict(
        w_in=inp["mamba_w_in"][0], w_out=inp["mamba_w_out"][0], up0=inp["ffn_w_up"][0], dn0=inp["ffn_w_down"][0],
        w_kv=inp["w_kv"], w_q=inp["attn_w_q"][0], w_o=inp["attn_w_o"][0], up1=inp["ffn_w_up"][1], dn1=inp["ffn_w_down"][1],
        mcw=c["mcw"], mcb=c["mcb"], fcw=c["fcw"], fcb=c["fcb"], lng=c["lng"], lnb=c["lnb"], dtb=c["dtb"],
        arow=alog, drow=c["drow"], nrmw=c["nrmw"], albias=alb.reshape(128, -1), dmat=dm.reshape(128, -1), fq=fqt, fqa=fqa_t.reshape(128, -1),
        vbias=vbias, esel=es_.reshape(16, -1))
    common = {k: np.ascontiguousarray(v, dtype=f) for k, v in common.items()}
    in_maps = []
    for core in range(NCORES):
        m = dict(common)
        m["x"] = np.ascontiguousarray(inp["x"][core // 2], dtype=f)
        in_maps.append(m)
    res = run_bass_kernel_spmd(nc, in_maps, core_ids=list(range(NCORES)), **({"trace": True} if TRACE else {}))
    if NCORES < 8:
        _CACHE["last_res"] = res
        return res.results[0]["out"]
    out = np.stack([res.results[2 * b]["out"] for b in range(4)], axis=0)
    return out.astype(np.float32)
```

```python
import numpy as np
from contextlib import ExitStack
import concourse.bass as bass
import concourse.mybir as mybir
from concourse.bass_utils import run_bass_kernel_spmd

F32, BF16 = mybir.dt.float32, mybir.dt.bfloat16
AF = mybir.ActivationFunctionType
ALU = mybir.AluOpType
AX = mybir.AxisListType

D = 2048
S = 4096
TT = 512
KC = D // 128
DI = 4096
NH = 64
NG = 8
DFF = 5632
FC = DFF // 128
INP = 10304
HEADS = 16
ALPHA = 4.0 ** 0.25
EPS = 1e-5
SCALE = 128 ** -0.5
NEG = -30000.0


class Buf:
    __slots__ = ("name", "t", "w", "r", "ds", "ps")

    def __init__(self, name, t=None, ps=False):
        self.name, self.t, self.w, self.r, self.ds, self.ps = name, t, [], [], None, ps


def _key(ev):
    return (ev[0], ev[1])


class KB:
    def __init__(self, nc, es):
        self.nc, self.es = nc, es
        self.E = dict(pe=nc.tensor, act=nc.scalar, dve=nc.vector, pool=nc.gpsimd, sp=nc.sync)
        self.esem = {k: es.enter_context(nc.semaphore("e_" + k)) for k in self.E}
        self.cnt = {k: 0 for k in self.E}
        self.seen = {k: {} for k in self.E}
        self.dsems, self.dcnt = [], []
        self.skip_ds = set()
        self.ds_by_name = {}
        self.uid = 0

    def sb(self, name, shape, dt, es=None):
        self.uid += 1
        tname = name if es is None else "%s_u%d" % (name, self.uid)
        t = (es or self.es).enter_context(self.nc.sbuf_tensor(tname, shape, dt))
        return Buf(name, t)

    def _wait(self, eng, evs):
        for ev in evs:
            if ev[0] == 'e':
                if eng == 'pe' and ev[1] == 'pe':
                    continue
                key, val, sem = ev[1], ev[2], self.esem[ev[1]]
            else:
                key, val, sem = ('d', ev[1]), self.dcnt[ev[1]], self.dsems[ev[1]]
            if self.seen[eng].get(key, 0) < val:
                self.E[eng].wait_ge(sem, val)
                self.seen[eng][key] = val

    @staticmethod
    def _merge(lst, ev):
        k = _key(ev)
        return [e for e in lst if _key(e) != k] + [ev]

    def _deps(self, reads, writes):
        evs = []
        for b in reads:
            evs += b.w
            if b.ps:
                evs += b.r
        for b in writes:
            evs += b.w
            evs += b.r
        return evs

    def _commit(self, ev, reads, writes):
        for b in reads:
            b.r = self._merge(b.r, ev)
        for b in writes:
            b.w = self._merge(b.w, ev)
            b.r = []

    def op(self, eng, fn, reads=(), writes=()):
        self._wait(eng, self._deps(reads, writes))
        ins = fn(self.E[eng])
        self.cnt[eng] += 1
        ins.then_inc(self.esem[eng], 1)
        ev = ('e', eng, self.cnt[eng])
        self._commit(ev, reads, writes)
        return ev

    def mm(self, ps, out, lhsT, rhs, start, stop, reads=(), sig=None, transpose=False):
        if sig is None:
            sig = stop
        evs = self._deps(reads, ())
        if start:
            evs += ps.w + ps.r
        self._wait('pe', evs)
        if transpose:
            ins = self.nc.tensor.transpose(out, lhsT, rhs)
        else:
            ins = self.nc.tensor.matmul(out, lhsT=lhsT, rhs=rhs, start=start, stop=stop)
        ev = ('e', 'pe', self.cnt['pe'] + 1)
        if sig:
            self.cnt['pe'] += 1
            ins.then_inc(self.esem['pe'], 1)
        for b in reads:
            b.r = self._merge(b.r, ev)
        if stop:
            ps.w = self._merge(ps.w, ev)
            ps.r = []
        return ev

    def dma(self, q, out, in_, sem_buf, reads=(), writes=()):
        self._wait(q, self._deps(reads, writes))
        if sem_buf.ds is None:
            if sem_buf.name in self.ds_by_name:
                sem_buf.ds = self.ds_by_name[sem_buf.name]
            else:
                sem_buf.ds = len(self.dsems)
                self.ds_by_name[sem_buf.name] = sem_buf.ds
                self.dsems.append(self.es.enter_context(self.nc.semaphore("d_%d" % sem_buf.ds)))
                self.dcnt.append(0)
        i = sem_buf.ds
        ins = self.E[q].dma_start(out=out, in_=in_)
        self.dcnt[i] += 16
        ins.then_inc(self.dsems[i], 16)
        ev = ('d', i, self.dcnt[i])
        self._commit(ev, reads, writes)
        return ev

    def barrier(self, engs=('pe', 'act', 'dve', 'pool')):
        evs = [('e', k, self.cnt[k]) for k in self.E if self.cnt[k] > 0]
        for nm, i in self.ds_by_name.items():
            if nm.startswith("wb"):
                self.skip_ds.add(i)
        evs += [('d', i, 0) for i in range(len(self.dsems)) if i not in self.skip_ds]
        for e in engs:
            self._wait(e, evs)

    def finish(self):
        evs = [('e', k, self.cnt[k]) for k in self.E if self.cnt[k] > 0]
        evs += [('d', i, 0) for i in range(len(self.dsems))]
        self._wait('sp', evs)


def build(NT=8, STOP=None):
    nc = bass.Bass("TRN2", target_bir_lowering=False)
    es = ExitStack()
    kb = KB(nc, es)

    def din(name, shape, dt=F32):
        return nc.dram_tensor(name, list(shape), dt, kind="ExternalInput").ap()

    def dscr(name, shape, dt):
        return nc.dram_tensor(name, list(shape), dt, kind="Internal").ap()

    x_d = din("x", [S, D])
    W = {}
    wshapes = dict(w_in=[D, INP], w_out=[DI, D], up0=[D, 2 * DFF], dn0=[DFF, D], w_kv=[D, 2 * D],
                   w_q=[D, D], w_o=[D, D], up1=[D, 2 * DFF], dn1=[DFF, D])
    for n, shp in wshapes.items():
        W[n] = (din(n, shp), dscr(n + "_b", shp, BF16), Buf("wd_" + n))
    mcw_d = din("mcw", [128, 48 * 4]); mcb_d = din("mcb", [128, 48])
    fcw_d = din("fcw", [128, 2 * 88 * 3]); fcb_d = din("fcb", [128, 2 * 88])
    lng_d = din("lng", [128, 4 * 16]); lnb_d = din("lnb", [128, 4 * 16])
    dtb_d = din("dtb", [64, 1]); arow_d = din("arow", [128, 64]); drow_d = din("drow", [128, 64])
    nrmw_d = din("nrmw", [128, DI])
    albias_d = din("albias", [128, HEADS * 36]); dmat_d = din("dmat", [128, HEADS * 128]); fq_d = din("fq", [128, HEADS]); fqa_d = din("fqa", [128, 4 * HEADS])
    vbias_d = din("vbias", [128, 8 * 64]); esel_d = din("esel", [16, 16 * 128])
    out_d = nc.dram_tensor("out", [S, D], F32, kind="ExternalOutput").ap()
    KT_d = dscr("KT", [HEADS, 128, S], BF16)
    V_d = dscr("V", [HEADS, S // 128, 128, 129], BF16)

    ident_f = kb.sb("ident_f", [128, 128], F32); ident_b = kb.sb("ident_b", [128, 128], BF16)
    tri_f = kb.sb("tri_f", [128, 128], F32); mgt_f = kb.sb("mgt_f", [128, 128], F32); ones_f = kb.sb("ones_f", [128, 128], F32)
    hT_f = kb.sb("hT_f", [128, KC, TT], F32); hT_b = kb.sb("hT_b", [128, KC, TT], BF16)
    NWB = 3
    wb = [kb.sb("wb%d" % i, [128, 16, 256], BF16) for i in range(NWB)]
    S_f = kb.sb("S_f", [128, NG, 512], F32)
    uhalo = kb.sb("uhalo", [128, 48, 3], F32); fhalo = kb.sb("fhalo", [128, 2, 88, 2], F32)
    mcw = kb.sb("mcw_s", [128, 48, 4], F32); mcb = kb.sb("mcb_s", [128, 48], F32)
    fcw = kb.sb("fcw_s", [128, 2, 88, 3], F32); fcb = kb.sb("fcb_s", [128, 2, 88], F32)
    lng = kb.sb("lng_s", [128, 4, 16], F32); lnb = kb.sb("lnb_s", [128, 4, 16], F32)
    dtb = kb.sb("dtb_s", [64, 1], F32); arow = kb.sb("arow_s", [128, 64], F32); drow = kb.sb("drow_s", [128, 64], F32)
    albias = kb.sb("albias_s", [128, HEADS, 36], F32)
    fqa = kb.sb("fqa_s", [128, 4, HEADS], F32)
    fq = kb.sb("fq_s", [128, HEADS], F32); vbias = kb.sb("vbias_s", [128, 8, 64], F32)
    esel = kb.sb("esel_s", [16, 16, 128], BF16)
    kmT = kb.sb("kmT", [128, HEADS, 16], BF16)
    bigT = kb.sb("bigT", [128, 32, TT], BF16)
    mean_s = kb.sb("mean_s", [128, TT], F32); rstd_s = kb.sb("rstd_s", [128, TT], F32)
    sq_s = [kb.sb("sq0", [128, TT], F32)] * 2
    lnt = [kb.sb("lnt%d" % i, [128, TT], F32) for i in range(2)]
    PS = []
    for i in range(8):
        t = es.enter_context(nc.psum_tensor("ps%d" % i, [128, 512], F32))
        PS.append(Buf("ps%d" % i, t, ps=True))

    def psb(i):
        return PS[i].t[:].bitcast(BF16)

    V, A, G = 'dve', 'act', 'pool'

    kb.op(G, lambda e: e.memset(ident_f.t[:], 1.0), writes=[ident_f])
    kb.op(G, lambda e: e.affine_select(out=ident_f.t[:], in_=ident_f.t[:], pattern=[[-1, 128]], compare_op=ALU.is_equal,
                                      fill=0.0, base=0, channel_multiplier=1), reads=[ident_f], writes=[ident_f])
    kb.op(V, lambda e: e.tensor_copy(out=ident_b.t[:], in_=ident_f.t[:]), reads=[ident_f], writes=[ident_b])
    kb.op(G, lambda e: e.memset(tri_f.t[:], 1.0), writes=[tri_f])
    kb.op(G, lambda e: e.affine_select(out=tri_f.t[:], in_=tri_f.t[:], pattern=[[1, 128]], compare_op=ALU.is_ge,
                                      fill=0.0, base=0, channel_multiplier=-1), reads=[tri_f], writes=[tri_f])
    kb.op(G, lambda e: e.memset(mgt_f.t[:], 1.0), writes=[mgt_f])
    kb.op(G, lambda e: e.affine_select(out=mgt_f.t[:], in_=mgt_f.t[:], pattern=[[-1, 128]], compare_op=ALU.is_gt,
                                      fill=0.0, base=0, channel_multiplier=1), reads=[mgt_f], writes=[mgt_f])
    kb.op(G, lambda e: e.memset(ones_f.t[:], 1.0), writes=[ones_f])
    kb.op(G, lambda e: e.memset(S_f.t[:], 0.0), writes=[S_f])
    kb.op(G, lambda e: e.memset(uhalo.t[:], 0.0), writes=[uhalo])
    kb.op(G, lambda e: e.memset(fhalo.t[:], 0.0), writes=[fhalo])

    if STOP == "c0":
        kb.finish(); es.close(); return nc

    def ld(buf, src, view=None):
        kb.dma(G, buf.t[:] if view is None else view, src, buf, writes=[buf])

    ld(mcw, mcw_d.rearrange("p (c k) -> p c k", k=4)); ld(mcb, mcb_d)
    ld(fcw, fcw_d.rearrange("p (l c k) -> p l c k", l=2, k=3)); ld(fcb, fcb_d.rearrange("p (l c) -> p l c", l=2))
    ld(lng, lng_d.rearrange("p (l c) -> p l c", l=4)); ld(lnb, lnb_d.rearrange("p (l c) -> p l c", l=4))
    ld(dtb, dtb_d); ld(arow, arow_d); ld(drow, drow_d)
    kb.op(A, lambda e: e.activation(out=arow.t[:], in_=arow.t[:], func=AF.Exp), reads=[arow], writes=[arow])
    kb.op(V, lambda e: e.tensor_scalar(out=arow.t[:], in0=arow.t[:], scalar1=-1.0, scalar2=None, op0=ALU.mult), reads=[arow], writes=[arow])
    kb.op(G, lambda e: e.memset(kmT.t[:], 0.0), writes=[kmT])
    ld(albias, albias_d.rearrange("p (h j) -> p h j", h=HEADS))
    ld(fqa, fqa_d.rearrange("p (s h) -> p s h", s=4)); ld(fq, fq_d); ld(vbias, vbias_d.rearrange("p (i c) -> p i c", i=8))
    ld(esel, esel_d.rearrange("p (n k) -> p n k", n=16))

    if STOP == "c1":
        kb.finish(); es.close(); return nc
    def convert(names):
        for n in names:
            src, dst, wbuf = W[n]
            K = src.shape[0]
            rb = 256
            for r0 in range(0, K, rb):
                kb.dma(G, dst[r0:r0 + rb, :], src[r0:r0 + rb, :], wbuf, writes=[wbuf])
            kb.skip_ds.add(wbuf.ds)
    if STOP not in ("x0", "x1", "x2"):
        convert(["w_in"])

    wctr = [0]

    def linT(wname, K, blocks, rhs_fn, rhs_bufs, epi, ps_banks=(0, 1, 2, 3)):
        _, wdst, wbuf = W[wname]
        nkb = (K + 2047) // 2048
        pending = []
        for bi, segs in enumerate(blocks):
            ncols = sum(s[1] for s in segs)
            nj = ncols // 128 if ncols >= 128 else 1
            banks = [PS[ps_banks[(2 * (bi % 2) + j) % len(ps_banks)]] for j in range(nj)]
            for kbi in range(nkb):
                k0 = kbi * 2048
                kcs = min(16, (K - k0) // 128)
                wt = wb[wctr[0] % NWB]
                wctr[0] += 1
                c = 0
                for (c0, n) in segs:
                    kb.dma('sp', wt.t[:, 0:kcs, c:c + n],
                           wdst[k0:k0 + kcs * 128, c0:c0 + n].rearrange("(kc p) n -> p kc n", p=128),
                           wt, reads=[wbuf], writes=[wt])
                    c += n
                for j in range(nj):
                    m = min(128, ncols)
                    for kc in range(kcs):
                        first = (kbi == 0 and kc == 0)
                        last = (kbi == nkb - 1 and kc == kcs - 1)
                        kb.mm(banks[j], banks[j].t[0:m, :], lhsT=wt.t[:, kc, j * 128:j * 128 + m], rhs=rhs_fn(k0 // 128 + kc),
                              start=first, stop=last, reads=[wt] + list(rhs_bufs), sig=(kc == kcs - 1))
            for f in pending:
                f()
            pending = [(lambda bi=bi, j=j, b=banks[j]: epi(bi, j, b)) for j in range(nj)]
        for f in pending:
            f()

    def layer_norm(li):
        ps_sum, ps_sq = PS[6], PS[7]
        for j in range(KC):
            sq = sq_s[j % 2]
            kb.op(A, lambda e: e.activation(out=sq.t[:], in_=hT_f.t[:, j, :], func=AF.Square), reads=[hT_f], writes=[sq])
            kb.mm(ps_sum, ps_sum.t[:], lhsT=ones_f.t[:], rhs=hT_f.t[:, j, :], start=(j == 0), stop=(j == KC - 1), reads=[ones_f, hT_f])
            kb.mm(ps_sq, ps_sq.t[:], lhsT=ones_f.t[:], rhs=sq.t[:], start=(j == 0), stop=(j == KC - 1), reads=[ones_f, sq], sig=True)
        kb.op(V, lambda e: e.tensor_scalar(out=mean_s.t[:], in0=ps_sum.t[:], scalar1=1.0 / D, scalar2=None, op0=ALU.mult),
              reads=[ps_sum], writes=[mean_s])
        t0 = lnt[0]
        kb.op(V, lambda e: e.tensor_tensor(out=t0.t[:], in0=mean_s.t[:], in1=mean_s.t[:], op=ALU.mult), reads=[mean_s], writes=[t0])
        kb.op(V, lambda e: e.scalar_tensor_tensor(out=rstd_s.t[:], in0=ps_sq.t[:], scalar=1.0 / D, in1=t0.t[:],
                                                  op0=ALU.mult, op1=ALU.subtract), reads=[ps_sq, t0], writes=[rstd_s])
        kb.op(V, lambda e: e.tensor_scalar(out=rstd_s.t[:], in0=rstd_s.t[:], scalar1=EPS, scalar2=None, op0=ALU.add),
              reads=[rstd_s], writes=[rstd_s])
        kb.op(A, lambda e: e.activation(out=rstd_s.t[:], in_=rstd_s.t[:], func=AF.Sqrt), reads=[rstd_s], writes=[rstd_s])
        kb.op(V, lambda e: e.reciprocal(out=rstd_s.t[:], in_=rstd_s.t[:]), reads=[rstd_s], writes=[rstd_s])
        for j in range(KC):
            t = lnt[j % 2]
            kb.op(V, lambda e: e.tensor_tensor(out=t.t[:], in0=hT_f.t[:, j, :], in1=mean_s.t[:], op=ALU.subtract),
                  reads=[hT_f, mean_s], writes=[t])
            kb.op(V, lambda e: e.tensor_tensor(out=t.t[:], in0=t.t[:], in1=rstd_s.t[:], op=ALU.mult), reads=[t, rstd_s], writes=[t])
            kb.op(V, lambda e: e.tensor_scalar(out=hT_f.t[:, j, :], in0=t.t[:], scalar1=lng.t[:, li, j:j + 1], scalar2=lnb.t[:, li, j:j + 1],
                                               op0=ALU.mult, op1=ALU.add), reads=[t, lng, lnb], writes=[hT_f])
            kb.op(A, lambda e: e.activation(out=hT_b.t[:, j, :], in_=hT_f.t[:, j, :], func=AF.Copy), reads=[hT_f], writes=[hT_b])

    def resid_epi(bi, j, ps):
        ch = bi * 2 + j
        kb.op(V, lambda e: e.scalar_tensor_tensor(out=hT_f.t[:, ch, :], in0=hT_f.t[:, ch, :], scalar=ALPHA, in1=ps.t[:],
                                                  op0=ALU.mult, op1=ALU.add), reads=[hT_f, ps], writes=[hT_f])

    blocks_dm = [[(c0, 256)] for c0 in range(0, D, 256)]

    def ffn(layer, pes):
        act = kb.sb("ffn_act", [128, FC, TT], BF16, pes)
        fu = [kb.sb("ffn_u%d" % i, [128, 2 + TT], F32, pes) for i in range(2)]
        facc = [kb.sb("ffn_acc%d" % i, [128, TT], F32, pes) for i in range(2)]
        gact = [kb.sb("ffn_g%d" % i, [128, TT], BF16, pes) for i in range(2)]
        wn = "up%d" % layer
        blocks = []
        for c in range(0, FC, 2):
            blocks.append([(c * 128, 256)])
            blocks.append([(DFF + c * 128, 256)])
        ctr = [0]

        def epi(bi, j, ps):
            isval = bi % 2
            c = (bi // 2) * 2 + j
            ch = isval * FC + c
            u = fu[ctr[0] % 2]; acc = facc[ctr[0] % 2]; ctr[0] += 1
            kb.op(V, lambda e: e.tensor_copy(out=u.t[:, 0:2], in_=fhalo.t[:, layer, ch, :]), reads=[fhalo], writes=[u])
            kb.op(A, lambda e: e.activation(out=u.t[:, 2:2 + TT], in_=ps.t[:], func=AF.Copy), reads=[ps], writes=[u])
            kb.op(V, lambda e: e.tensor_copy(out=fhalo.t[:, layer, ch, :], in_=u.t[:, TT:TT + 2]), reads=[u], writes=[fhalo])
            kb.op(V, lambda e: e.tensor_scalar(out=acc.t[:], in0=u.t[:, 0:TT], scalar1=fcw.t[:, layer, ch, 0:1], scalar2=None, op0=ALU.mult),
                  reads=[u, fcw], writes=[acc])
            for k in (1, 2):
                kb.op(V, lambda e: e.scalar_tensor_tensor(out=acc.t[:], in0=u.t[:, k:k + TT], scalar=fcw.t[:, layer, ch, k:k + 1], in1=acc.t[:],
                                                          op0=ALU.mult, op1=ALU.add), reads=[u, fcw, acc], writes=[acc])
            if not isval:
                g = gact[j]
                kb.op(A, lambda e: e.activation(out=g.t[:], in_=acc.t[:], func=AF.Silu, bias=fcb.t[:, layer, ch:ch + 1], scale=1.0),
                      reads=[acc, fcb], writes=[g])
            else:
                g = gact[j]
                kb.op(V, lambda e: e.scalar_tensor_tensor(out=act.t[:, c, :], in0=acc.t[:], scalar=fcb.t[:, layer, ch:ch + 1], in1=g.t[:],
                                                          op0=ALU.add, op1=ALU.mult), reads=[acc, fcb, g], writes=[act])

        linT(wn, D, blocks, lambda kc: hT_b.t[:, kc, :], [hT_b], epi)
        linT("dn%d" % layer, DFF, blocks_dm, lambda kc: act.t[:, kc, :], [act], resid_epi)

    for it in range(NT):
        tok0 = it * TT
        with ExitStack() as pes:
            xin = [kb.sb("xin%d" % i, [128, D], F32, pes) for i in range(2)]
            for s in range(4):
                xt = xin[s % 2]
                kb.dma(G, xt.t[:], x_d[tok0 + s * 128: tok0 + (s + 1) * 128, :], xt, writes=[xt])
                for b4 in range(4 if STOP != "x2" else 0):
                    ps = PS[4 + (b4 % 2)]
                    for j in range(4):
                        kc = b4 * 4 + j
                        kb.mm(ps, ps.t[:, j * 128:(j + 1) * 128], xt.t[:, kc * 128:(kc + 1) * 128], ident_f.t[:], start=True, stop=True,
                              reads=[xt, ident_f], sig=(j == 3), transpose=True)
                    src = ps.t[:].rearrange("p (j t) -> p j t", j=4)
                    kb.op(A, lambda e: e.activation(out=hT_f.t[:, b4 * 4:(b4 + 1) * 4, s * 128:(s + 1) * 128], in_=src, func=AF.Copy),
                          reads=[ps], writes=[hT_f])
                    kb.op(V, lambda e: e.tensor_copy(out=hT_b.t[:, b4 * 4:(b4 + 1) * 4, s * 128:(s + 1) * 128], in_=src),
                          reads=[ps], writes=[hT_b])
            kb.barrier()
        if STOP in ("x1", "x2"):
            kb.finish(); es.close(); return nc
        if STOP in ("x", "x0"):
            break
        if it == 0:
            convert(["w_out", "up0", "dn0", "w_kv", "w_q", "w_o", "up1", "dn1"])

        with ExitStack() as pes:
            u_s = [kb.sb("m_u%d" % i, [128, 3 + TT], F32, pes) for i in range(2)]
            acc_s = [kb.sb("m_acc%d" % i, [128, TT], F32, pes) for i in range(2)]
            sT = [kb.sb("m_sT%d" % i, [128, TT], BF16, pes) for i in range(2)]
            xg = kb.sb("m_xg", [128, 4, 512], BF16, pes); zg = kb.sb("m_zg", [128, 4, 512], BF16, pes)
            Btok = kb.sb("m_Btok", [128, 4, 128], BF16, pes)
            BT = kb.sb("m_BT", [128, TT], BF16, pes); CT = kb.sb("m_CT", [128, TT], BF16, pes)
            dtT = kb.sb("m_dtT", [64, TT], F32, pes)
            dt_t = kb.sb("m_dt", [128, 4, 64], F32, pes); da_t = kb.sb("m_da", [128, 4, 64], F32, pes)
            E_t = kb.sb("m_E", [128, 4, 64], F32, pes); dtx_t = kb.sb("m_dtx", [128, 4, 64], F32, pes)
            Dt_t = kb.sb("m_Dt", [128, 4, 64], F32, pes)
            S_b = kb.sb("m_Sb", [128, 512], BF16, pes)
            sr = [kb.sb("m_sr%d" % i, [128, 8, 128], F32, pes) for i in range(2)]
            Lg = [kb.sb("m_Lg%d" % i, [128, 1024], BF16, pes) for i in range(4)]
            CBm = kb.sb("m_CBm", [128, 128], F32, pes)
            MT = [kb.sb("m_MT%d" % i, [128, 512], BF16, pes) for i in range(2)]
            xdt = kb.sb("m_xdt", [128, 512], BF16, pes); xdte = kb.sb("m_xdte", [128, 512], BF16, pes)
            y1 = kb.sb("m_y1", [128, 512], F32, pes); y2 = kb.sb("m_y2", [128, 512], F32, pes)
            junk = kb.sb("m_junk", [128, 512], F32, pes)
            ss = kb.sb("m_ss", [128, 1], F32, pes); rs = kb.sb("m_rs", [128, 1], F32, pes)
            yn = kb.sb("m_yn", [128, 512], BF16, pes)
            nrm_g = [kb.sb("m_nrm0", [128, 512], F32, pes)] * 2

            def dt_epi(bi, j, ps):
                kb.op(A, lambda e: e.activation(out=dtT.t[:], in_=ps.t[0:64, :], func=AF.Exp, bias=dtb.t[:, 0:1], scale=1.0),
                      reads=[ps, dtb], writes=[dtT])
                kb.op(A, lambda e: e.activation(out=dtT.t[:], in_=dtT.t[:], func=AF.Ln, bias=1.0, scale=1.0), reads=[dtT], writes=[dtT])
            linT("w_in", D, [[(10240, 64)]], lambda kc: hT_b.t[:, kc, :], [hT_b], dt_epi)
            psd = PS[4]
            for s in range(4):
                kb.mm(psd, psd.t[:, s * 64:(s + 1) * 64], dtT.t[:, s * 128:(s + 1) * 128], ident_f.t[0:64, 0:64], start=True, stop=True,
                      reads=[dtT, ident_f], sig=(s == 3), transpose=True)
            kb.op(V, lambda e: e.tensor_copy(out=dt_t.t[:], in_=psd.t[:, 0:256].rearrange("p (s h) -> p s h", s=4)), reads=[psd], writes=[dt_t])
            kb.op(V, lambda e: e.tensor_tensor(out=da_t.t[:], in0=dt_t.t[:], in1=arow.t[:].unsqueeze(1).to_broadcast([128, 4, 64]), op=ALU.mult),
                  reads=[dt_t, arow], writes=[da_t])
            pc, ptot = PS[5], PS[6]
            for s in range(4):
                kb.mm(pc, pc.t[:, s * 64:(s + 1) * 64], lhsT=tri_f.t[:], rhs=da_t.t[:, s, :], start=True, stop=True, reads=[tri_f, da_t], sig=(s == 3))
            for s in range(4):
                kb.mm(ptot, ptot.t[:, s * 64:(s + 1) * 64], lhsT=ones_f.t[:], rhs=da_t.t[:, s, :], start=True, stop=True, reads=[ones_f, da_t], sig=(s == 3))
            v4 = lambda b: b.t[:].rearrange("p s h -> p (s h)")
            kb.op(A, lambda e: e.activation(out=v4(E_t), in_=pc.t[:, 0:256], func=AF.Exp), reads=[pc], writes=[E_t])
            kb.op(A, lambda e: e.activation(out=v4(Dt_t), in_=ptot.t[:, 0:256], func=AF.Exp), reads=[ptot], writes=[Dt_t])
            kb.op(V, lambda e: e.tensor_copy(out=v4(dtx_t), in_=pc.t[:, 0:256]), reads=[pc], writes=[dtx_t])
            kb.op(V, lambda e: e.tensor_tensor(out=v4(dtx_t), in0=ptot.t[:, 0:256], in1=v4(dtx_t), op=ALU.subtract),
                  reads=[ptot, dtx_t], writes=[dtx_t])
            kb.op(A, lambda e: e.activation(out=v4(dtx_t), in_=v4(dtx_t), func=AF.Exp), reads=[dtx_t], writes=[dtx_t])
            kb.op(V, lambda e: e.tensor_tensor(out=v4(dtx_t), in0=v4(dtx_t), in1=v4(dt_t), op=ALU.mult), reads=[dtx_t, dt_t], writes=[dtx_t])

            cctr = [0]

            def conv_silu(ps, ch, dst):
                u = u_s[cctr[0] % 2]; acc = acc_s[cctr[0] % 2]; cctr[0] += 1
                kb.op(V, lambda e: e.tensor_copy(out=u.t[:, 0:3], in_=uhalo.t[:, ch, :]), reads=[uhalo], writes=[u])
                kb.op(A, lambda e: e.activation(out=u.t[:, 3:3 + TT], in_=ps.t[:], func=AF.Copy), reads=[ps], writes=[u])
                kb.op(V, lambda e: e.tensor_copy(out=uhalo.t[:, ch, :], in_=u.t[:, TT:TT + 3]), reads=[u], writes=[uhalo])
                kb.op(V, lambda e: e.tensor_scalar(out=acc.t[:], in0=u.t[:, 0:TT], scalar1=mcw.t[:, ch, 0:1], scalar2=None, op0=ALU.mult),
                      reads=[u, mcw], writes=[acc])
                for k in (1, 2, 3):
                    kb.op(V, lambda e: e.scalar_tensor_tensor(out=acc.t[:], in0=u.t[:, k:k + TT], scalar=mcw.t[:, ch, k:k + 1], in1=acc.t[:],
                                                              op0=ALU.mult, op1=ALU.add), reads=[u, mcw, acc], writes=[acc])
                kb.op(A, lambda e: e.activation(out=dst.t[:], in_=acc.t[:], func=AF.Silu, bias=mcb.t[:, ch:ch + 1], scale=1.0),
                      reads=[acc, mcb], writes=[dst])

            tctr = [0]

            def to_tokmajor(src, dst, col0):
                bank = 6 + (tctr[0] % 2); tctr[0] += 1
                ps = PS[bank]
                pv = psb(bank)
                for s in range(4):
                    kb.mm(ps, pv[:, s * 128:(s + 1) * 128], src.t[:, s * 128:(s + 1) * 128], ident_b.t[:], start=True, stop=True,
                          reads=[src, ident_b], sig=(s == 3), transpose=True)
                kb.op(A, lambda e: e.activation(out=dst.t[:, :, col0:col0 + 128], in_=pv[:, 0:512].rearrange("p (s c) -> p s c", s=4), func=AF.Copy),
                      reads=[ps], writes=[dst])

            for g in range(NG):
                nrmw = nrm_g[g % 2]
                kb.dma(G, nrmw.t[:], nrmw_d[:, g * 512:(g + 1) * 512], nrmw, writes=[nrmw])
                hs = slice(g * 8, (g + 1) * 8)
                for s in range(4):
                    srb = sr[s % 2]
                    kb.op(V, lambda e: e.tensor_tensor(out=srb.t[:], in0=tri_f.t[:].unsqueeze(1).to_broadcast([128, 8, 128]),
                                                       in1=da_t.t[:, s, hs].unsqueeze(2).to_broadcast([128, 8, 128]), op=ALU.mult),
                          reads=[tri_f, da_t], writes=[srb])
                    for hh in range(2):
                        pseg = PS[hh]
                        kb.mm(pseg, pseg.t[:], lhsT=mgt_f.t[:], rhs=srb.t[:, hh * 4:(hh + 1) * 4, :].rearrange("p q l -> p (q l)"), start=True, stop=True,
                              reads=[mgt_f, srb])
                        kb.op(A, lambda e: e.activation(out=Lg[s].t[:, hh * 512:(hh + 1) * 512], in_=pseg.t[:], func=AF.Exp), reads=[pseg], writes=[Lg[s]])
                def x_epi(bi, j, ps, g=g):
                    c = bi * 2 + j
                    st = sT[c % 2]
                    conv_silu(ps, g * 4 + c, st)
                    to_tokmajor(st, xg, c * 128)
                linT("w_in", D, [[(DI + g * 512, 256)], [(DI + g * 512 + 256, 256)]], lambda kc: hT_b.t[:, kc, :], [hT_b], x_epi)

                def bc_epi(bi, j, ps, g=g):
                    if j == 0:
                        conv_silu(ps, 32 + g, BT)
                        bank = 6 + (tctr[0] % 2); tctr[0] += 1
                        pst = PS[bank]; pv = psb(bank)
                        for s in range(4):
                            kb.mm(pst, pv[:, s * 128:(s + 1) * 128], BT.t[:, s * 128:(s + 1) * 128], ident_b.t[:], start=True, stop=True,
                                  reads=[BT, ident_b], sig=(s == 3), transpose=True)
                        kb.op(A, lambda e: e.activation(out=Btok.t[:], in_=pv[:, 0:512].rearrange("p (s c) -> p s c", s=4), func=AF.Copy),
                              reads=[pst], writes=[Btok])
                    else:
                        conv_silu(ps, 40 + g, CT)
                linT("w_in", D, [[(2 * DI + g * 128, 128), (2 * DI + 1024 + g * 128, 128)]], lambda kc: hT_b.t[:, kc, :], [hT_b], bc_epi)

                def z_epi(bi, j, ps, g=g):
                    c = bi * 2 + j
                    st = sT[c % 2]
                    kb.op(A, lambda e: e.activation(out=st.t[:], in_=ps.t[:], func=AF.Silu), reads=[ps], writes=[st])
                    to_tokmajor(st, zg, c * 128)
                linT("w_in", D, [[(g * 512, 256)], [(g * 512 + 256, 256)]], lambda kc: hT_b.t[:, kc, :], [hT_b], z_epi)

                kb.op(A, lambda e: e.activation(out=S_b.t[:], in_=S_f.t[:, g, :], func=AF.Copy), reads=[S_f], writes=[S_b])
                hs = slice(g * 8, (g + 1) * 8)
                v3 = lambda ap: ap.rearrange("p (h d) -> p h d", h=8)

                def ssd_early(s):
                    cs = slice(s * 128, (s + 1) * 128)
                    kb.op(G, lambda e: e.tensor_tensor(out=v3(xdt.t[:]), in0=v3(xg.t[:, s, :]),
                                                       in1=dt_t.t[:, s, hs].unsqueeze(2).to_broadcast([128, 8, 64]), op=ALU.mult),
                          reads=[xg, dt_t], writes=[xdt])
                    kb.op(G, lambda e: e.tensor_tensor(out=v3(xdte.t[:]), in0=v3(xg.t[:, s, :]),
                                                       in1=dtx_t.t[:, s, hs].unsqueeze(2).to_broadcast([128, 8, 64]), op=ALU.mult),
                          reads=[xg, dtx_t], writes=[xdte])
                    kb.op(G, lambda e: e.tensor_tensor(out=v3(y2.t[:]), in0=v3(xg.t[:, s, :]), in1=drow.t[:, hs].unsqueeze(2).to_broadcast([128, 8, 64]), op=ALU.mult),
                          reads=[xg, drow], writes=[y2])
                    poff = PS[4]
                    kb.mm(poff, poff.t[:], lhsT=CT.t[:, cs], rhs=S_b.t[:], start=True, stop=True, reads=[CT, S_b])
                    pstt = PS[5]
                    kb.mm(pstt, pstt.t[:], lhsT=Btok.t[:, s, :], rhs=xdte.t[:], start=True, stop=True, reads=[Btok, xdte])
                    pcb = PS[2]
                    kb.mm(pcb, pcb.t[:, 0:128], lhsT=BT.t[:, cs], rhs=CT.t[:, cs], start=True, stop=True, reads=[BT, CT])
                    kb.op(V, lambda e: e.tensor_tensor(out=CBm.t[:], in0=pcb.t[:, 0:128], in1=tri_f.t[:], op=ALU.mult),
                          reads=[pcb, tri_f], writes=[CBm])
                    kb.op(G, lambda e: e.tensor_tensor(out=v3(S_f.t[:, g, :]), in0=v3(S_f.t[:, g, :]), in1=Dt_t.t[:, s, hs].unsqueeze(2).to_broadcast([128, 8, 64]), op=ALU.mult),
                          reads=[S_f, Dt_t], writes=[S_f])
                    kb.op(V, lambda e: e.tensor_tensor(out=S_f.t[:, g, :], in0=S_f.t[:, g, :], in1=pstt.t[:], op=ALU.add), reads=[S_f, pstt], writes=[S_f])
                    kb.op(A, lambda e: e.activation(out=S_b.t[:], in_=S_f.t[:, g, :], func=AF.Copy), reads=[S_f], writes=[S_b])

                def ssd_late(s):
                    py = PS[3]; poff = PS[4]
                    for hh in range(2):
                        L = Lg[s]; M = MT[hh]
                        kb.op(V, lambda e: e.tensor_tensor(out=M.t[:].rearrange("p (q l) -> p q l", q=4), in0=L.t[:, hh * 512:(hh + 1) * 512].rearrange("p (q l) -> p q l", q=4),
                                                           in1=CBm.t[:].unsqueeze(1).to_broadcast([128, 4, 128]), op=ALU.mult),
                              reads=[L, CBm], writes=[M])
                        for q in range(4):
                            hl = hh * 4 + q
                            kb.mm(py, py.t[:, hl * 64:(hl + 1) * 64], lhsT=M.t[:, q * 128:(q + 1) * 128], rhs=xdt.t[:, hl * 64:(hl + 1) * 64],
                                  start=True, stop=True, reads=[M, xdt], sig=(q == 3))
                    kb.op(V, lambda e: e.tensor_tensor(out=v3(y1.t[:]), in0=v3(poff.t[:]), in1=E_t.t[:, s, hs].unsqueeze(2).to_broadcast([128, 8, 64]), op=ALU.mult),
                          reads=[poff, E_t], writes=[y1])
                    kb.op(V, lambda e: e.tensor_tensor(out=y1.t[:], in0=y1.t[:], in1=y2.t[:], op=ALU.add), reads=[y1, y2], writes=[y1])
                    kb.op(V, lambda e: e.tensor_tensor(out=y1.t[:], in0=y1.t[:], in1=py.t[:], op=ALU.add), reads=[y1, py], writes=[y1])
                    kb.op(V, lambda e: e.tensor_tensor(out=y1.t[:], in0=y1.t[:], in1=zg.t[:, s, :], op=ALU.mult), reads=[y1, zg], writes=[y1])
                    kb.op(V, lambda e: e.tensor_tensor(out=junk.t[:], in0=y1.t[:], in1=y1.t[:], op=ALU.mult), reads=[y1], writes=[junk])
                    kb.op(V, lambda e: e.reduce_sum(out=ss.t[:], in_=junk.t[:], axis=AX.X), reads=[junk], writes=[ss])
                    kb.op(V, lambda e: e.tensor_scalar(out=rs.t[:], in0=ss.t[:], scalar1=1.0 / 512, scalar2=EPS, op0=ALU.mult, op1=ALU.add),
                          reads=[ss], writes=[rs])
                    kb.op(A, lambda e: e.activation(out=rs.t[:], in_=rs.t[:], func=AF.Ln), reads=[rs], writes=[rs])
                    kb.op(A, lambda e: e.activation(out=rs.t[:], in_=rs.t[:], func=AF.Exp, scale=-0.5), reads=[rs], writes=[rs])
                    kb.op(V, lambda e: e.scalar_tensor_tensor(out=yn.t[:], in0=y1.t[:], scalar=rs.t[:, 0:1], in1=nrmw.t[:],
                                                              op0=ALU.mult, op1=ALU.mult), reads=[y1, rs, nrmw], writes=[yn])

                def ssd_trans(s):
                    cs = slice(s * 128, (s + 1) * 128)
                    bank = 6 + (tctr[0] % 2); tctr[0] += 1
                    pst = PS[bank]; pv = psb(bank)
                    for q in range(4):
                        kb.mm(pst, pv[:, q * 128:(q + 1) * 128], yn.t[:, q * 128:(q + 1) * 128], ident_b.t[:], start=True, stop=True,
                              reads=[yn, ident_b], sig=(q == 3), transpose=True)
                    kb.op(A, lambda e: e.activation(out=bigT.t[:, g * 4:(g + 1) * 4, cs], in_=pv[:, 0:512].rearrange("p (q t) -> p q t", q=4), func=AF.Copy),
                          reads=[pst], writes=[bigT])

                for s in range(4):
                    ssd_early(s)
                    if s > 0:
                        ssd_trans(s - 1)
                    ssd_late(s)
                ssd_trans(3)
            kb.barrier()

        linT("w_out", DI, blocks_dm, lambda kc: bigT.t[:, kc, :], [bigT], resid_epi)
        layer_norm(0)
        if STOP == "hmid0":
            break
        kb.barrier()
        with ExitStack() as pes:
            ffn(0, pes)
            layer_norm(1)
            kb.barrier()
        if STOP == "h1":
            break

        with ExitStack() as pes:
            kst = [kb.sb("kv_k%d" % i, [128, TT], BF16, pes) for i in range(2)]
            vT = [kb.sb("kv_vT%d" % i, [128, TT], BF16, pes) for i in range(2)]
            vst = [kb.sb("kv_v%d" % i, [128, 4, 129], BF16, pes) for i in range(2)]
            kms = kb.sb("kv_kms", [128, 2], F32, pes)
            for b in vst:
                kb.op(G, lambda e: e.memset(b.t[:], 1.0), writes=[b])

            def kv_epi(bi, j, ps):
                ch = bi * 2 + j
                if ch < 16:
                    h = ch
                    k = kst[h % 2]
                    kb.op(A, lambda e: e.activation(out=k.t[:], in_=ps.t[:], func=AF.Copy), reads=[ps], writes=[k])
                    kb.dma(G, KT_d[h, :, tok0:tok0 + TT], k.t[:], k, reads=[k])
                    kb.op(V, lambda e: e.tensor_reduce(out=kms.t[:], in_=ps.t[:].rearrange("p (b t) -> p b t", b=2), axis=AX.X, op=ALU.add),
                          reads=[ps], writes=[kms])
                    kb.op(V, lambda e: e.tensor_scalar(out=kmT.t[:, h, 2 * it:2 * it + 2], in0=kms.t[:], scalar1=1.0 / 256, scalar2=None, op0=ALU.mult),
                          reads=[kms], writes=[kmT])
                else:
                    h = ch - 16
                    vt = vT[h % 2]; vs = vst[h % 2]
                    kb.op(A, lambda e: e.activation(out=vt.t[:], in_=ps.t[:], func=AF.Copy), reads=[ps], writes=[vt])
                    bank = 6 + (h % 2)
                    pst = PS[bank]; pv = psb(bank)
                    for s in range(4):
                        kb.mm(pst, pv[:, s * 128:(s + 1) * 128], vt.t[:, s * 128:(s + 1) * 128], ident_b.t[:], start=True, stop=True,
                              reads=[vt, ident_b], sig=(s == 3), transpose=True)
                    kb.op(V, lambda e: e.tensor_copy(out=vs.t[:, :, 0:128], in_=pv[:, 0:512].rearrange("p (s c) -> p s c", s=4)), reads=[pst], writes=[vs])
                    kb.dma(G, V_d[h, 4 * it:4 * it + 4, :, :].rearrange("s p c -> p s c"), vs.t[:], vs, reads=[vs])
            linT("w_kv", D, [[(c0, 256)] for c0 in range(0, 2 * D, 256)], lambda kc: hT_b.t[:, kc, :], [hT_b], kv_epi)
            kb.barrier()

        with ExitStack() as pes:
            nk = (it + 1) * TT
            nkt = nk // 128
            qT = kb.sb("a_qT", [128, HEADS, TT], BF16, pes)
            kbuf = [kb.sb("a_k%d" % i, [128, S], BF16, pes) for i in range(1)]
            vbuf = [kb.sb("a_v%d" % i, [128, S // 128, 129], BF16, pes) for i in range(1)]
            gm = kb.sb("a_gm", [128, 4, 16], F32, pes); mx8 = kb.sb("a_mx8", [128, 8], F32, pes)
            mb = kb.sb("a_mb", [128, 4, 16], BF16, pes); mbf = kb.sb("a_mbf", [128, 4, 16], F32, pes)
            negm = kb.sb("a_negm", [16, TT], BF16, pes)
            PT = [kb.sb("a_PT%d" % i, [128, TT], BF16, pes) for i in range(3)]
            dtmp = kb.sb("a_dtmp", [128, 128], F32, pes)
            num = kb.sb("a_num", [128, 129], F32, pes); rec = kb.sb("a_rec", [128, 1], F32, pes)
            otok = kb.sb("a_otok", [128, 4, 128], BF16, pes)
            dmat = kb.sb("a_dmat", [128, HEADS, 128], F32, pes)
            kb.dma(G, dmat.t[:], dmat_d.rearrange("p (h q) -> p h q", h=HEADS), dmat, writes=[dmat])

            def q_epi(bi, j, ps):
                h = bi * 2 + j
                kb.op(A, lambda e: e.activation(out=qT.t[:, h, :], in_=ps.t[:], func=AF.Copy), reads=[ps], writes=[qT])
            linT("w_q", D, blocks_dm, lambda kc: hT_b.t[:, kc, :], [hT_b], q_epi)

            pctr = [0]
            for h in range(HEADS):
                kbf = kbuf[0]; vbf = vbuf[0]
                kb.dma(G, kbf.t[:, 0:nk], KT_d[h, :, 0:nk], kbf, writes=[kbf])
                kb.dma(G, vbf.t[:, 0:nkt, :], V_d[h, 0:nkt, :, :].rearrange("s p c -> p s c"), vbf, writes=[vbf])
                pg = PS[7]
                for s in range(4):
                    kb.mm(pg, pg.t[:, s * 16:(s + 1) * 16], lhsT=qT.t[:, h, s * 128:(s + 1) * 128], rhs=kmT.t[:, h, :], start=True, stop=True,
                          reads=[qT, kmT], sig=(s == 3))
                kb.op(V, lambda e: e.tensor_tensor(out=gm.t[:].rearrange("p s n -> p (s n)"), in0=pg.t[:, 0:64], in1=vbias.t[:, it, :], op=ALU.add),
                      reads=[pg, vbias], writes=[gm])
                for s in range(4):
                    kb.op(V, lambda e: e.max(out=mx8.t[:], in_=gm.t[:, s, :]), reads=[gm], writes=[mx8])
                    kb.op(V, lambda e: e.tensor_scalar(out=mbf.t[:, s, :], in0=gm.t[:, s, :], scalar1=mx8.t[:, 2:3], scalar2=-NEG, op0=ALU.is_ge, op1=ALU.mult),
                          reads=[gm, mx8], writes=[mbf])
                    kb.op(V, lambda e: e.scalar_tensor_tensor(out=mb.t[:, s, :], in0=mbf.t[:, s, :], scalar=NEG, in1=vbias.t[:, it, s * 16:(s + 1) * 16],
                                                              op0=ALU.add, op1=ALU.add), reads=[mbf, vbias], writes=[mb])
                pm = PS[7]; pmv = psb(7)
                for s in range(4):
                    kb.mm(pm, pmv[0:16, s * 128:(s + 1) * 128], mb.t[:, s, :], ident_b.t[:], start=True, stop=True, reads=[mb, ident_b], sig=(s == 3), transpose=True)
                kb.op(V, lambda e: e.tensor_copy(out=negm.t[:], in_=pmv[0:16, 0:512]), reads=[pm], writes=[negm])

                poA = [PS[2], PS[2], PS[3], PS[3]]; colA = [0, 129, 0, 129]
                poB = [PS[4], PS[5], PS[4], PS[5]]; colB = [0, 0, 129, 129]
                pdiag = PS[6]
                nA = 4 * it
                nB = [0, 1, 2, 3]
                doneA = [0, 0, 0, 0]; doneB = [0, 0, 0, 0]

                def pv_mm(kind, s, kt, Pbuf, P):
                    if kind == 'D':
                        kb.mm(pdiag, pdiag.t[:, 0:129], lhsT=P, rhs=vbf.t[:, kt, :], start=True, stop=True, reads=[vbf, Pbuf])
                    elif kind == 'A':
                        ps = poA[s]; c = colA[s]
                        first = doneA[s] == 0
                        doneA[s] += 1
                        kb.mm(ps, ps.t[:, c:c + 129], lhsT=P, rhs=vbf.t[:, kt, :], start=first, stop=(doneA[s] == nA), reads=[vbf, Pbuf], sig=True)
                    else:
                        ps = poB[s]; c = colB[s]
                        first = doneB[s] == 0
                        doneB[s] += 1
                        kb.mm(ps, ps.t[:, c:c + 129], lhsT=P, rhs=vbf.t[:, kt, :], start=first, stop=(doneB[s] == nB[s]), reads=[vbf, Pbuf], sig=True)

                prevA = None
                for kt in range(4 * it):
                    n = kt // 2
                    pss = PS[pctr[0] % 2]; P = PT[pctr[0] % 3]; pctr[0] += 1
                    kb.mm(pss, pss.t[:], lhsT=kbf.t[:, kt * 128:(kt + 1) * 128], rhs=qT.t[:, h, :], start=True, stop=False, reads=[kbf, qT], sig=False)
                    kb.mm(pss, pss.t[:], lhsT=esel.t[:, n, :], rhs=negm.t[:], start=False, stop=True, reads=[esel, negm], sig=True)
                    jj = 4 * it - kt
                    kb.op(A, lambda e: e.activation(out=P.t[:], in_=pss.t[:], func=AF.Exp, bias=albias.t[:, h, jj:jj + 1], scale=SCALE),
                          reads=[pss, albias], writes=[P])
                    if prevA is not None:
                        pk, pP = prevA
                        for s in range(4):
                            pv_mm('A', s, pk, pP, pP.t[:, s * 128:(s + 1) * 128])
                    prevA = (kt, P)
                if prevA is not None:
                    pk, pP = prevA
                    for s in range(4):
                        pv_mm('A', s, pk, pP, pP.t[:, s * 128:(s + 1) * 128])
                for s in range(4):
                    own = 2 * it + s // 2
                    for sp in range(s + 1):
                        kt = 4 * it + sp
                        n = kt // 2
                        pss = PS[pctr[0] % 2]; P = PT[pctr[0] % 3]; pctr[0] += 1
                        qs = slice(s * 128, (s + 1) * 128)
                        need_mask = (n < own)
                        kb.mm(pss, pss.t[:, 0:128], lhsT=kbf.t[:, kt * 128:(kt + 1) * 128], rhs=qT.t[:, h, qs], start=True, stop=not need_mask,
                              reads=[kbf, qT], sig=not need_mask)
                        if need_mask:
                            kb.mm(pss, pss.t[:, 0:128], lhsT=esel.t[:, n, :], rhs=negm.t[:, qs], start=False, stop=True, reads=[esel, negm], sig=True)
                        if sp < s:
                            jj = s - sp
                            kb.op(A, lambda e: e.activation(out=P.t[:, 0:128], in_=pss.t[:, 0:128], func=AF.Exp, bias=albias.t[:, h, jj:jj + 1], scale=SCALE),
                                  reads=[pss, albias], writes=[P])
                            pv_mm('B', s, kt, P, P.t[:, 0:128])
                        else:
                            kb.op(V, lambda e: e.scalar_tensor_tensor(out=dtmp.t[:], in0=pss.t[:, 0:128], scalar=SCALE, in1=dmat.t[:, h, :],
                                                                      op0=ALU.mult, op1=ALU.add), reads=[pss, dmat], writes=[dtmp])
                            kb.op(A, lambda e: e.activation(out=P.t[:, 0:128], in_=dtmp.t[:], func=AF.Exp), reads=[dtmp], writes=[P])
                            pv_mm('D', s, kt, P, P.t[:, 0:128])
                    kb.op(V, lambda e: e.tensor_copy(out=num.t[:], in_=pdiag.t[:, 0:129]), reads=[pdiag], writes=[num])
                    if nB[s] > 0:
                        pb = poB[s]; c = colB[s]
                        kb.op(V, lambda e: e.scalar_tensor_tensor(out=num.t[:], in0=pb.t[:, c:c + 129], scalar=fq.t[:, h:h + 1], in1=num.t[:],
                                                                  op0=ALU.mult, op1=ALU.add), reads=[pb, num, fq], writes=[num])
                    if nA > 0:
                        pa = poA[s]; c = colA[s]
                        kb.op(V, lambda e: e.scalar_tensor_tensor(out=num.t[:], in0=pa.t[:, c:c + 129], scalar=fqa.t[:, s, h:h + 1], in1=num.t[:],
                                                                  op0=ALU.mult, op1=ALU.add), reads=[pa, num, fqa], writes=[num])
                    kb.op(V, lambda e: e.reciprocal(out=rec.t[:], in_=num.t[:, 128:129]), reads=[num], writes=[rec])
                    kb.op(V, lambda e: e.tensor_scalar(out=otok.t[:, s, :], in0=num.t[:, 0:128], scalar1=rec.t[:, 0:1], scalar2=None, op0=ALU.mult),
                          reads=[num, rec], writes=[otok])
                pt = PS[7]; ptv = psb(7)
                for s in range(4):
                    kb.mm(pt, ptv[:, s * 128:(s + 1) * 128], otok.t[:, s, :], ident_b.t[:], start=True, stop=True, reads=[otok, ident_b], sig=(s == 3), transpose=True)
                kb.op(A, lambda e: e.activation(out=bigT.t[:, h, :], in_=ptv[:, 0:512], func=AF.Copy), reads=[pt], writes=[bigT])
            kb.barrier()
        linT("w_o", D, blocks_dm, lambda kc: bigT.t[:, kc, :], [bigT], resid_epi)
        layer_norm(2)
        if STOP == "hmid1":
            break
        kb.barrier()
        with ExitStack() as pes:
            ffn(1, pes)
            layer_norm(3)
            kb.barrier()
        with ExitStack() as pes:
            ost = [kb.sb("o_st%d" % i, [128, D], F32, pes) for i in range(2)]
            for s in range(4):
                o = ost[s % 2]
                for b4 in range(4):
                    ps = PS[4 + (b4 % 2)]
                    for j in range(4):
                        kc = b4 * 4 + j
                        kb.mm(ps, ps.t[:, j * 128:(j + 1) * 128], hT_f.t[:, kc, s * 128:(s + 1) * 128], ident_f.t[:], start=True, stop=True,
                              reads=[hT_f, ident_f], sig=(j == 3), transpose=True)
                    kb.op(A if b4 % 2 else V, (lambda e: e.activation(out=o.t[:, b4 * 512:(b4 + 1) * 512], in_=ps.t[:], func=AF.Copy)) if b4 % 2 else
                          (lambda e: e.tensor_copy(out=o.t[:, b4 * 512:(b4 + 1) * 512], in_=ps.t[:])), reads=[ps], writes=[o])
                kb.dma(G, out_d[tok0 + s * 128: tok0 + (s + 1) * 128, :], o.t[:], o, reads=[o])
            kb.barrier()

    if STOP is not None:
        with ExitStack() as pes:
            ost = [kb.sb("dbg_st%d" % i, [128, D], F32, pes) for i in range(2)]
            for s in range(4):
                o = ost[s % 2]
                for b4 in range(4):
                    ps = PS[4 + (b4 % 2)]
                    for j in range(4):
                        kc = b4 * 4 + j
                        kb.mm(ps, ps.t[:, j * 128:(j + 1) * 128], hT_f.t[:, kc, s * 128:(s + 1) * 128], ident_f.t[:], start=True, stop=True,
                              reads=[hT_f, ident_f], sig=(j == 3), transpose=True)
                    kb.op(V, lambda e: e.tensor_copy(out=o.t[:, b4 * 512:(b4 + 1) * 512], in_=ps.t[:]), reads=[ps], writes=[o])
                kb.dma(G, out_d[tok0 + s * 128: tok0 + (s + 1) * 128, :], o.t[:], o, reads=[o])
    kb.finish()
    es.close()
    return nc


def host_consts(inp):
    f = np.float32
    c = {}

    def percol(v, nch):
        return np.ascontiguousarray(v.reshape(nch, 128).T).astype(f)
    mcw = inp["mamba_conv_w"][0]
    c["mcw"] = np.ascontiguousarray(mcw.reshape(4, 48, 128).transpose(2, 1, 0)).reshape(128, 48 * 4).astype(f)
    c["mcb"] = percol(inp["mamba_conv_b"][0], 48)
    fw = inp["ffn_conv_w"]
    c["fcw"] = np.ascontiguousarray(fw.reshape(2, 3, 88, 128).transpose(3, 0, 2, 1)).reshape(128, 2 * 88 * 3).astype(f)
    c["fcb"] = np.ascontiguousarray(inp["ffn_conv_b"].reshape(2, 88, 128).transpose(2, 0, 1)).reshape(128, 2 * 88).astype(f)
    g = np.stack([inp["ln_mix_g"][0], inp["ln_ffn_g"][0], inp["ln_mix_g"][1], inp["ln_ffn_g"][1]])
    b = np.stack([inp["ln_mix_b"][0], inp["ln_ffn_b"][0], inp["ln_mix_b"][1], inp["ln_ffn_b"][1]])
    c["lng"] = np.ascontiguousarray(g.reshape(4, 16, 128).transpose(2, 0, 1)).reshape(128, 64).astype(f)
    c["lnb"] = np.ascontiguousarray(b.reshape(4, 16, 128).transpose(2, 0, 1)).reshape(128, 64).astype(f)
    c["dtb"] = np.ascontiguousarray(inp["mamba_dt_bias"][0].reshape(64, 1)).astype(f)
    return c


def kernel(**inputs):
    return _run(inputs)


_CACHE = {}


def _run(inputs, NT=8, STOP=None, NCORES=8, TRACE=False):
    inp = {k: np.asarray(v) for k, v in inputs.items()}
    f = np.float32
    c = host_consts(inp)
    c["drow"] = np.ascontiguousarray(np.broadcast_to(inp["mamba_d"][0][None, :], (128, 64))).astype(f)
    c["nrmw"] = np.ascontiguousarray(np.broadcast_to(inp["mamba_norm_w"][0][None, :], (128, DI))).astype(f)
    alog = np.ascontiguousarray(np.broadcast_to(inp["mamba_a_log"][0][None, :], (128, 64))).astype(f)
    slopes = (2.0 ** (-8.0 * np.arange(1, HEADS + 1, dtype=np.float64) / HEADS))
    kk = np.arange(128, dtype=np.float64)
    alb = np.zeros((128, HEADS, 36), f)
    for j in range(1, 36):
        alb[:, :, j] = (-(slopes[None, :]) * (128.0 * j - kk[:, None])).astype(f)
    dm = np.zeros((128, HEADS, 128), f)
    qq = np.arange(128, dtype=np.float64)
    dist = qq[None, :] - kk[:, None]
    for h in range(HEADS):
        dm[:, h, :] = np.where(dist >= 0, -slopes[h] * dist, -1.0e4).astype(f)
    fqt = np.exp(-slopes[None, :] * qq[:, None]).astype(f)
    fqa_t = np.stack([np.exp(-slopes[None, :] * (128.0 * s_ + qq[:, None])) for s_ in range(4)], axis=1).astype(f)
    vb = np.zeros((8, 4, 16), f)
    for it in range(8):
        for s in range(4):
            own = 2 * it + s // 2
            vb[it, s, :] = np.where(np.arange(16) < own, 0.0, NEG)
    vbias = np.ascontiguousarray(np.broadcast_to(vb.reshape(1, 8 * 64), (128, 512))).astype(f)
    es_ = np.zeros((16, 16, 128), f)
    for n in range(16):
        es_[n, n, :] = 1.0
    key = (NT, STOP)
    if key not in _CACHE:
        _CACHE[key] = build(NT, STOP)
    nc = _CACHE[key]
    common = dict(
        w_in=inp["mamba_w_in"][0], w_out=inp["mamba_w_out"][0], up0=inp["ffn_w_up"][0], dn0=inp["ffn_w_down"][0],
        w_kv=inp["w_kv"], w_q=inp["attn_w_q"][0], w_o=inp["attn_w_o"][0], up1=inp["ffn_w_up"][1], dn1=inp["ffn_w_down"][1],
        mcw=c["mcw"], mcb=c["mcb"], fcw=c["fcw"], fcb=c["fcb"], lng=c["lng"], lnb=c["lnb"], dtb=c["dtb"],
        arow=alog, drow=c["drow"], nrmw=c["nrmw"], albias=alb.reshape(128, -1), dmat=dm.reshape(128, -1), fq=fqt, fqa=fqa_t.reshape(128, -1),
        vbias=vbias, esel=es_.reshape(16, -1))
    common = {k: np.ascontiguousarray(v, dtype=f) for k, v in common.items()}
    in_maps = []
    for core in range(NCORES):
        m = dict(common)
        m["x"] = np.ascontiguousarray(inp["x"][core // 2], dtype=f)
        in_maps.append(m)
    res = run_bass_kernel_spmd(nc, in_maps, core_ids=list(range(NCORES)), **({"trace": True} if TRACE else {}))
    if NCORES < 8:
        _CACHE["last_res"] = res
        return res.results[0]["out"]
    out = np.stack([res.results[2 * b]["out"] for b in range(4)], axis=0)
    return out.astype(np.float32)
```

```python
import numpy as np
from contextlib import ExitStack
import concourse.bass as bass
import concourse.mybir as mybir
from concourse.bass_utils import run_bass_kernel_spmd

F32, BF16 = mybir.dt.float32, mybir.dt.bfloat16
AF = mybir.ActivationFunctionType
ALU = mybir.AluOpType
AX = mybir.AxisListType

D = 2048
S = 4096
TT = 512
KC = D // 128
DI = 4096
NH = 64
NG = 8
DFF = 5632
FC = DFF // 128
INP = 10304
HEADS = 16
ALPHA = 4.0 ** 0.25
EPS = 1e-5
SCALE = 128 ** -0.5
NEG = -30000.0


class Buf:
    __slots__ = ("name", "t", "w", "r", "ds", "ps")

    def __init__(self, name, t=None, ps=False):
        self.name, self.t, self.w, self.r, self.ds, self.ps = name, t, [], [], None, ps


def _key(ev):
    return (ev[0], ev[1])


class KB:
    def __init__(self, nc, es):
        self.nc, self.es = nc, es
        self.E = dict(pe=nc.tensor, act=nc.scalar, dve=nc.vector, pool=nc.gpsimd, sp=nc.sync)
        self.esem = {k: es.enter_context(nc.semaphore("e_" + k)) for k in self.E}
        self.cnt = {k: 0 for k in self.E}
        self.seen = {k: {} for k in self.E}
        self.dsems, self.dcnt = [], []
        self.skip_ds = set()
        self.ds_by_name = {}
        self.uid = 0

    def sb(self, name, shape, dt, es=None):
        self.uid += 1
        tname = name if es is None else "%s_u%d" % (name, self.uid)
        t = (es or self.es).enter_context(self.nc.sbuf_tensor(tname, shape, dt))
        return Buf(name, t)

    def _wait(self, eng, evs):
        for ev in evs:
            if ev[0] == 'e':
                if eng == 'pe' and ev[1] == 'pe':
                    continue
                key, val, sem = ev[1], ev[2], self.esem[ev[1]]
            else:
                key, val, sem = ('d', ev[1]), self.dcnt[ev[1]], self.dsems[ev[1]]
            if self.seen[eng].get(key, 0) < val:
                self.E[eng].wait_ge(sem, val)
                self.seen[eng][key] = val

    @staticmethod
    def _merge(lst, ev):
        k = _key(ev)
        return [e for e in lst if _key(e) != k] + [ev]

    def _deps(self, reads, writes):
        evs = []
        for b in reads:
            evs += b.w
            if b.ps:
                evs += b.r
        for b in writes:
            evs += b.w
            evs += b.r
        return evs

    def _commit(self, ev, reads, writes):
        for b in reads:
            b.r = self._merge(b.r, ev)
        for b in writes:
            b.w = self._merge(b.w, ev)
            b.r = []

    def op(self, eng, fn, reads=(), writes=()):
        self._wait(eng, self._deps(reads, writes))
        ins = fn(self.E[eng])
        self.cnt[eng] += 1
        ins.then_inc(self.esem[eng], 1)
        ev = ('e', eng, self.cnt[eng])
        self._commit(ev, reads, writes)
        return ev

    def mm(self, ps, out, lhsT, rhs, start, stop, reads=(), sig=None, transpose=False):
        if sig is None:
            sig = stop
        evs = self._deps(reads, ())
        if start:
            evs += ps.w + ps.r
        self._wait('pe', evs)
        if transpose:
            ins = self.nc.tensor.transpose(out, lhsT, rhs)
        else:
            ins = self.nc.tensor.matmul(out, lhsT=lhsT, rhs=rhs, start=start, stop=stop)
        ev = ('e', 'pe', self.cnt['pe'] + 1)
        if sig:
            self.cnt['pe'] += 1
            ins.then_inc(self.esem['pe'], 1)
        for b in reads:
            b.r = self._merge(b.r, ev)
        if stop:
            ps.w = self._merge(ps.w, ev)
            ps.r = []
        return ev

    def dma(self, q, out, in_, sem_buf, reads=(), writes=()):
        self._wait(q, self._deps(reads, writes))
        if sem_buf.ds is None:
            if sem_buf.name in self.ds_by_name:
                sem_buf.ds = self.ds_by_name[sem_buf.name]
            else:
                sem_buf.ds = len(self.dsems)
                self.ds_by_name[sem_buf.name] = sem_buf.ds
                self.dsems.append(self.es.enter_context(self.nc.semaphore("d_%d" % sem_buf.ds)))
                self.dcnt.append(0)
        i = sem_buf.ds
        ins = self.E[q].dma_start(out=out, in_=in_)
        self.dcnt[i] += 16
        ins.then_inc(self.dsems[i], 16)
        ev = ('d', i, self.dcnt[i])
        self._commit(ev, reads, writes)
        return ev

    def barrier(self, engs=('pe', 'act', 'dve', 'pool')):
        evs = [('e', k, self.cnt[k]) for k in self.E if self.cnt[k] > 0]
        for nm, i in self.ds_by_name.items():
            if nm.startswith("wb"):
                self.skip_ds.add(i)
        evs += [('d', i, 0) for i in range(len(self.dsems)) if i not in self.skip_ds]
        for e in engs:
            self._wait(e, evs)

    def finish(self):
        evs = [('e', k, self.cnt[k]) for k in self.E if self.cnt[k] > 0]
        evs += [('d', i, 0) for i in range(len(self.dsems))]
        self._wait('sp', evs)


def build(NT=8, STOP=None):
    nc = bass.Bass("TRN2", target_bir_lowering=False)
    es = ExitStack()
    kb = KB(nc, es)

    def din(name, shape, dt=F32):
        return nc.dram_tensor(name, list(shape), dt, kind="ExternalInput").ap()

    def dscr(name, shape, dt):
        return nc.dram_tensor(name, list(shape), dt, kind="Internal").ap()

    x_d = din("x", [S, D])
    W = {}
    wshapes = dict(w_in=[D, INP], w_out=[DI, D], up0=[D, 2 * DFF], dn0=[DFF, D], w_kv=[D, 2 * D],
                   w_q=[D, D], w_o=[D, D], up1=[D, 2 * DFF], dn1=[DFF, D])
    for n, shp in wshapes.items():
        W[n] = (din(n, shp), dscr(n + "_b", shp, BF16), Buf("wd_" + n))
    mcw_d = din("mcw", [128, 48 * 4]); mcb_d = din("mcb", [128, 48])
    fcw_d = din("fcw", [128, 2 * 88 * 3]); fcb_d = din("fcb", [128, 2 * 88])
    lng_d = din("lng", [128, 4 * 16]); lnb_d = din("lnb", [128, 4 * 16])
    dtb_d = din("dtb", [64, 1]); arow_d = din("arow", [128, 64]); drow_d = din("drow", [128, 64])
    nrmw_d = din("nrmw", [128, DI])
    albias_d = din("albias", [128, HEADS * 36]); dmat_d = din("dmat", [128, HEADS * 128]); fq_d = din("fq", [128, HEADS]); fqa_d = din("fqa", [128, 4 * HEADS])
    vbias_d = din("vbias", [128, 8 * 64]); esel_d = din("esel", [16, 16 * 128])
    out_d = nc.dram_tensor("out", [S, D], F32, kind="ExternalOutput").ap()
    KT_d = dscr("KT", [HEADS, 128, S], BF16)
    V_d = dscr("V", [HEADS, S // 128, 128, 129], BF16)

    ident_f = kb.sb("ident_f", [128, 128], F32); ident_b = kb.sb("ident_b", [128, 128], BF16)
    tri_f = kb.sb("tri_f", [128, 128], F32); mgt_f = kb.sb("mgt_f", [128, 128], F32); ones_f = kb.sb("ones_f", [128, 128], F32)
    hT_f = kb.sb("hT_f", [128, KC, TT], F32); hT_b = kb.sb("hT_b", [128, KC, TT], BF16)
    NWB = 3
    wb = [kb.sb("wb%d" % i, [128, 16, 256], BF16) for i in range(NWB)]
    S_f = kb.sb("S_f", [128, NG, 512], F32)
    uhalo = kb.sb("uhalo", [128, 48, 3], F32); fhalo = kb.sb("fhalo", [128, 2, 88, 2], F32)
    mcw = kb.sb("mcw_s", [128, 48, 4], F32); mcb = kb.sb("mcb_s", [128, 48], F32)
    fcw = kb.sb("fcw_s", [128, 2, 88, 3], F32); fcb = kb.sb("fcb_s", [128, 2, 88], F32)
    lng = kb.sb("lng_s", [128, 4, 16], F32); lnb = kb.sb("lnb_s", [128, 4, 16], F32)
    dtb = kb.sb("dtb_s", [64, 1], F32); arow = kb.sb("arow_s", [128, 64], F32); drow = kb.sb("drow_s", [128, 64], F32)
    albias = kb.sb("albias_s", [128, HEADS, 36], F32)
    fqa = kb.sb("fqa_s", [128, 4, HEADS], F32)
    fq = kb.sb("fq_s", [128, HEADS], F32); vbias = kb.sb("vbias_s", [128, 8, 64], F32)
    esel = kb.sb("esel_s", [16, 16, 128], BF16)
    kmT = kb.sb("kmT", [128, HEADS, 16], BF16)
    bigT = kb.sb("bigT", [128, 32, TT], BF16)
    mean_s = kb.sb("mean_s", [128, TT], F32); rstd_s = kb.sb("rstd_s", [128, TT], F32)
    sq_s = [kb.sb("sq0", [128, TT], F32)] * 2
    lnt = [kb.sb("lnt%d" % i, [128, TT], F32) for i in range(2)]
    PS = []
    for i in range(8):
        t = es.enter_context(nc.psum_tensor("ps%d" % i, [128, 512], F32))
        PS.append(Buf("ps%d" % i, t, ps=True))

    def psb(i):
        return PS[i].t[:].bitcast(BF16)

    V, A, G = 'dve', 'act', 'pool'

    kb.op(G, lambda e: e.memset(ident_f.t[:], 1.0), writes=[ident_f])
    kb.op(G, lambda e: e.affine_select(out=ident_f.t[:], in_=ident_f.t[:], pattern=[[-1, 128]], compare_op=ALU.is_equal,
                                      fill=0.0, base=0, channel_multiplier=1), reads=[ident_f], writes=[ident_f])
    kb.op(V, lambda e: e.tensor_copy(out=ident_b.t[:], in_=ident_f.t[:]), reads=[ident_f], writes=[ident_b])
    kb.op(G, lambda e: e.memset(tri_f.t[:], 1.0), writes=[tri_f])
    kb.op(G, lambda e: e.affine_select(out=tri_f.t[:], in_=tri_f.t[:], pattern=[[1, 128]], compare_op=ALU.is_ge,
                                      fill=0.0, base=0, channel_multiplier=-1), reads=[tri_f], writes=[tri_f])
    kb.op(G, lambda e: e.memset(mgt_f.t[:], 1.0), writes=[mgt_f])
    kb.op(G, lambda e: e.affine_select(out=mgt_f.t[:], in_=mgt_f.t[:], pattern=[[-1, 128]], compare_op=ALU.is_gt,
                                      fill=0.0, base=0, channel_multiplier=1), reads=[mgt_f], writes=[mgt_f])
    kb.op(G, lambda e: e.memset(ones_f.t[:], 1.0), writes=[ones_f])
    kb.op(G, lambda e: e.memset(S_f.t[:], 0.0), writes=[S_f])
    kb.op(G, lambda e: e.memset(uhalo.t[:], 0.0), writes=[uhalo])
    kb.op(G, lambda e: e.memset(fhalo.t[:], 0.0), writes=[fhalo])

    if STOP == "c0":
        kb.finish(); es.close(); return nc

    def ld(buf, src, view=None):
        kb.dma(G, buf.t[:] if view is None else view, src, buf, writes=[buf])

    ld(mcw, mcw_d.rearrange("p (c k) -> p c k", k=4)); ld(mcb, mcb_d)
    ld(fcw, fcw_d.rearrange("p (l c k) -> p l c k", l=2, k=3)); ld(fcb, fcb_d.rearrange("p (l c) -> p l c", l=2))
    ld(lng, lng_d.rearrange("p (l c) -> p l c", l=4)); ld(lnb, lnb_d.rearrange("p (l c) -> p l c", l=4))
    ld(dtb, dtb_d); ld(arow, arow_d); ld(drow, drow_d)
    kb.op(A, lambda e: e.activation(out=arow.t[:], in_=arow.t[:], func=AF.Exp), reads=[arow], writes=[arow])
    kb.op(V, lambda e: e.tensor_scalar(out=arow.t[:], in0=arow.t[:], scalar1=-1.0, scalar2=None, op0=ALU.mult), reads=[arow], writes=[arow])
    kb.op(G, lambda e: e.memset(kmT.t[:], 0.0), writes=[kmT])
    ld(albias, albias_d.rearrange("p (h j) -> p h j", h=HEADS))
    ld(fqa, fqa_d.rearrange("p (s h) -> p s h", s=4)); ld(fq, fq_d); ld(vbias, vbias_d.rearrange("p (i c) -> p i c", i=8))
    ld(esel, esel_d.rearrange("p (n k) -> p n k", n=16))

    if STOP == "c1":
        kb.finish(); es.close(); return nc
    def convert(names):
        for n in names:
            src, dst, wbuf = W[n]
            K = src.shape[0]
            rb = 256
            for r0 in range(0, K, rb):
                kb.dma(G, dst[r0:r0 + rb, :], src[r0:r0 + rb, :], wbuf, writes=[wbuf])
            kb.skip_ds.add(wbuf.ds)
    if STOP not in ("x0", "x1", "x2"):
        convert(["w_in"])

    wctr = [0]

    def linT(wname, K, blocks, rhs_fn, rhs_bufs, epi, ps_banks=(0, 1, 2, 3)):
        _, wdst, wbuf = W[wname]
        nkb = (K + 2047) // 2048
        pending = []
        for bi, segs in enumerate(blocks):
            ncols = sum(s[1] for s in segs)
            nj = ncols // 128 if ncols >= 128 else 1
            banks = [PS[ps_banks[(2 * (bi % 2) + j) % len(ps_banks)]] for j in range(nj)]
            for kbi in range(nkb):
                k0 = kbi * 2048
                kcs = min(16, (K - k0) // 128)
                wt = wb[wctr[0] % NWB]
                wctr[0] += 1
                c = 0
                for (c0, n) in segs:
                    kb.dma('sp', wt.t[:, 0:kcs, c:c + n],
                           wdst[k0:k0 + kcs * 128, c0:c0 + n].rearrange("(kc p) n -> p kc n", p=128),
                           wt, reads=[wbuf], writes=[wt])
                    c += n
                for j in range(nj):
                    m = min(128, ncols)
                    for kc in range(kcs):
                        first = (kbi == 0 and kc == 0)
                        last = (kbi == nkb - 1 and kc == kcs - 1)
                        kb.mm(banks[j], banks[j].t[0:m, :], lhsT=wt.t[:, kc, j * 128:j * 128 + m], rhs=rhs_fn(k0 // 128 + kc),
                              start=first, stop=last, reads=[wt] + list(rhs_bufs), sig=(kc == kcs - 1))
            for f in pending:
                f()
            pending = [(lambda bi=bi, j=j, b=banks[j]: epi(bi, j, b)) for j in range(nj)]
        for f in pending:
            f()

    def layer_norm(li):
        ps_sum, ps_sq = PS[6], PS[7]
        for j in range(KC):
            sq = sq_s[j % 2]
            kb.op(A, lambda e: e.activation(out=sq.t[:], in_=hT_f.t[:, j, :], func=AF.Square), reads=[hT_f], writes=[sq])
            kb.mm(ps_sum, ps_sum.t[:], lhsT=ones_f.t[:], rhs=hT_f.t[:, j, :], start=(j == 0), stop=(j == KC - 1), reads=[ones_f, hT_f])
            kb.mm(ps_sq, ps_sq.t[:], lhsT=ones_f.t[:], rhs=sq.t[:], start=(j == 0), stop=(j == KC - 1), reads=[ones_f, sq], sig=True)
        kb.op(V, lambda e: e.tensor_scalar(out=mean_s.t[:], in0=ps_sum.t[:], scalar1=1.0 / D, scalar2=None, op0=ALU.mult),
              reads=[ps_sum], writes=[mean_s])
        t0 = lnt[0]
        kb.op(V, lambda e: e.tensor_tensor(out=t0.t[:], in0=mean_s.t[:], in1=mean_s.t[:], op=ALU.mult), reads=[mean_s], writes=[t0])
        kb.op(V, lambda e: e.scalar_tensor_tensor(out=rstd_s.t[:], in0=ps_sq.t[:], scalar=1.0 / D, in1=t0.t[:],
                                                  op0=ALU.mult, op1=ALU.subtract), reads=[ps_sq, t0], writes=[rstd_s])
        kb.op(V, lambda e: e.tensor_scalar(out=rstd_s.t[:], in0=rstd_s.t[:], scalar1=EPS, scalar2=None, op0=ALU.add),
              reads=[rstd_s], writes=[rstd_s])
        kb.op(A, lambda e: e.activation(out=rstd_s.t[:], in_=rstd_s.t[:], func=AF.Sqrt), reads=[rstd_s], writes=[rstd_s])
        kb.op(V, lambda e: e.reciprocal(out=rstd_s.t[:], in_=rstd_s.t[:]), reads=[rstd_s], writes=[rstd_s])
        for j in range(KC):
            t = lnt[j % 2]
            kb.op(V, lambda e: e.tensor_tensor(out=t.t[:], in0=hT_f.t[:, j, :], in1=mean_s.t[:], op=ALU.subtract),
                  reads=[hT_f, mean_s], writes=[t])
            kb.op(V, lambda e: e.tensor_tensor(out=t.t[:], in0=t.t[:], in1=rstd_s.t[:], op=ALU.mult), reads=[t, rstd_s], writes=[t])
            kb.op(V, lambda e: e.tensor_scalar(out=hT_f.t[:, j, :], in0=t.t[:], scalar1=lng.t[:, li, j:j + 1], scalar2=lnb.t[:, li, j:j + 1],
                                               op0=ALU.mult, op1=ALU.add), reads=[t, lng, lnb], writes=[hT_f])
            kb.op(A, lambda e: e.activation(out=hT_b.t[:, j, :], in_=hT_f.t[:, j, :], func=AF.Copy), reads=[hT_f], writes=[hT_b])

    def resid_epi(bi, j, ps):
        ch = bi * 2 + j
        kb.op(V, lambda e: e.scalar_tensor_tensor(out=hT_f.t[:, ch, :], in0=hT_f.t[:, ch, :], scalar=ALPHA, in1=ps.t[:],
                                                  op0=ALU.mult, op1=ALU.add), reads=[hT_f, ps], writes=[hT_f])

    blocks_dm = [[(c0, 256)] for c0 in range(0, D, 256)]

    def ffn(layer, pes):
        act = kb.sb("ffn_act", [128, FC, TT], BF16, pes)
        fu = [kb.sb("ffn_u%d" % i, [128, 2 + TT], F32, pes) for i in range(2)]
        facc = [kb.sb("ffn_acc%d" % i, [128, TT], F32, pes) for i in range(2)]
        gact = [kb.sb("ffn_g%d" % i, [128, TT], BF16, pes) for i in range(2)]
        wn = "up%d" % layer
        blocks = []
        for c in range(0, FC, 2):
            blocks.append([(c * 128, 256)])
            blocks.append([(DFF + c * 128, 256)])
        ctr = [0]

        def epi(bi, j, ps):
            isval = bi % 2
            c = (bi // 2) * 2 + j
            ch = isval * FC + c
            u = fu[ctr[0] % 2]; acc = facc[ctr[0] % 2]; ctr[0] += 1
            kb.op(V, lambda e: e.tensor_copy(out=u.t[:, 0:2], in_=fhalo.t[:, layer, ch, :]), reads=[fhalo], writes=[u])
            kb.op(A, lambda e: e.activation(out=u.t[:, 2:2 + TT], in_=ps.t[:], func=AF.Copy), reads=[ps], writes=[u])
            kb.op(V, lambda e: e.tensor_copy(out=fhalo.t[:, layer, ch, :], in_=u.t[:, TT:TT + 2]), reads=[u], writes=[fhalo])
            kb.op(V, lambda e: e.tensor_scalar(out=acc.t[:], in0=u.t[:, 0:TT], scalar1=fcw.t[:, layer, ch, 0:1], scalar2=None, op0=ALU.mult),
                  reads=[u, fcw], writes=[acc])
            for k in (1, 2):
                kb.op(V, lambda e: e.scalar_tensor_tensor(out=acc.t[:], in0=u.t[:, k:k + TT], scalar=fcw.t[:, layer, ch, k:k + 1], in1=acc.t[:],
                                                          op0=ALU.mult, op1=ALU.add), reads=[u, fcw, acc], writes=[acc])
            if not isval:
                g = gact[j]
                kb.op(A, lambda e: e.activation(out=g.t[:], in_=acc.t[:], func=AF.Silu, bias=fcb.t[:, layer, ch:ch + 1], scale=1.0),
                      reads=[acc, fcb], writes=[g])
            else:
                g = gact[j]
                kb.op(V, lambda e: e.scalar_tensor_tensor(out=act.t[:, c, :], in0=acc.t[:], scalar=fcb.t[:, layer, ch:ch + 1], in1=g.t[:],
                                                          op0=ALU.add, op1=ALU.mult), reads=[acc, fcb, g], writes=[act])

        linT(wn, D, blocks, lambda kc: hT_b.t[:, kc, :], [hT_b], epi)
        linT("dn%d" % layer, DFF, blocks_dm, lambda kc: act.t[:, kc, :], [act], resid_epi)

    for it in range(NT):
        tok0 = it * TT
        with ExitStack() as pes:
            xin = [kb.sb("xin%d" % i, [128, D], F32, pes) for i in range(2)]
            for s in range(4):
                xt = xin[s % 2]
                kb.dma(G, xt.t[:], x_d[tok0 + s * 128: tok0 + (s + 1) * 128, :], xt, writes=[xt])
                for b4 in range(4 if STOP != "x2" else 0):
                    ps = PS[4 + (b4 % 2)]
                    for j in range(4):
                        kc = b4 * 4 + j
                        kb.mm(ps, ps.t[:, j * 128:(j + 1) * 128], xt.t[:, kc * 128:(kc + 1) * 128], ident_f.t[:], start=True, stop=True,
                              reads=[xt, ident_f], sig=(j == 3), transpose=True)
                    src = ps.t[:].rearrange("p (j t) -> p j t", j=4)
                    kb.op(A, lambda e: e.activation(out=hT_f.t[:, b4 * 4:(b4 + 1) * 4, s * 128:(s + 1) * 128], in_=src, func=AF.Copy),
                          reads=[ps], writes=[hT_f])
                    kb.op(V, lambda e: e.tensor_copy(out=hT_b.t[:, b4 * 4:(b4 + 1) * 4, s * 128:(s + 1) * 128], in_=src),
                          reads=[ps], writes=[hT_b])
            kb.barrier()
        if STOP in ("x1", "x2"):
            kb.finish(); es.close(); return nc
        if STOP in ("x", "x0"):
            break
        if it == 0:
            convert(["w_out", "up0", "dn0", "w_kv", "w_q", "w_o", "up1", "dn1"])

        with ExitStack() as pes:
            u_s = [kb.sb("m_u%d" % i, [128, 3 + TT], F32, pes) for i in range(2)]
            acc_s = [kb.sb("m_acc%d" % i, [128, TT], F32, pes) for i in range(2)]
            sT = [kb.sb("m_sT%d" % i, [128, TT], BF16, pes) for i in range(2)]
            xg = kb.sb("m_xg", [128, 4, 512], BF16, pes); zg = kb.sb("m_zg", [128, 4, 512], BF16, pes)
            Btok = kb.sb("m_Btok", [128, 4, 128], BF16, pes)
            BT = kb.sb("m_BT", [128, TT], BF16, pes); CT = kb.sb("m_CT", [128, TT], BF16, pes)
            dtT = kb.sb("m_dtT", [64, TT], F32, pes)
            dt_t = kb.sb("m_dt", [128, 4, 64], F32, pes); da_t = kb.sb("m_da", [128, 4, 64], F32, pes)
            E_t = kb.sb("m_E", [128, 4, 64], F32, pes); dtx_t = kb.sb("m_dtx", [128, 4, 64], F32, pes)
            Dt_t = kb.sb("m_Dt", [128, 4, 64], F32, pes)
            S_b = kb.sb("m_Sb", [128, 512], BF16, pes)
            sr = [kb.sb("m_sr%d" % i, [128, 8, 128], F32, pes) for i in range(2)]
            Lg = [kb.sb("m_Lg%d" % i, [128, 1024], BF16, pes) for i in range(4)]
            CBm = kb.sb("m_CBm", [128, 128], F32, pes)
            MT = [kb.sb("m_MT%d" % i, [128, 512], BF16, pes) for i in range(2)]
            xdt = kb.sb("m_xdt", [128, 512], BF16, pes); xdte = kb.sb("m_xdte", [128, 512], BF16, pes)
            y1 = kb.sb("m_y1", [128, 512], F32, pes); y2 = kb.sb("m_y2", [128, 512], F32, pes)
            junk = kb.sb("m_junk", [128, 512], F32, pes)
            ss = kb.sb("m_ss", [128, 1], F32, pes); rs = kb.sb("m_rs", [128, 1], F32, pes)
            yn = kb.sb("m_yn", [128, 512], BF16, pes)
            nrm_g = [kb.sb("m_nrm0", [128, 512], F32, pes)] * 2

            def dt_epi(bi, j, ps):
                kb.op(A, lambda e: e.activation(out=dtT.t[:], in_=ps.t[0:64, :], func=AF.Exp, bias=dtb.t[:, 0:1], scale=1.0),
                      reads=[ps, dtb], writes=[dtT])
                kb.op(A, lambda e: e.activation(out=dtT.t[:], in_=dtT.t[:], func=AF.Ln, bias=1.0, scale=1.0), reads=[dtT], writes=[dtT])
            linT("w_in", D, [[(10240, 64)]], lambda kc: hT_b.t[:, kc, :], [hT_b], dt_epi)
            psd = PS[4]
            for s in range(4):
                kb.mm(psd, psd.t[:, s * 64:(s + 1) * 64], dtT.t[:, s * 128:(s + 1) * 128], ident_f.t[0:64, 0:64], start=True, stop=True,
                      reads=[dtT, ident_f], sig=(s == 3), transpose=True)
            kb.op(V, lambda e: e.tensor_copy(out=dt_t.t[:], in_=psd.t[:, 0:256].rearrange("p (s h) -> p s h", s=4)), reads=[psd], writes=[dt_t])
            kb.op(V, lambda e: e.tensor_tensor(out=da_t.t[:], in0=dt_t.t[:], in1=arow.t[:].unsqueeze(1).to_broadcast([128, 4, 64]), op=ALU.mult),
                  reads=[dt_t, arow], writes=[da_t])
            pc, ptot = PS[5], PS[6]
            for s in range(4):
                kb.mm(pc, pc.t[:, s * 64:(s + 1) * 64], lhsT=tri_f.t[:], rhs=da_t.t[:, s, :], start=True, stop=True, reads=[tri_f, da_t], sig=(s == 3))
            for s in range(4):
                kb.mm(ptot, ptot.t[:, s * 64:(s + 1) * 64], lhsT=ones_f.t[:], rhs=da_t.t[:, s, :], start=True, stop=True, reads=[ones_f, da_t], sig=(s == 3))
            v4 = lambda b: b.t[:].rearrange("p s h -> p (s h)")
            kb.op(A, lambda e: e.activation(out=v4(E_t), in_=pc.t[:, 0:256], func=AF.Exp), reads=[pc], writes=[E_t])
            kb.op(A, lambda e: e.activation(out=v4(Dt_t), in_=ptot.t[:, 0:256], func=AF.Exp), reads=[ptot], writes=[Dt_t])
            kb.op(V, lambda e: e.tensor_copy(out=v4(dtx_t), in_=pc.t[:, 0:256]), reads=[pc], writes=[dtx_t])
            kb.op(V, lambda e: e.tensor_tensor(out=v4(dtx_t), in0=ptot.t[:, 0:256], in1=v4(dtx_t), op=ALU.subtract),
                  reads=[ptot, dtx_t], writes=[dtx_t])
            kb.op(A, lambda e: e.activation(out=v4(dtx_t), in_=v4(dtx_t), func=AF.Exp), reads=[dtx_t], writes=[dtx_t])
            kb.op(V, lambda e: e.tensor_tensor(out=v4(dtx_t), in0=v4(dtx_t), in1=v4(dt_t), op=ALU.mult), reads=[dtx_t, dt_t], writes=[dtx_t])

            cctr = [0]

            def conv_silu(ps, ch, dst):
                u = u_s[cctr[0] % 2]; acc = acc_s[cctr[0] % 2]; cctr[0] += 1
                kb.op(V, lambda e: e.tensor_copy(out=u.t[:, 0:3], in_=uhalo.t[:, ch, :]), reads=[uhalo], writes=[u])
                kb.op(A, lambda e: e.activation(out=u.t[:, 3:3 + TT], in_=ps.t[:], func=AF.Copy), reads=[ps], writes=[u])
                kb.op(V, lambda e: e.tensor_copy(out=uhalo.t[:, ch, :], in_=u.t[:, TT:TT + 3]), reads=[u], writes=[uhalo])
                kb.op(V, lambda e: e.tensor_scalar(out=acc.t[:], in0=u.t[:, 0:TT], scalar1=mcw.t[:, ch, 0:1], scalar2=None, op0=ALU.mult),
                      reads=[u, mcw], writes=[acc])
                for k in (1, 2, 3):
                    kb.op(V, lambda e: e.scalar_tensor_tensor(out=acc.t[:], in0=u.t[:, k:k + TT], scalar=mcw.t[:, ch, k:k + 1], in1=acc.t[:],
                                                              op0=ALU.mult, op1=ALU.add), reads=[u, mcw, acc], writes=[acc])
                kb.op(A, lambda e: e.activation(out=dst.t[:], in_=acc.t[:], func=AF.Silu, bias=mcb.t[:, ch:ch + 1], scale=1.0),
                      reads=[acc, mcb], writes=[dst])

            tctr = [0]

            def to_tokmajor(src, dst, col0):
                bank = 6 + (tctr[0] % 2); tctr[0] += 1
                ps = PS[bank]
                pv = psb(bank)
                for s in range(4):
                    kb.mm(ps, pv[:, s * 128:(s + 1) * 128], src.t[:, s * 128:(s + 1) * 128], ident_b.t[:], start=True, stop=True,
                          reads=[src, ident_b], sig=(s == 3), transpose=True)
                kb.op(A, lambda e: e.activation(out=dst.t[:, :, col0:col0 + 128], in_=pv[:, 0:512].rearrange("p (s c) -> p s c", s=4), func=AF.Copy),
                      reads=[ps], writes=[dst])

            for g in range(NG):
                nrmw = nrm_g[g % 2]
                kb.dma(G, nrmw.t[:], nrmw_d[:, g * 512:(g + 1) * 512], nrmw, writes=[nrmw])
                hs = slice(g * 8, (g + 1) * 8)
                for s in range(4):
                    srb = sr[s % 2]
                    kb.op(V, lambda e: e.tensor_tensor(out=srb.t[:], in0=tri_f.t[:].unsqueeze(1).to_broadcast([128, 8, 128]),
                                                       in1=da_t.t[:, s, hs].unsqueeze(2).to_broadcast([128, 8, 128]), op=ALU.mult),
                          reads=[tri_f, da_t], writes=[srb])
                    for hh in range(2):
                        pseg = PS[hh]
                        kb.mm(pseg, pseg.t[:], lhsT=mgt_f.t[:], rhs=srb.t[:, hh * 4:(hh + 1) * 4, :].rearrange("p q l -> p (q l)"), start=True, stop=True,
                              reads=[mgt_f, srb])
                        kb.op(A, lambda e: e.activation(out=Lg[s].t[:, hh * 512:(hh + 1) * 512], in_=pseg.t[:], func=AF.Exp), reads=[pseg], writes=[Lg[s]])
                def x_epi(bi, j, ps, g=g):
                    c = bi * 2 + j
                    st = sT[c % 2]
                    conv_silu(ps, g * 4 + c, st)
                    to_tokmajor(st, xg, c * 128)
                linT("w_in", D, [[(DI + g * 512, 256)], [(DI + g * 512 + 256, 256)]], lambda kc: hT_b.t[:, kc, :], [hT_b], x_epi)

                def bc_epi(bi, j, ps, g=g):
                    if j == 0:
                        conv_silu(ps, 32 + g, BT)
                        bank = 6 + (tctr[0] % 2); tctr[0] += 1
                        pst = PS[bank]; pv = psb(bank)
                        for s in range(4):
                            kb.mm(pst, pv[:, s * 128:(s + 1) * 128], BT.t[:, s * 128:(s + 1) * 128], ident_b.t[:], start=True, stop=True,
                                  reads=[BT, ident_b], sig=(s == 3), transpose=True)
                        kb.op(A, lambda e: e.activation(out=Btok.t[:], in_=pv[:, 0:512].rearrange("p (s c) -> p s c", s=4), func=AF.Copy),
                              reads=[pst], writes=[Btok])
                    else:
                        conv_silu(ps, 40 + g, CT)
                linT("w_in", D, [[(2 * DI + g * 128, 128), (2 * DI + 1024 + g * 128, 128)]], lambda kc: hT_b.t[:, kc, :], [hT_b], bc_epi)

                def z_epi(bi, j, ps, g=g):
                    c = bi * 2 + j
                    st = sT[c % 2]
                    kb.op(A, lambda e: e.activation(out=st.t[:], in_=ps.t[:], func=AF.Silu), reads=[ps], writes=[st])
                    to_tokmajor(st, zg, c * 128)
                linT("w_in", D, [[(g * 512, 256)], [(g * 512 + 256, 256)]], lambda kc: hT_b.t[:, kc, :], [hT_b], z_epi)

                kb.op(A, lambda e: e.activation(out=S_b.t[:], in_=S_f.t[:, g, :], func=AF.Copy), reads=[S_f], writes=[S_b])
                hs = slice(g * 8, (g + 1) * 8)
                v3 = lambda ap: ap.rearrange("p (h d) -> p h d", h=8)

                def ssd_early(s):
                    cs = slice(s * 128, (s + 1) * 128)
                    kb.op(G, lambda e: e.tensor_tensor(out=v3(xdt.t[:]), in0=v3(xg.t[:, s, :]),
                                                       in1=dt_t.t[:, s, hs].unsqueeze(2).to_broadcast([128, 8, 64]), op=ALU.mult),
                          reads=[xg, dt_t], writes=[xdt])
                    kb.op(G, lambda e: e.tensor_tensor(out=v3(xdte.t[:]), in0=v3(xg.t[:, s, :]),
                                                       in1=dtx_t.t[:, s, hs].unsqueeze(2).to_broadcast([128, 8, 64]), op=ALU.mult),
                          reads=[xg, dtx_t], writes=[xdte])
                    kb.op(G, lambda e: e.tensor_tensor(out=v3(y2.t[:]), in0=v3(xg.t[:, s, :]), in1=drow.t[:, hs].unsqueeze(2).to_broadcast([128, 8, 64]), op=ALU.mult),
                          reads=[xg, drow], writes=[y2])
                    poff = PS[4]
                    kb.mm(poff, poff.t[:], lhsT=CT.t[:, cs], rhs=S_b.t[:], start=True, stop=True, reads=[CT, S_b])
                    pstt = PS[5]
                    kb.mm(pstt, pstt.t[:], lhsT=Btok.t[:, s, :], rhs=xdte.t[:], start=True, stop=True, reads=[Btok, xdte])
                    pcb = PS[2]
                    kb.mm(pcb, pcb.t[:, 0:128], lhsT=BT.t[:, cs], rhs=CT.t[:, cs], start=True, stop=True, reads=[BT, CT])
                    kb.op(V, lambda e: e.tensor_tensor(out=CBm.t[:], in0=pcb.t[:, 0:128], in1=tri_f.t[:], op=ALU.mult),
                          reads=[pcb, tri_f], writes=[CBm])
                    kb.op(G, lambda e: e.tensor_tensor(out=v3(S_f.t[:, g, :]), in0=v3(S_f.t[:, g, :]), in1=Dt_t.t[:, s, hs].unsqueeze(2).to_broadcast([128, 8, 64]), op=ALU.mult),
                          reads=[S_f, Dt_t], writes=[S_f])
                    kb.op(V, lambda e: e.tensor_tensor(out=S_f.t[:, g, :], in0=S_f.t[:, g, :], in1=pstt.t[:], op=ALU.add), reads=[S_f, pstt], writes=[S_f])
                    kb.op(A, lambda e: e.activation(out=S_b.t[:], in_=S_f.t[:, g, :], func=AF.Copy), reads=[S_f], writes=[S_b])

                def ssd_late(s):
                    py = PS[3]; poff = PS[4]
                    for hh in range(2):
                        L = Lg[s]; M = MT[hh]
                        kb.op(V, lambda e: e.tensor_tensor(out=M.t[:].rearrange("p (q l) -> p q l", q=4), in0=L.t[:, hh * 512:(hh + 1) * 512].rearrange("p (q l) -> p q l", q=4),
                                                           in1=CBm.t[:].unsqueeze(1).to_broadcast([128, 4, 128]), op=ALU.mult),
                              reads=[L, CBm], writes=[M])
                        for q in range(4):
                            hl = hh * 4 + q
                            kb.mm(py, py.t[:, hl * 64:(hl + 1) * 64], lhsT=M.t[:, q * 128:(q + 1) * 128], rhs=xdt.t[:, hl * 64:(hl + 1) * 64],
                                  start=True, stop=True, reads=[M, xdt], sig=(q == 3))
                    kb.op(V, lambda e: e.tensor_tensor(out=v3(y1.t[:]), in0=v3(poff.t[:]), in1=E_t.t[:, s, hs].unsqueeze(2).to_broadcast([128, 8, 64]), op=ALU.mult),
                          reads=[poff, E_t], writes=[y1])
                    kb.op(V, lambda e: e.tensor_tensor(out=y1.t[:], in0=y1.t[:], in1=y2.t[:], op=ALU.add), reads=[y1, y2], writes=[y1])
                    kb.op(V, lambda e: e.tensor_tensor(out=y1.t[:], in0=y1.t[:], in1=py.t[:], op=ALU.add), reads=[y1, py], writes=[y1])
                    kb.op(V, lambda e: e.tensor_tensor(out=y1.t[:], in0=y1.t[:], in1=zg.t[:, s, :], op=ALU.mult), reads=[y1, zg], writes=[y1])
                    kb.op(V, lambda e: e.tensor_tensor(out=junk.t[:], in0=y1.t[:], in1=y1.t[:], op=ALU.mult), reads=[y1], writes=[junk])
                    kb.op(V, lambda e: e.reduce_sum(out=ss.t[:], in_=junk.t[:], axis=AX.X), reads=[junk], writes=[ss])
                    kb.op(V, lambda e: e.tensor_scalar(out=rs.t[:], in0=ss.t[:], scalar1=1.0 / 512, scalar2=EPS, op0=ALU.mult, op1=ALU.add),
                          reads=[ss], writes=[rs])
                    kb.op(A, lambda e: e.activation(out=rs.t[:], in_=rs.t[:], func=AF.Ln), reads=[rs], writes=[rs])
                    kb.op(A, lambda e: e.activation(out=rs.t[:], in_=rs.t[:], func=AF.Exp, scale=-0.5), reads=[rs], writes=[rs])
                    kb.op(V, lambda e: e.scalar_tensor_tensor(out=yn.t[:], in0=y1.t[:], scalar=rs.t[:, 0:1], in1=nrmw.t[:],
                                                              op0=ALU.mult, op1=ALU.mult), reads=[y1, rs, nrmw], writes=[yn])

                def ssd_trans(s):
                    cs = slice(s * 128, (s + 1) * 128)
                    bank = 6 + (tctr[0] % 2); tctr[0] += 1
                    pst = PS[bank]; pv = psb(bank)
                    for q in range(4):
                        kb.mm(pst, pv[:, q * 128:(q + 1) * 128], yn.t[:, q * 128:(q + 1) * 128], ident_b.t[:], start=True, stop=True,
                              reads=[yn, ident_b], sig=(q == 3), transpose=True)
                    kb.op(A, lambda e: e.activation(out=bigT.t[:, g * 4:(g + 1) * 4, cs], in_=pv[:, 0:512].rearrange("p (q t) -> p q t", q=4), func=AF.Copy),
                          reads=[pst], writes=[bigT])

                for s in range(4):
                    ssd_early(s)
                    if s > 0:
                        ssd_trans(s - 1)
                    ssd_late(s)
                ssd_trans(3)
            kb.barrier()

        linT("w_out", DI, blocks_dm, lambda kc: bigT.t[:, kc, :], [bigT], resid_epi)
        layer_norm(0)
        if STOP == "hmid0":
            break
        kb.barrier()
        with ExitStack() as pes:
            ffn(0, pes)
            layer_norm(1)
            kb.barrier()
        if STOP == "h1":
            break

        with ExitStack() as pes:
            kst = [kb.sb("kv_k%d" % i, [128, TT], BF16, pes) for i in range(2)]
            vT = [kb.sb("kv_vT%d" % i, [128, TT], BF16, pes) for i in range(2)]
            vst = [kb.sb("kv_v%d" % i, [128, 4, 129], BF16, pes) for i in range(2)]
            kms = kb.sb("kv_kms", [128, 2], F32, pes)
            for b in vst:
                kb.op(G, lambda e: e.memset(b.t[:], 1.0), writes=[b])

            def kv_epi(bi, j, ps):
                ch = bi * 2 + j
                if ch < 16:
                    h = ch
                    k = kst[h % 2]
                    kb.op(A, lambda e: e.activation(out=k.t[:], in_=ps.t[:], func=AF.Copy), reads=[ps], writes=[k])
                    kb.dma(G, KT_d[h, :, tok0:tok0 + TT], k.t[:], k, reads=[k])
                    kb.op(V, lambda e: e.tensor_reduce(out=kms.t[:], in_=ps.t[:].rearrange("p (b t) -> p b t", b=2), axis=AX.X, op=ALU.add),
                          reads=[ps], writes=[kms])
                    kb.op(V, lambda e: e.tensor_scalar(out=kmT.t[:, h, 2 * it:2 * it + 2], in0=kms.t[:], scalar1=1.0 / 256, scalar2=None, op0=ALU.mult),
                          reads=[kms], writes=[kmT])
                else:
                    h = ch - 16
                    vt = vT[h % 2]; vs = vst[h % 2]
                    kb.op(A, lambda e: e.activation(out=vt.t[:], in_=ps.t[:], func=AF.Copy), reads=[ps], writes=[vt])
                    bank = 6 + (h % 2)
                    pst = PS[bank]; pv = psb(bank)
                    for s in range(4):
                        kb.mm(pst, pv[:, s * 128:(s + 1) * 128], vt.t[:, s * 128:(s + 1) * 128], ident_b.t[:], start=True, stop=True,
                              reads=[vt, ident_b], sig=(s == 3), transpose=True)
                    kb.op(V, lambda e: e.tensor_copy(out=vs.t[:, :, 0:128], in_=pv[:, 0:512].rearrange("p (s c) -> p s c", s=4)), reads=[pst], writes=[vs])
                    kb.dma(G, V_d[h, 4 * it:4 * it + 4, :, :].rearrange("s p c -> p s c"), vs.t[:], vs, reads=[vs])
            linT("w_kv", D, [[(c0, 256)] for c0 in range(0, 2 * D, 256)], lambda kc: hT_b.t[:, kc, :], [hT_b], kv_epi)
            kb.barrier()

        with ExitStack() as pes:
            nk = (it + 1) * TT
            nkt = nk // 128
            qT = kb.sb("a_qT", [128, HEADS, TT], BF16, pes)
            kbuf = [kb.sb("a_k%d" % i, [128, S], BF16, pes) for i in range(1)]
            vbuf = [kb.sb("a_v%d" % i, [128, S // 128, 129], BF16, pes) for i in range(1)]
            gm = kb.sb("a_gm", [128, 4, 16], F32, pes); mx8 = kb.sb("a_mx8", [128, 8], F32, pes)
            mb = kb.sb("a_mb", [128, 4, 16], BF16, pes); mbf = kb.sb("a_mbf", [128, 4, 16], F32, pes)
            negm = kb.sb("a_negm", [16, TT], BF16, pes)
            PT = [kb.sb("a_PT%d" % i, [128, TT], BF16, pes) for i in range(3)]
            dtmp = kb.sb("a_dtmp", [128, 128], F32, pes)
            num = kb.sb("a_num", [128, 129], F32, pes); rec = kb.sb("a_rec", [128, 1], F32, pes)
            otok = kb.sb("a_otok", [128, 4, 128], BF16, pes)
            dmat = kb.sb("a_dmat", [128, HEADS, 128], F32, pes)
            kb.dma(G, dmat.t[:], dmat_d.rearrange("p (h q) -> p h q", h=HEADS), dmat, writes=[dmat])

            def q_epi(bi, j, ps):
                h = bi * 2 + j
                kb.op(A, lambda e: e.activation(out=qT.t[:, h, :], in_=ps.t[:], func=AF.Copy), reads=[ps], writes=[qT])
            linT("w_q", D, blocks_dm, lambda kc: hT_b.t[:, kc, :], [hT_b], q_epi)

            pctr = [0]
            for h in range(HEADS):
                kbf = kbuf[0]; vbf = vbuf[0]
                kb.dma(G, kbf.t[:, 0:nk], KT_d[h, :, 0:nk], kbf, writes=[kbf])
                kb.dma(G, vbf.t[:, 0:nkt, :], V_d[h, 0:nkt, :, :].rearrange("s p c -> p s c"), vbf, writes=[vbf])
                pg = PS[7]
                for s in range(4):
                    kb.mm(pg, pg.t[:, s * 16:(s + 1) * 16], lhsT=qT.t[:, h, s * 128:(s + 1) * 128], rhs=kmT.t[:, h, :], start=True, stop=True,
                          reads=[qT, kmT], sig=(s == 3))
                kb.op(V, lambda e: e.tensor_tensor(out=gm.t[:].rearrange("p s n -> p (s n)"), in0=pg.t[:, 0:64], in1=vbias.t[:, it, :], op=ALU.add),
                      reads=[pg, vbias], writes=[gm])
                for s in range(4):
                    kb.op(V, lambda e: e.max(out=mx8.t[:], in_=gm.t[:, s, :]), reads=[gm], writes=[mx8])
                    kb.op(V, lambda e: e.tensor_scalar(out=mbf.t[:, s, :], in0=gm.t[:, s, :], scalar1=mx8.t[:, 2:3], scalar2=-NEG, op0=ALU.is_ge, op1=ALU.mult),
                          reads=[gm, mx8], writes=[mbf])
                    kb.op(V, lambda e: e.scalar_tensor_tensor(out=mb.t[:, s, :], in0=mbf.t[:, s, :], scalar=NEG, in1=vbias.t[:, it, s * 16:(s + 1) * 16],
                                                              op0=ALU.add, op1=ALU.add), reads=[mbf, vbias], writes=[mb])
                pm = PS[7]; pmv = psb(7)
                for s in range(4):
                    kb.mm(pm, pmv[0:16, s * 128:(s + 1) * 128], mb.t[:, s, :], ident_b.t[:], start=True, stop=True, reads=[mb, ident_b], sig=(s == 3), transpose=True)
                kb.op(V, lambda e: e.tensor_copy(out=negm.t[:], in_=pmv[0:16, 0:512]), reads=[pm], writes=[negm])

                poA = [PS[2], PS[2], PS[3], PS[3]]; colA = [0, 129, 0, 129]
                poB = [PS[4], PS[5], PS[4], PS[5]]; colB = [0, 0, 129, 129]
                pdiag = PS[6]
                nA = 4 * it
                nB = [0, 1, 2, 3]
                doneA = [0, 0, 0, 0]; doneB = [0, 0, 0, 0]

                def pv_mm(kind, s, kt, Pbuf, P):
                    if kind == 'D':
                        kb.mm(pdiag, pdiag.t[:, 0:129], lhsT=P, rhs=vbf.t[:, kt, :], start=True, stop=True, reads=[vbf, Pbuf])
                    elif kind == 'A':
                        ps = poA[s]; c = colA[s]
                        first = doneA[s] == 0
                        doneA[s] += 1
                        kb.mm(ps, ps.t[:, c:c + 129], lhsT=P, rhs=vbf.t[:, kt, :], start=first, stop=(doneA[s] == nA), reads=[vbf, Pbuf], sig=True)
                    else:
                        ps = poB[s]; c = colB[s]
                        first = doneB[s] == 0
                        doneB[s] += 1
                        kb.mm(ps, ps.t[:, c:c + 129], lhsT=P, rhs=vbf.t[:, kt, :], start=first, stop=(doneB[s] == nB[s]), reads=[vbf, Pbuf], sig=True)

                prevA = None
                for kt in range(4 * it):
                    n = kt // 2
                    pss = PS[pctr[0] % 2]; P = PT[pctr[0] % 3]; pctr[0] += 1
                    kb.mm(pss, pss.t[:], lhsT=kbf.t[:, kt * 128:(kt + 1) * 128], rhs=qT.t[:, h, :], start=True, stop=False, reads=[kbf, qT], sig=False)
                    kb.mm(pss, pss.t[:], lhsT=esel.t[:, n, :], rhs=negm.t[:], start=False, stop=True, reads=[esel, negm], sig=True)
                    jj = 4 * it - kt
                    kb.op(A, lambda e: e.activation(out=P.t[:], in_=pss.t[:], func=AF.Exp, bias=albias.t[:, h, jj:jj + 1], scale=SCALE),
                          reads=[pss, albias], writes=[P])
                    if prevA is not None:
                        pk, pP = prevA
                        for s in range(4):
                            pv_mm('A', s, pk, pP, pP.t[:, s * 128:(s + 1) * 128])
                    prevA = (kt, P)
                if prevA is not None:
                    pk, pP = prevA
                    for s in range(4):
                        pv_mm('A', s, pk, pP, pP.t[:, s * 128:(s + 1) * 128])
                for s in range(4):
                    own = 2 * it + s // 2
                    pendB = None
                    for sp in range(s + 1):
                        kt = 4 * it + sp
                        n = kt // 2
                        pss = PS[pctr[0] % 2]; P = PT[pctr[0] % 3]; pctr[0] += 1
                        qs = slice(s * 128, (s + 1) * 128)
                        need_mask = (n < own)
                        kb.mm(pss, pss.t[:, 0:128], lhsT=kbf.t[:, kt * 128:(kt + 1) * 128], rhs=qT.t[:, h, qs], start=True, stop=not need_mask,
                              reads=[kbf, qT], sig=not need_mask)
                        if need_mask:
                            kb.mm(pss, pss.t[:, 0:128], lhsT=esel.t[:, n, :], rhs=negm.t[:, qs], start=False, stop=True, reads=[esel, negm], sig=True)
                        if sp < s:
                            jj = s - sp
                            kb.op(A, lambda e: e.activation(out=P.t[:, 0:128], in_=pss.t[:, 0:128], func=AF.Exp, bias=albias.t[:, h, jj:jj + 1], scale=SCALE),
                                  reads=[pss, albias], writes=[P])
                            if pendB is not None:
                                pv_mm(*pendB)
                            pendB = ('B', s, kt, P, P.t[:, 0:128])
                        else:
                            kb.op(V, lambda e: e.scalar_tensor_tensor(out=dtmp.t[:], in0=pss.t[:, 0:128], scalar=SCALE, in1=dmat.t[:, h, :],
                                                                      op0=ALU.mult, op1=ALU.add), reads=[pss, dmat], writes=[dtmp])
                            kb.op(A, lambda e: e.activation(out=P.t[:, 0:128], in_=dtmp.t[:], func=AF.Exp), reads=[dtmp], writes=[P])
                            if pendB is not None:
                                pv_mm(*pendB)
                                pendB = None
                            pv_mm('D', s, kt, P, P.t[:, 0:128])
                    kb.op(V, lambda e: e.tensor_copy(out=num.t[:], in_=pdiag.t[:, 0:129]), reads=[pdiag], writes=[num])
                    if nB[s] > 0:
                        pb = poB[s]; c = colB[s]
                        kb.op(V, lambda e: e.scalar_tensor_tensor(out=num.t[:], in0=pb.t[:, c:c + 129], scalar=fq.t[:, h:h + 1], in1=num.t[:],
                                                                  op0=ALU.mult, op1=ALU.add), reads=[pb, num, fq], writes=[num])
                    if nA > 0:
                        pa = poA[s]; c = colA[s]
                        kb.op(V, lambda e: e.scalar_tensor_tensor(out=num.t[:], in0=pa.t[:, c:c + 129], scalar=fqa.t[:, s, h:h + 1], in1=num.t[:],
                                                                  op0=ALU.mult, op1=ALU.add), reads=[pa, num, fqa], writes=[num])
                    kb.op(V, lambda e: e.reciprocal(out=rec.t[:], in_=num.t[:, 128:129]), reads=[num], writes=[rec])
                    kb.op(V, lambda e: e.tensor_scalar(out=otok.t[:, s, :], in0=num.t[:, 0:128], scalar1=rec.t[:, 0:1], scalar2=None, op0=ALU.mult),
                          reads=[num, rec], writes=[otok])
                pt = PS[7]; ptv = psb(7)
                for s in range(4):
                    kb.mm(pt, ptv[:, s * 128:(s + 1) * 128], otok.t[:, s, :], ident_b.t[:], start=True, stop=True, reads=[otok, ident_b], sig=(s == 3), transpose=True)
                kb.op(A, lambda e: e.activation(out=bigT.t[:, h, :], in_=ptv[:, 0:512], func=AF.Copy), reads=[pt], writes=[bigT])
            kb.barrier()
        linT("w_o", D, blocks_dm, lambda kc: bigT.t[:, kc, :], [bigT], resid_epi)
        layer_norm(2)
        if STOP == "hmid1":
            break
        kb.barrier()
        with ExitStack() as pes:
            ffn(1, pes)
            layer_norm(3)
            kb.barrier()
        with ExitStack() as pes:
            ost = [kb.sb("o_st%d" % i, [128, D], F32, pes) for i in range(2)]
            for s in range(4):
                o = ost[s % 2]
                for b4 in range(4):
                    ps = PS[4 + (b4 % 2)]
                    for j in range(4):
                        kc = b4 * 4 + j
                        kb.mm(ps, ps.t[:, j * 128:(j + 1) * 128], hT_f.t[:, kc, s * 128:(s + 1) * 128], ident_f.t[:], start=True, stop=True,
                              reads=[hT_f, ident_f], sig=(j == 3), transpose=True)
                    kb.op(A if b4 % 2 else V, (lambda e: e.activation(out=o.t[:, b4 * 512:(b4 + 1) * 512], in_=ps.t[:], func=AF.Copy)) if b4 % 2 else
                          (lambda e: e.tensor_copy(out=o.t[:, b4 * 512:(b4 + 1) * 512], in_=ps.t[:])), reads=[ps], writes=[o])
                kb.dma(G, out_d[tok0 + s * 128: tok0 + (s + 1) * 128, :], o.t[:], o, reads=[o])
            kb.barrier()

    if STOP is not None:
        with ExitStack() as pes:
            ost = [kb.sb("dbg_st%d" % i, [128, D], F32, pes) for i in range(2)]
            for s in range(4):
                o = ost[s % 2]
                for b4 in range(4):
                    ps = PS[4 + (b4 % 2)]
                    for j in range(4):
                        kc = b4 * 4 + j
                        kb.mm(ps, ps.t[:, j * 128:(j + 1) * 128], hT_f.t[:, kc, s * 128:(s + 1) * 128], ident_f.t[:], start=True, stop=True,
                              reads=[hT_f, ident_f], sig=(j == 3), transpose=True)
                    kb.op(V, lambda e: e.tensor_copy(out=o.t[:, b4 * 512:(b4 + 1) * 512], in_=ps.t[:]), reads=[ps], writes=[o])
                kb.dma(G, out_d[tok0 + s * 128: tok0 + (s + 1) * 128, :], o.t[:], o, reads=[o])
    kb.finish()
    es.close()
    return nc


def host_consts(inp):
    f = np.float32
    c = {}

    def percol(v, nch):
        return np.ascontiguousarray(v.reshape(nch, 128).T).astype(f)
    mcw = inp["mamba_conv_w"][0]
    c["mcw"] = np.ascontiguousarray(mcw.reshape(4, 48, 128).transpose(2, 1, 0)).reshape(128, 48 * 4).astype(f)
    c["mcb"] = percol(inp["mamba_conv_b"][0], 48)
    fw = inp["ffn_conv_w"]
    c["fcw"] = np.ascontiguousarray(fw.reshape(2, 3, 88, 128).transpose(3, 0, 2, 1)).reshape(128, 2 * 88 * 3).astype(f)
    c["fcb"] = np.ascontiguousarray(inp["ffn_conv_b"].reshape(2, 88, 128).transpose(2, 0, 1)).reshape(128, 2 * 88).astype(f)
    g = np.stack([inp["ln_mix_g"][0], inp["ln_ffn_g"][0], inp["ln_mix_g"][1], inp["ln_ffn_g"][1]])
    b = np.stack([inp["ln_mix_b"][0], inp["ln_ffn_b"][0], inp["ln_mix_b"][1], inp["ln_ffn_b"][1]])
    c["lng"] = np.ascontiguousarray(g.reshape(4, 16, 128).transpose(2, 0, 1)).reshape(128, 64).astype(f)
    c["lnb"] = np.ascontiguousarray(b.reshape(4, 16, 128).transpose(2, 0, 1)).reshape(128, 64).astype(f)
    c["dtb"] = np.ascontiguousarray(inp["mamba_dt_bias"][0].reshape(64, 1)).astype(f)
    return c


def kernel(**inputs):
    return _run(inputs)


_CACHE = {}


def _run(inputs, NT=8, STOP=None, NCORES=8, TRACE=False):
    inp = {k: np.asarray(v) for k, v in inputs.items()}
    f = np.float32
    c = host_consts(inp)
    c["drow"] = np.ascontiguousarray(np.broadcast_to(inp["mamba_d"][0][None, :], (128, 64))).astype(f)
    c["nrmw"] = np.ascontiguousarray(np.broadcast_to(inp["mamba_norm_w"][0][None, :], (128, DI))).astype(f)
    alog = np.ascontiguousarray(np.broadcast_to(inp["mamba_a_log"][0][None, :], (128, 64))).astype(f)
    slopes = (2.0 ** (-8.0 * np.arange(1, HEADS + 1, dtype=np.float64) / HEADS))
    kk = np.arange(128, dtype=np.float64)
    alb = np.zeros((128, HEADS, 36), f)
    for j in range(1, 36):
        alb[:, :, j] = (-(slopes[None, :]) * (128.0 * j - kk[:, None])).astype(f)
    dm = np.zeros((128, HEADS, 128), f)
    qq = np.arange(128, dtype=np.float64)
    dist = qq[None, :] - kk[:, None]
    for h in range(HEADS):
        dm[:, h, :] = np.where(dist >= 0, -slopes[h] * dist, -1.0e4).astype(f)
    fqt = np.exp(-slopes[None, :] * qq[:, None]).astype(f)
    fqa_t = np.stack([np.exp(-slopes[None, :] * (128.0 * s_ + qq[:, None])) for s_ in range(4)], axis=1).astype(f)
    vb = np.zeros((8, 4, 16), f)
    for it in range(8):
        for s in range(4):
            own = 2 * it + s // 2
            vb[it, s, :] = np.where(np.arange(16) < own, 0.0, NEG)
    vbias = np.ascontiguousarray(np.broadcast_to(vb.reshape(1, 8 * 64), (128, 512))).astype(f)
    es_ = np.zeros((16, 16, 128), f)
    for n in range(16):
        es_[n, n, :] = 1.0
    key = (NT, STOP)
    if key not in _CACHE:
        _CACHE[key] = build(NT, STOP)
    nc = _CACHE[key]
    common = dict(
        w_in=inp["mamba_w_in"][0], w_out=inp["mamba_w_out"][0], up0=inp["ffn_w_up"][0], dn0=inp["ffn_w_down"][0],
        w_kv=inp["w_kv"], w_q=inp["attn_w_q"][0], w_o=inp["attn_w_o"][0], up1=inp["ffn_w_up"][1], dn1=inp["ffn_w_down"][1],
        mcw=c["mcw"], mcb=c["mcb"], fcw=c["fcw"], fcb=c["fcb"], lng=c["lng"], lnb=c["lnb"], dtb=c["dtb"],
        arow=alog, drow=c["drow"], nrmw=c["nrmw"], albias=alb.reshape(128, -1), dmat=dm.reshape(128, -1), fq=fqt, fqa=fqa_t.reshape(128, -1),
        vbias=vbias, esel=es_.reshape(16, -1))
    common = {k: np.ascontiguousarray(v, dtype=f) for k, v in common.items()}
    in_maps = []
    for core in range(NCORES):
        m = dict(common)
        m["x"] = np.ascontiguousarray(inp["x"][core // 2], dtype=f)
        in_maps.append(m)
    res = run_bass_kernel_spmd(nc, in_maps, core_ids=list(range(NCORES)), **({"trace": True} if TRACE else {}))
    if NCORES < 8:
        _CACHE["last_res"] = res
        return res.results[0]["out"]
    out = np.stack([res.results[2 * b]["out"] for b in range(4)], axis=0)
    return out.astype(np.float32)
```
